# Optimizing a Trainium2 kernel written in Bass

```python
import math
import jax
import jax.numpy as jnp
from jax import lax
import numpy as np

D_MODEL = 1024
BATCH = 8
SEQ = 4096
DEPTH = 4

CONV_WIDTH_MIX = D_MODEL // 2
CONV_GROUPS = 8
CONV_K = 3
DIFF_HEADS = 4
DIFF_HEAD_DIM = 64
DIFF_WIDTH = DIFF_HEADS * 2 * DIFF_HEAD_DIM
IN_WIDTH = 3 * CONV_WIDTH_MIX + 3 * DIFF_WIDTH
SPLITS = [CONV_WIDTH_MIX, 2 * CONV_WIDTH_MIX, 3 * CONV_WIDTH_MIX,
          3 * CONV_WIDTH_MIX + DIFF_WIDTH, 3 * CONV_WIDTH_MIX + 2 * DIFF_WIDTH]
Q_BLOCK = 128
ROPE_THETA = 10000.0
SUBLN_EPS = 1e-5
RWKV_HEAD = 64
RWKV_HEADS = D_MODEL // RWKV_HEAD
LORA_DECAY = 64
LORA_AAA = 64
LORA_MV = 32
LORA_GATE = 160
GN_EPS = 64e-5
D_FF = 2816
FFN_CONV_K = 3
DN_ALPHA = (2 * DEPTH) ** 0.25
DN_BETA = (8 * DEPTH) ** -0.25
LN_EPS = 1e-5
N_EVEN = (DEPTH + 1) // 2
N_ODD = DEPTH // 2

kernel_name = 'hybrid_shortconv_diffattn_rwkv7_deepnorm'


def layer_norm(x, g, b):
    xf = x.astype(jnp.float32)
    mu = jnp.mean(xf, -1, keepdims=True)
    var = jnp.mean(jnp.square(xf - mu), -1, keepdims=True)
    return ((xf - mu) * lax.rsqrt(var + LN_EPS) * g + b).astype(x.dtype)


def causal_dwconv(u, w):
    K, C = w.shape
    return lax.conv_general_dilated(
        u, w[:, None, :].astype(u.dtype), window_strides=(1,), padding=[(K - 1, 0)],
        dimension_numbers=('NWC', 'WIO', 'NWC'), feature_group_count=C)


def rope_tables(T, dim):
    inv = 1.0 / (ROPE_THETA ** (jnp.arange(0, dim, 2, dtype=jnp.float32) / dim))
    ang = jnp.arange(T, dtype=jnp.float32)[:, None] * inv[None, :]
    return jnp.cos(ang), jnp.sin(ang)


def apply_rope(x, cos, sin):
    c = cos[None, :, None, :]
    s = sin[None, :, None, :]
    x1, x2 = jnp.split(x, 2, axis=-1)
    return jnp.concatenate([x1 * c - x2 * s, x2 * c + x1 * s], axis=-1).astype(x.dtype)


def diff_attention(q, k, v, lam):
    T = q.shape[3]
    scale = DIFF_HEAD_DIM ** -0.5
    outs = []
    for i in range(T // Q_BLOCK):
        s0 = i * Q_BLOCK
        e = s0 + Q_BLOCK
        s = jnp.einsum('bhmqd,bhmkd->bhmqk', q[:, :, :, s0:e], k[:, :, :, :e],
                       preferred_element_type=jnp.float32) * scale
        mask = jnp.arange(e)[None, :] <= jnp.arange(s0, e)[:, None]
        p = jax.nn.softmax(jnp.where(mask, s, -jnp.inf), axis=-1)
        a = p[:, :, 0] - lam * p[:, :, 1]
        outs.append(jnp.einsum('bhqk,bhkd->bhqd', a.astype(v.dtype), v[:, :, :e]))
    return jnp.concatenate(outs, axis=2)


def even_mixer(x, w_in, conv_w, lam_q1, lam_k1, lam_q2, lam_k2, subln_g, w_out,
               lam_init, cos, sin):
    B, T, _ = x.shape
    h = x @ w_in
    gb, gc, xin, q, k, v = jnp.split(h, SPLITS, axis=-1)
    y_a = gb * causal_dwconv(gc * xin, conv_w)
    q = apply_rope(q.reshape(B, T, 2 * DIFF_HEADS, DIFF_HEAD_DIM), cos, sin)
    k = apply_rope(k.reshape(B, T, 2 * DIFF_HEADS, DIFF_HEAD_DIM), cos, sin)
    qh = q.reshape(B, T, DIFF_HEADS, 2, DIFF_HEAD_DIM).transpose(0, 2, 3, 1, 4)
    kh = k.reshape(B, T, DIFF_HEADS, 2, DIFF_HEAD_DIM).transpose(0, 2, 3, 1, 4)
    vh = v.reshape(B, T, DIFF_HEADS, 2 * DIFF_HEAD_DIM).transpose(0, 2, 1, 3)
    lam = (jnp.exp(jnp.sum(lam_q1.astype(jnp.float32) * lam_k1.astype(jnp.float32)))
           - jnp.exp(jnp.sum(lam_q2.astype(jnp.float32) * lam_k2.astype(jnp.float32)))
           + lam_init)
    o = diff_attention(qh, kh, vh, lam).astype(jnp.float32)
    o = o * lax.rsqrt(jnp.mean(jnp.square(o), -1, keepdims=True) + SUBLN_EPS) * subln_g
    o = o * (1.0 - lam_init)
    y_b = o.transpose(0, 2, 1, 3).reshape(B, T, DIFF_WIDTH).astype(x.dtype)
    return jnp.concatenate([y_a, y_b], axis=-1) @ w_out


def wkv7_scan(r, w, k, v, a, b):
    B, T, H, N = r.shape

    def step(S, inp):
        r_t, w_t, k_t, v_t, a_t, b_t = inp
        sa = jnp.einsum('bhvk,bhk->bhv', S, a_t)
        S = S * w_t[:, :, None, :] + sa[..., None] * b_t[:, :, None, :] + v_t[..., None] * k_t[:, :, None, :]
        return S, jnp.einsum('bhvk,bhk->bhv', S, r_t)

    xs = tuple(jnp.moveaxis(t.astype(jnp.float32), 1, 0) for t in (r, w, k, v, a, b))
    _, o = lax.scan(step, jnp.zeros((B, H, N, N), jnp.float32), xs)
    return jnp.moveaxis(o, 0, 1)


def rwkv7_time_mix(x, mix, w_r, w_k, w_v, w_o, w0, w1, w2, a0, a1, a2, g1, g2,
                   k_k, k_a, r_k, gn_g, gn_b, v_first, vres):
    B, T, C = x.shape
    H, N = RWKV_HEADS, RWKV_HEAD
    xx = jnp.pad(x, ((0, 0), (1, 0), (0, 0)))[:, :-1] - x
    xr, xw, xk, xv, xa, xg = [x + xx * mix[i] for i in range(6)]
    r = xr @ w_r
    k = xk @ w_k
    v = xv @ w_v
    w_log = -jax.nn.softplus(-(w0 + jnp.tanh(xw @ w1) @ w2)) - 0.5
    decay = jnp.exp(-jnp.exp(w_log.astype(jnp.float32)))
    if vres is None:
        v_first = v
    else:
        v0, v1, v2 = vres
        v = v + (v_first - v) * jax.nn.sigmoid(v0 + (xv @ v1) @ v2)
    a = jax.nn.sigmoid(a0 + (xa @ a1) @ a2)
    g = jax.nn.sigmoid(xg @ g1) @ g2
    kk = (k * k_k).astype(jnp.float32).reshape(B, T, H, N)
    kk = kk / jnp.maximum(jnp.sqrt(jnp.sum(jnp.square(kk), -1, keepdims=True)), 1e-12)
    k = k * (1 + (a - 1) * k_a)
    rh = r.reshape(B, T, H, N)
    kh = k.reshape(B, T, H, N)
    vh = v.reshape(B, T, H, N)
    ah = a.astype(jnp.float32).reshape(B, T, H, N)
    o = wkv7_scan(rh, decay.reshape(B, T, H, N), kh, vh, -kk, kk * ah)
    mu = jnp.mean(o, -1, keepdims=True)
    var = jnp.mean(jnp.square(o - mu), -1, keepdims=True)
    o = ((o - mu) * lax.rsqrt(var + GN_EPS)).reshape(B, T, C) * gn_g + gn_b
    bonus = jnp.sum((rh * kh * r_k).astype(jnp.float32), -1, keepdims=True) * vh.astype(jnp.float32)
    o = (o + bonus.reshape(B, T, C)).astype(x.dtype)
    return (o * g) @ w_o, v_first


def conv_glu_ffn(x, w_up, conv_w, conv_b, w_down):
    h = causal_dwconv(x @ w_up, conv_w) + conv_b
    gate, up = jnp.split(h, 2, axis=-1)
    return (jax.nn.silu(gate) * up) @ w_down


def setup_inputs(seed: int = 0) -> dict:
    key = jax.random.key(seed)
    ks = iter(jax.random.split(key, 48))
    f32 = jnp.float32
    D = D_MODEL

    def nrm(shape, scale):
        return jax.random.normal(next(ks), shape, f32) * scale

    def unif(shape, lo, hi):
        return jax.random.uniform(next(ks), shape, f32, lo, hi)

    x = nrm((BATCH, SEQ, D), 1.0)
    col_scale = jnp.concatenate([
        jnp.ones((2 * CONV_WIDTH_MIX,), f32), jnp.full((CONV_WIDTH_MIX,), DN_BETA, f32),
        jnp.ones((2 * DIFF_WIDTH,), f32), jnp.full((DIFF_WIDTH,), DN_BETA, f32)])
    ev_w_in = nrm((N_EVEN, D, IN_WIDTH), D ** -0.5) * col_scale
    ev_conv_w = nrm((N_EVEN, CONV_K, CONV_WIDTH_MIX), CONV_K ** -0.5)
    ev_lam_q1 = nrm((N_EVEN, DIFF_HEAD_DIM), 0.1)
    ev_lam_k1 = nrm((N_EVEN, DIFF_HEAD_DIM), 0.1)
    ev_lam_q2 = nrm((N_EVEN, DIFF_HEAD_DIM), 0.1)
    ev_lam_k2 = nrm((N_EVEN, DIFF_HEAD_DIM), 0.1)
    ev_subln_g = 1.0 + nrm((N_EVEN, 2 * DIFF_HEAD_DIM), 0.02)
    ev_w_out = nrm((N_EVEN, D, D), D ** -0.5 * DN_BETA)

    rw_mix = unif((N_ODD, 6, D), 0.0, 1.0)
    rw_w_r = nrm((N_ODD, D, D), D ** -0.5)
    rw_w_k = nrm((N_ODD, D, D), D ** -0.5)
    rw_w_v = nrm((N_ODD, D, D), D ** -0.5 * DN_BETA)
    rw_w_o = nrm((N_ODD, D, D), D ** -0.5 * DN_BETA)
    rw_w0 = unif((N_ODD, D), -5.0, 0.5)
    rw_w1 = nrm((N_ODD, D, LORA_DECAY), D ** -0.5)
    rw_w2 = nrm((N_ODD, LORA_DECAY, D), 0.1 * LORA_DECAY ** -0.5)
    rw_a0 = nrm((N_ODD, D), 0.1)
    rw_a1 = nrm((N_ODD, D, LORA_AAA), D ** -0.5)
    rw_a2 = nrm((N_ODD, LORA_AAA, D), 0.5 * LORA_AAA ** -0.5)
    rw_g1 = nrm((N_ODD, D, LORA_GATE), D ** -0.5)
    rw_g2 = nrm((N_ODD, LORA_GATE, D), LORA_GATE ** -0.5)
    rw_k_k = 0.85 + nrm((N_ODD, D), 0.02)
    rw_k_a = 1.0 + nrm((N_ODD, D), 0.02)
    rw_r_k = -0.04 + nrm((N_ODD, RWKV_HEADS, RWKV_HEAD), 0.02)
    rw_gn_g = 1.0 + nrm((N_ODD, D), 0.02)
    rw_gn_b = nrm((N_ODD, D), 0.02)
    rw_v0 = nrm((N_ODD - 1, D), 0.1)
    rw_v1 = nrm((N_ODD - 1, D, LORA_MV), D ** -0.5)
    rw_v2 = nrm((N_ODD - 1, LORA_MV, D), 0.5 * LORA_MV ** -0.5)

    ffn_w_up = nrm((DEPTH, D, 2 * D_FF), D ** -0.5 * DN_BETA)
    ffn_conv_w = nrm((DEPTH, FFN_CONV_K, 2 * D_FF), FFN_CONV_K ** -0.5)
    ffn_conv_b = nrm((DEPTH, 2 * D_FF), 0.02)
    ffn_w_down = nrm((DEPTH, D_FF, D), D_FF ** -0.5 * DN_BETA)

    ln1_g = 1.0 + nrm((DEPTH, D), 0.02)
    ln1_b = nrm((DEPTH, D), 0.02)
    ln2_g = 1.0 + nrm((DEPTH, D), 0.02)
    ln2_b = nrm((DEPTH, D), 0.02)
    return {
        'x': x,
        'ev_w_in': ev_w_in, 'ev_conv_w': ev_conv_w,
        'ev_lam_q1': ev_lam_q1, 'ev_lam_k1': ev_lam_k1, 'ev_lam_q2': ev_lam_q2, 'ev_lam_k2': ev_lam_k2,
        'ev_subln_g': ev_subln_g, 'ev_w_out': ev_w_out,
        'rw_mix': rw_mix, 'rw_w_r': rw_w_r, 'rw_w_k': rw_w_k, 'rw_w_v': rw_w_v, 'rw_w_o': rw_w_o,
        'rw_w0': rw_w0, 'rw_w1': rw_w1, 'rw_w2': rw_w2,
        'rw_a0': rw_a0, 'rw_a1': rw_a1, 'rw_a2': rw_a2,
        'rw_g1': rw_g1, 'rw_g2': rw_g2,
        'rw_k_k': rw_k_k, 'rw_k_a': rw_k_a, 'rw_r_k': rw_r_k,
        'rw_gn_g': rw_gn_g, 'rw_gn_b': rw_gn_b,
        'rw_v0': rw_v0, 'rw_v1': rw_v1, 'rw_v2': rw_v2,
        'ffn_w_up': ffn_w_up, 'ffn_conv_w': ffn_conv_w, 'ffn_conv_b': ffn_conv_b, 'ffn_w_down': ffn_w_down,
        'ln1_g': ln1_g, 'ln1_b': ln1_b, 'ln2_g': ln2_g, 'ln2_b': ln2_b,
    }


def reference(x, ev_w_in, ev_conv_w, ev_lam_q1, ev_lam_k1, ev_lam_q2, ev_lam_k2,
              ev_subln_g, ev_w_out,
              rw_mix, rw_w_r, rw_w_k, rw_w_v, rw_w_o, rw_w0, rw_w1, rw_w2,
              rw_a0, rw_a1, rw_a2, rw_g1, rw_g2, rw_k_k, rw_k_a, rw_r_k,
              rw_gn_g, rw_gn_b, rw_v0, rw_v1, rw_v2,
              ffn_w_up, ffn_conv_w, ffn_conv_b, ffn_w_down,
              ln1_g, ln1_b, ln2_g, ln2_b):
    T = x.shape[1]
    cos, sin = rope_tables(T, DIFF_HEAD_DIM)
    v_first = None
    for l in range(DEPTH):
        if l % 2 == 0:
            i = l // 2
            lam_init = 0.8 - 0.6 * math.exp(-0.3 * l)
            y = even_mixer(x, ev_w_in[i], ev_conv_w[i], ev_lam_q1[i], ev_lam_k1[i],
                           ev_lam_q2[i], ev_lam_k2[i], ev_subln_g[i], ev_w_out[i],
                           lam_init, cos, sin)
        else:
            j = l // 2
            vres = None if j == 0 else (rw_v0[j - 1], rw_v1[j - 1], rw_v2[j - 1])
            y, v_first = rwkv7_time_mix(
                x, rw_mix[j], rw_w_r[j], rw_w_k[j], rw_w_v[j], rw_w_o[j],
                rw_w0[j], rw_w1[j], rw_w2[j], rw_a0[j], rw_a1[j], rw_a2[j],
                rw_g1[j], rw_g2[j], rw_k_k[j], rw_k_a[j], rw_r_k[j],
                rw_gn_g[j], rw_gn_b[j], v_first, vres)
        x = layer_norm(DN_ALPHA * x + y, ln1_g[l], ln1_b[l])
        f = conv_glu_ffn(x, ffn_w_up[l], ffn_conv_w[l], ffn_conv_b[l], ffn_w_down[l])
        x = layer_norm(DN_ALPHA * x + f, ln2_g[l], ln2_b[l])
    return x
```

```python
import contextlib
import math
import numpy as np
import concourse.bass as bass
import concourse.mybir as mybir
from concourse.bass_utils import run_bass_kernel_spmd

F32 = mybir.dt.float32
BF16 = mybir.dt.bfloat16
AF = mybir.ActivationFunctionType
ALU = mybir.AluOpType
AX = mybir.AxisListType

SEM_LIMIT = 30000
T = 4096
D = 1024
TT = 256
NT = T // TT
DFF = 2816
NFC = DFF // 128
ALPHA = float(8 ** 0.25)
LN_EPS = 1e-5
C0 = float(math.exp(-0.5))
NPAR = 320


import os as _os
_DBG_STOP = int(_os.environ.get("RWB_STOP", "9"))
_DBG_SUB = int(_os.environ.get("RWB_SUB", "9"))
_DBG_NMAX = int(_os.environ.get("RWB_NMAX", "64"))


class _Op:
    __slots__ = ("eng", "fn", "deps", "sig", "is_dma", "dkey", "sigval", "idx")


class Prog:
    ENGS = ("pe", "act", "dve", "pool", "sp")
    N_EPOCH = 4
    N_DMA_SEMS = 60

    def __init__(self, nc, stack):
        self.nc = nc
        self.ops = []
        self.last_w = {}
        self.readers = {}
        self.sems = {}
        for e in self.ENGS:
            for i in range(self.N_EPOCH):
                self.sems[(e, i)] = stack.enter_context(nc.semaphore("s_%s_%d" % (e, i)))
        self.dma_pool = [(stack.enter_context(nc.semaphore("s_dma_%d" % i)), 0) for i in range(self.N_DMA_SEMS)]
        self.eng_cnt = {e: 0 for e in self.ENGS}
        self.dma_cnt = {}
        self.waited = {e: {} for e in self.ENGS}
        self.emitted = 0
        self.n_instr = 0

    def add(self, eng, fn, reads=(), writes=(), dkey=None):
        op = _Op()
        op.eng = eng
        op.fn = fn
        op.is_dma = dkey is not None
        op.dkey = dkey
        op.sig = False
        op.sigval = None
        op.idx = len(self.ops)
        deps = {}

        def adddep(d):
            if d is None or d.idx < self.emitted:
                return
            if (not d.is_dma) and d.eng == "pe" and eng == "pe" and not op.is_dma:
                return
            k = ("dma", d.dkey) if d.is_dma else d.eng
            o = deps.get(k)
            if o is None or o.idx < d.idx:
                deps[k] = d
        for k in reads:
            adddep(self.last_w.get(k))
        for k in writes:
            adddep(self.last_w.get(k))
            for r in self.readers.get(k, {}).values():
                adddep(r)
        for k in writes:
            self.last_w[k] = op
            self.readers[k] = {}
        for k in reads:
            rk = ("dma", dkey) if op.is_dma else eng
            self.readers.setdefault(k, {})[rk] = op
        op.deps = list(deps.values())
        for d in op.deps:
            d.sig = True
        self.ops.append(op)
        return op

    def mm(self, out, lhsT, rhs, start=True, stop=True, reads=(), writes=()):
        return self.add("pe", lambda e: e.matmul(out, lhsT, rhs, start=start, stop=stop), reads, writes)

    def tr(self, out, in_, ident, reads=(), writes=()):
        return self.add("pe", lambda e: e.transpose(out, in_, ident), reads, writes)

    def act(self, out, in_, func, bias=None, scale=None, accum_out=None, reads=(), writes=()):
        kw = {}
        if bias is not None:
            kw["bias"] = bias
        if scale is not None:
            kw["scale"] = scale
        if accum_out is not None:
            kw["accum_out"] = accum_out
        return self.add("act", lambda e: e.activation(out, in_, func, **kw), reads, writes)

    def tt(self, eng, out, in0, in1, op, reads=(), writes=()):
        return self.add(eng, lambda e: e.tensor_tensor(out, in0, in1, op), reads, writes)

    def ts(self, eng, out, in0, s1, s2, op0, op1=None, reads=(), writes=()):
        if op1 is None:
            return self.add(eng, lambda e: e.tensor_scalar(out, in0, s1, None, op0), reads, writes)
        return self.add(eng, lambda e: e.tensor_scalar(out, in0, s1, s2, op0, op1), reads, writes)

    def stt(self, out, in0, scalar, in1, op0, op1, reads=(), writes=()):
        return self.add("dve", lambda e: e.scalar_tensor_tensor(out, in0, scalar, in1, op0, op1), reads, writes)

    def copy(self, eng, out, in_, reads=(), writes=()):
        if eng == "act":
            return self.add(eng, lambda e: e.copy(out, in_), reads, writes)
        return self.add(eng, lambda e: e.tensor_copy(out, in_), reads, writes)

    def recip(self, out, in_, reads=(), writes=()):
        return self.add("dve", lambda e: e.reciprocal(out, in_), reads, writes)

    def memset(self, eng, ap, val, writes=()):
        return self.add(eng, lambda e: e.memset(ap, val), (), writes)

    def dma(self, q, out, in_, dkey, reads=(), writes=(), **kw):
        return self.add(q, lambda e: e.dma_start(out, in_, **kw), reads, writes, dkey=dkey)

    def flush(self):
        nc = self.nc
        ops = self.ops[self.emitted:]
        self.emitted = len(self.ops)
        if not ops:
            return
        last = {}
        for op in ops:
            last[op.eng] = op
        for e, op in last.items():
            op.sig = True
        eng_cnt = self.eng_cnt
        dma_cnt = self.dma_cnt
        for op in ops:
            if op.is_dma:
                c = dma_cnt.get(op.dkey, 0) + 16
                dma_cnt[op.dkey] = c
                sn = ("dma", op.dkey)
                if sn not in self.sems:
                    self.dma_pool.sort(key=lambda t: -t[1])
                    sem_, base_ = self.dma_pool.pop()
                    self.sems[sn] = sem_
                    c = base_ + 16
                    dma_cnt[op.dkey] = c
                op.sigval = (sn, c)
            elif op.sig:
                c = eng_cnt[op.eng] + 1
                eng_cnt[op.eng] = c
                op.sigval = ((op.eng, (c - 1) // SEM_LIMIT), (c - 1) % SEM_LIMIT + 1)
        sems = self.sems
        per_eng = {e: [] for e in self.ENGS}
        for op in ops:
            per_eng[op.eng].append(op)
        finals = {}
        for e, op in last.items():
            if not op.is_dma:
                finals[op.sigval[0]] = max(finals.get(op.sigval[0], 0), op.sigval[1])
        for dk, c in dma_cnt.items():
            finals[("dma", dk)] = c
        N_EPOCH = self.N_EPOCH
        with nc.Block() as block:
            def make(ename):
                eops = per_eng[ename]
                waited = self.waited[ename]

                def do_wait(e, sn, v):
                    if waited.get(sn, 0) >= v:
                        return
                    if sn[0] != "dma":
                        if any(waited.get((sn[0], j), 0) > 0 for j in range(sn[1] + 1, N_EPOCH)):
                            return
                    e.wait_ge(sems[sn], v)
                    waited[sn] = v
                    self.n_instr += 1

                def body(e):
                    for op in eops:
                        need = {}
                        for d in op.deps:
                            sn, v = d.sigval
                            if need.get(sn, 0) < v:
                                need[sn] = v
                        for sn, v in need.items():
                            do_wait(e, sn, v)
                        ins = op.fn(e)
                        self.n_instr += 1
                        if op.is_dma:
                            ins.then_inc(sems[op.sigval[0]], 16)
                        elif op.sig:
                            ins.then_inc(sems[op.sigval[0]], 1)
                    for sn, v in finals.items():
                        do_wait(e, sn, v)
                return body

            block.tensor(make("pe"))
            block.scalar(make("act"))
            block.vector(make("dve"))
            block.gpsimd(make("pool"))
            block.sync(make("sp"))
        for op in ops:
            op.fn = None
        for dk in list(dma_cnt.keys()):
            sn = ("dma", dk)
            self.dma_pool.append((self.sems.pop(sn), dma_cnt.pop(dk)))
            for e in self.ENGS:
                self.waited[e].pop(sn, None)


_UID = [0]


def U(name):
    _UID[0] += 1
    return "sb%d_%s" % (_UID[0], name)


class Ring:
    def __init__(self, st, nc, name, shape, dtype, n):
        self.t = [st.enter_context(nc.sbuf_tensor(U("%s_%d" % (name, i)), shape, dtype)) for i in range(n)]
        self.i = 0
        self.name = name
        self.n = n

    def next(self):
        j = self.i % self.n
        self.i += 1
        return self.t[j], (self.name, j)


class Env:
    pass


C_ID, C_SU, C_UI, C_SL, C_BO = 0, 128, 256, 384, 512

PC_LN1G, PC_LN1B, PC_LN2G, PC_LN2B = 0, 8, 16, 24
PC_FCW, PC_FCB = 32, 164
PC_ECW = 208
PC_MIX, PC_W0, PC_A0, PC_KK, PC_KA, PC_RK, PC_GNG, PC_GNB = 208, 256, 264, 272, 280, 288, 296, 304


def bank_ring(E, idxs):
    r = Env()
    r.idxs = list(idxs)
    r.i = 0

    def nxt():
        j = r.idxs[r.i % len(r.idxs)]
        r.i += 1
        return E.PB[j], ("pb", j)
    r.next = nxt
    return r


def load_weight(P, E, dst, src_ap, key, nsplit=1):
    C = dst.shape[1]
    step = max(1, C // nsplit)
    for c0 in range(0, C, step):
        c1 = min(C, c0 + step)
        P.dma("pool", dst[:, c0:c1, :], src_ap[c0 * 128:c1 * 128, :].rearrange("(c p) n -> p c n", p=128),
              "w_" + key, writes=[key], max_dma_last_dim=4096)


def ln_tail(P, E, st_bufs, xres, xres_key, gcol, bcol, emit_y, tile, ybanks, sbanks):
    nc = E.nc
    s, sb, sq = st_bufs["s"], st_bufs["sb"], st_bufs["sq"]
    xo = xres
    par = E.par
    mb, mkey = sbanks.next()
    qb, qkey = sbanks.next()
    pend = []

    def stats(m, sbm, sbk, sqm, sqk):
        P.mm(mb[:, 0:TT], E.onesD[:], sbm[:], start=(m == 0), stop=(m == 7), reads=[sbk, "consts"], writes=[mkey])
        P.mm(qb[:, 0:TT], E.onesD[:], sqm[:], start=(m == 0), stop=(m == 7), reads=[sqk, "consts"], writes=[qkey])
    for m in range(8):
        bk, bkey = ybanks.next()
        emit_y(m, bk[:, 0:TT], bkey)
        P.stt(s[:, m, :], xres[:, m, :], ALPHA, bk[:, 0:TT], ALU.mult, ALU.add,
              reads=[xres_key, bkey], writes=[("ln_s", m)])
        sbm, sbk = sb.next()
        sqm, sqk = sq.next()
        P.act(sbm[:], s[:, m, :], AF.Identity, reads=[("ln_s", m)], writes=[sbk])
        P.act(sqm[:], s[:, m, :], AF.Square, reads=[("ln_s", m)], writes=[sqk])
        if pend:
            stats(*pend.pop())
        pend.append((m, sbm, sbk, sqm, sqk))
    stats(*pend.pop())
    mean, m2, rstd = st_bufs["mean"], st_bufs["m2"], st_bufs["rstd"]
    P.copy("act", mean[:], mb[:, 0:TT], reads=[mkey], writes=["ln_mean"])
    P.tt("pool", m2[:], mean[:], mean[:], ALU.mult, reads=["ln_mean"], writes=["ln_m2"])
    P.stt(rstd[:], qb[:, 0:TT], LN_EPS, m2[:], ALU.add, ALU.subtract, reads=[qkey, "ln_m2"], writes=["ln_rstd"])
    P.act(rstd[:], rstd[:], AF.Sqrt, reads=["ln_rstd"], writes=["ln_rstd"])
    P.recip(rstd[:], rstd[:], reads=["ln_rstd"], writes=["ln_rstd"])
    for m in range(8):
        P.tt("pool", s[:, m, :], s[:, m, :], mean[:], ALU.subtract, reads=[("ln_s", m), "ln_mean"], writes=[("ln_s", m)])
        P.tt("dve", s[:, m, :], s[:, m, :], rstd[:], ALU.mult, reads=[("ln_s", m), "ln_rstd"], writes=[("ln_s", m)])
        P.act(xo[:, m, :], s[:, m, :], AF.Identity, scale=par[:, gcol + m:gcol + m + 1], bias=par[:, bcol + m:bcol + m + 1],
              reads=[("ln_s", m), "par"], writes=[xres_key])
    P.dma("sp", E.xs[:, :, tile * TT:(tile + 1) * TT].rearrange("c p t -> p c t"), xo[:], "xs_st",
          reads=[xres_key], writes=[("xs", tile)])


def alloc_ln(st, nc):
    b = {}
    b["s"] = st.enter_context(nc.sbuf_tensor(U("ln_s"), [128, 8, TT], F32))
    b["sb"] = Ring(st, nc, "ln_sb", [128, TT], BF16, 3)
    b["sq"] = Ring(st, nc, "ln_sq", [128, TT], BF16, 3)
    b["mean"] = st.enter_context(nc.sbuf_tensor(U("ln_mean"), [128, TT], F32))
    b["m2"] = st.enter_context(nc.sbuf_tensor(U("ln_m2"), [128, TT], F32))
    b["rstd"] = st.enter_context(nc.sbuf_tensor(U("ln_rstd"), [128, TT], F32))
    return b


def load_par(P, E, st, l):
    nc = E.nc
    par = st.enter_context(nc.sbuf_tensor(U("par_sb"), [128, NPAR], F32))
    P.dma("sp", par[:], E.d_par[l], "par", writes=["par"])
    E.par = par


def load_xtile(P, E, xres_ring, xb_ring, tile):
    xres, xkey = xres_ring.next()
    P.dma("sp", xres[:], E.xs[:, :, tile * TT:(tile + 1) * TT].rearrange("c p t -> p c t"), "xs_ld" + str(xkey[1]),
          reads=[("xs", tile)], writes=[xkey])
    xb, xbkey = xb_ring.next()
    P.copy("pool", xb[:], xres[:], reads=[xkey], writes=[xbkey])
    return xres, xkey, xb, xbkey


def phase_in(P, E):
    nc = E.nc
    with contextlib.ExitStack() as st:
        xt_r = Ring(st, nc, "pi_xt", [128, D], F32, 2)
        xo_r = Ring(st, nc, "pi_xo", [128, 8, 128], F32, 2)
        banks = bank_ring(E, [0, 1, 2, 3])
        for blk in range(T // 128):
            xt, xk = xt_r.next()
            P.dma("sp", xt[:], E.d_x[blk * 128:(blk + 1) * 128, :], "pi_ld" + str(xk[1]), writes=[xk])
            xo, ok = xo_r.next()
            for half in range(2):
                bk, bkey = banks.next()
                for j in range(4):
                    c = half * 4 + j
                    P.tr(bk[:, j * 128:(j + 1) * 128], xt[:, c * 128:(c + 1) * 128], E.cst[:, C_ID:C_ID + 128],
                         reads=[xk, "consts"], writes=[bkey])
                eng = "act" if half == 0 else "dve"
                P.copy(eng, xo[:, half * 4:half * 4 + 4, :], bk[:].rearrange("p (j t) -> p j t", j=4),
                       reads=[bkey], writes=[(ok, half)])
            P.dma("sp", E.xs[:, :, blk * 128:(blk + 1) * 128].rearrange("c p t -> p c t"), xo[:], "pi_st" + str(ok[1]),
                  reads=[(ok, 0), (ok, 1)], writes=[("xs", blk // 2)])
        P.flush()


def phase_out(P, E):
    nc = E.nc
    with contextlib.ExitStack() as st:
        xi_r = Ring(st, nc, "po_xi", [128, 8, 128], F32, 2)
        xo_r = Ring(st, nc, "po_xo", [128, D], F32, 2)
        banks = bank_ring(E, [0, 1, 2, 3])
        for blk in range(T // 128):
            xi, ik = xi_r.next()
            P.dma("sp", xi[:], E.xs[:, :, blk * 128:(blk + 1) * 128].rearrange("c p t -> p c t"), "po_ld" + str(ik[1]),
                  reads=[("xs", blk // 2)], writes=[ik])
            xo, ok = xo_r.next()
            for half in range(2):
                bk, bkey = banks.next()
                for j in range(4):
                    c = half * 4 + j
                    P.tr(bk[:, j * 128:(j + 1) * 128], xi[:, c, :], E.cst[:, C_ID:C_ID + 128],
                         reads=[ik, "consts"], writes=[bkey])
                eng = "act" if half == 0 else "dve"
                P.copy(eng, xo[:, half * 512:(half + 1) * 512], bk[:], reads=[bkey], writes=[(ok, half)])
            P.dma("sp", E.d_out[blk * 128:(blk + 1) * 128, :], xo[:], "po_st" + str(ok[1]),
                  reads=[(ok, 0), (ok, 1)], writes=[("out", blk)])
        P.flush()


def phase_ffn(P, E, l):
    nc = E.nc
    with contextlib.ExitStack() as st:
        load_par(P, E, st, l)
        par = E.par
        wup = st.enter_context(nc.sbuf_tensor(U("wup"), [128, 8, 2 * DFF], BF16))
        wdn = st.enter_context(nc.sbuf_tensor(U("wdn"), [128, NFC, D], BF16))
        for c in range(8):
            P.dma("pool", wup[:, c, :], E.d_ffn_up[l, c * 128:(c + 1) * 128, :], "w_up", writes=["wup"],
                  max_dma_last_dim=4096)
        load_weight(P, E, wdn, E.d_ffn_dn[l], "wdn", nsplit=2)
        lnb = alloc_ln(st, nc)
        xres_r = Ring(st, nc, "xres", [128, 8, TT], F32, 2)
        xb_r = Ring(st, nc, "xb", [128, 8, TT], BF16, 1)
        gT = st.enter_context(nc.sbuf_tensor(U("gT"), [128, NFC, TT], BF16))
        carry = st.enter_context(nc.sbuf_tensor(U("carry"), [128, 2 * NFC, 2], F32))
        hs_r = Ring(st, nc, "hs", [128, TT + 2], F32, 4)
        cv_r = Ring(st, nc, "cv", [128, TT], F32, 4)
        sg_r = Ring(st, nc, "sg", [128, TT], F32, 2)
        P.memset("pool", carry[:], 0.0, writes=[("carry", ch) for ch in range(2 * NFC)])
        hbanks = bank_ring(E, [0, 1, 2, 3])
        ybanks = bank_ring(E, [4, 5])
        sbanks = bank_ring(E, [6, 7])
        nxt = load_xtile(P, E, xres_r, xb_r, 0)
        for tile in range(NT):
            xres, xkey, xb, xbkey = nxt
            cvs = {}
            for j in range(NFC):
                for which in range(2):
                    ch = j + which * NFC
                    bk, bkey = hbanks.next()
                    for c in range(8):
                        P.mm(bk[:, 0:TT], wup[:, c, ch * 128:(ch + 1) * 128], xb[:, c, :], start=(c == 0), stop=(c == 7),
                             reads=["wup", xbkey], writes=[bkey])
                    hs, hkey = hs_r.next()
                    P.copy("pool", hs[:, 0:2], carry[:, ch, :], reads=[("carry", ch)], writes=[(hkey, "c")])
                    P.copy("act", hs[:, 2:TT + 2], bk[:, 0:TT], reads=[bkey], writes=[(hkey, "m")])
                    P.copy("pool", carry[:, ch, :], hs[:, TT:TT + 2], reads=[(hkey, "m")], writes=[("carry", ch)])
                    cv, ckey = cv_r.next()
                    w0c = par[:, PC_FCW + ch:PC_FCW + ch + 1]
                    w1c = par[:, PC_FCW + 44 + ch:PC_FCW + 44 + ch + 1]
                    w2c = par[:, PC_FCW + 88 + ch:PC_FCW + 88 + ch + 1]
                    bc = par[:, PC_FCB + ch:PC_FCB + ch + 1]
                    P.ts("dve", cv[:], hs[:, 2:TT + 2], w2c, bc, ALU.mult, ALU.add, reads=[(hkey, "m"), "par"], writes=[ckey])
                    P.stt(cv[:], hs[:, 1:TT + 1], w1c, cv[:], ALU.mult, ALU.add,
                          reads=[(hkey, "m"), (hkey, "c"), ckey, "par"], writes=[ckey])
                    P.stt(cv[:], hs[:, 0:TT], w0c, cv[:], ALU.mult, ALU.add,
                          reads=[(hkey, "m"), (hkey, "c"), ckey, "par"], writes=[ckey])
                    cvs[which] = (cv, ckey)
                sg, skey = sg_r.next()
                P.act(sg[:], cvs[0][0][:], AF.Silu, reads=[cvs[0][1]], writes=[skey])
                P.tt("dve", gT[:, j, :], sg[:], cvs[1][0][:], ALU.mult, reads=[skey, cvs[1][1]], writes=[("gT", j)])
            if tile + 1 < NT:
                nxt = load_xtile(P, E, xres_r, xb_r, tile + 1)

            def emit_y(m, out_ap, okey):
                for j in range(NFC):
                    P.mm(out_ap, wdn[:, j, m * 128:(m + 1) * 128], gT[:, j, :], start=(j == 0), stop=(j == NFC - 1),
                         reads=["wdn", ("gT", j)], writes=[okey])
            ln_tail(P, E, lnb, xres, xkey, PC_LN2G, PC_LN2B, emit_y, tile, ybanks, sbanks)
        P.flush()


def phase_even(P, E, l):
    nc = E.nc
    i = l // 2
    lam_init = 0.8 - 0.6 * math.exp(-0.3 * l)
    with contextlib.ExitStack() as st:
        load_par(P, E, st, l)
        par = E.par
        win = st.enter_context(nc.sbuf_tensor(U("win"), [128, 8, 4096], BF16))
        wout = st.enter_context(nc.sbuf_tensor(U("wout"), [128, 8, D], BF16))
        for c in range(8):
            P.dma("pool", win[:, c, :], E.d_ev_win[i, c * 128:(c + 1) * 128, :], "w_in", writes=["win"],
                  max_dma_last_dim=4096)
        load_weight(P, E, wout, E.d_ev_wout[i], "wout", nsplit=2)
        KT = st.enter_context(nc.sbuf_tensor(U("KT"), [128, 4, T], BF16))
        VP = st.enter_context(nc.sbuf_tensor(U("VP"), [128, T // 128, 4, 129], BF16))
        P.memset("pool", VP[:], 1.0, writes=[("VP", kb) for kb in range(T // 128)])
        lamp = st.enter_context(nc.sbuf_tensor(U("lamp"), [128, 256], F32))
        gsub = st.enter_context(nc.sbuf_tensor(U("gsub_sb"), [128, 128], F32))
        lsm = st.enter_context(nc.sbuf_tensor(U("lsm"), [128, 8], F32))
        lpr = st.enter_context(nc.sbuf_tensor(U("lpr"), [128, 2, 64], F32))
        P.dma("sp", lamp[:], E.d_lamp[i], "lamp", writes=["lamp"])
        P.dma("sp", gsub[:], E.d_gsub[i], "gsubd", writes=["gsub"])
        P.ts("dve", gsub[:], gsub[:], float(1.0 - lam_init), None, ALU.mult, reads=["gsub"], writes=["gsub"])
        P.tt("dve", lpr[:, 0, :], lamp[:, 0:64], lamp[:, 64:128], ALU.mult, reads=["lamp"], writes=["lpr"])
        P.tt("dve", lpr[:, 1, :], lamp[:, 128:192], lamp[:, 192:256], ALU.mult, reads=["lamp"], writes=["lpr"])
        P.add("dve", lambda e: e.reduce_sum(lsm[:, 0:1], lpr[:, 0, :], AX.X), ["lpr"], ["lsm"])
        P.add("dve", lambda e: e.reduce_sum(lsm[:, 1:2], lpr[:, 1, :], AX.X), ["lpr"], ["lsm"])
        P.act(lsm[:, 2:4], lsm[:, 0:2], AF.Exp, reads=["lsm"], writes=["lsm"])
        P.tt("dve", lsm[:, 4:5], lsm[:, 3:4], lsm[:, 2:3], ALU.subtract, reads=["lsm"], writes=["lsm"])
        P.ts("dve", lsm[:, 5:6], lsm[:, 4:5], float(-lam_init), None, ALU.add, reads=["lsm"], writes=["neglam"])
        neglam = lsm[:, 5:6]
        lnb = alloc_ln(st, nc)
        xres_r = Ring(st, nc, "xres", [128, 8, TT], F32, 2)
        xb_r = Ring(st, nc, "xb", [128, 8, TT], BF16, 1)
        catT = st.enter_context(nc.sbuf_tensor(U("catT"), [128, 8, TT], BF16))
        QT = st.enter_context(nc.sbuf_tensor(U("QT"), [128, 4, TT], BF16))
        carry = st.enter_context(nc.sbuf_tensor(U("carry_u"), [128, 4, 2], F32))
        P.memset("pool", carry[:], 0.0, writes=[("carry", c) for c in range(4)])
        rc_r = Ring(st, nc, "rc", [128, TT], F32, 1)
        rs_r = Ring(st, nc, "rs", [128, TT], F32, 1)
        gcs_r = Ring(st, nc, "gcs", [128, TT], F32, 1)
        u_r = Ring(st, nc, "u", [128, TT + 2], F32, 2)
        cv_r = Ring(st, nc, "cv", [128, TT], F32, 1)
        t1_r = Ring(st, nc, "t1", [128, TT], F32, 1)
        t2_r = Ring(st, nc, "t2", [128, TT], F32, 1)
        pT_r = Ring(st, nc, "pT", [128, 4, 128], BF16, 3)
        sm_r = Ring(st, nc, "sm", [128, 8], F32, 2)
        tn_r = Ring(st, nc, "tn", [128, 128], F32, 2)
        o_r = Ring(st, nc, "o", [128, 128], F32, 2)
        on_r = Ring(st, nc, "on", [128, 128], F32, 2)
        junk = st.enter_context(nc.sbuf_tensor(U("junk"), [128, 128], BF16))
        ibanks = bank_ring(E, [0, 1, 2, 3])
        abanks = bank_ring(E, [4, 5, 6, 7])
        ybanks = bank_ring(E, [4, 5])
        sbanks = bank_ring(E, [6, 7])
        maskUI = E.cst[:, C_UI:C_UI + 128]
        ident = E.cst[:, C_ID:C_ID + 128]

        def proj(bk, bkey, col0, xb, xbkey):
            for c in range(8):
                P.mm(bk[:, 0:TT], win[:, c, col0:col0 + 128], xb[:, c, :], start=(c == 0), stop=(c == 7),
                     reads=["win", xbkey], writes=[bkey])

        nxt = load_xtile(P, E, xres_r, xb_r, 0)
        for tile in range(NT):
            xres, xkey, xb, xbkey = nxt
            tsl = slice(tile * TT, (tile + 1) * TT)
            rc, rckey = rc_r.next()
            rs, rskey = rs_r.next()
            P.dma("sp", rc[:], E.d_ropec[:, tsl], "rc" + str(rckey[1]), writes=[rckey])
            P.dma("sp", rs[:], E.d_ropes[:, tsl], "rs" + str(rskey[1]), writes=[rskey])
            for cc in range(4):
                b_gc, k_gc = ibanks.next()
                proj(b_gc, k_gc, 512 + cc * 128, xb, xbkey)
                b_xi, k_xi = ibanks.next()
                proj(b_xi, k_xi, 1024 + cc * 128, xb, xbkey)
                b_gb, k_gb = ibanks.next()
                proj(b_gb, k_gb, cc * 128, xb, xbkey)
                gcs, gkey = gcs_r.next()
                P.copy("act", gcs[:], b_gc[:, 0:TT], reads=[k_gc], writes=[gkey])
                u, ukey = u_r.next()
                P.copy("pool", u[:, 0:2], carry[:, cc, :], reads=[("carry", cc)], writes=[(ukey, "c")])
                P.tt("dve", u[:, 2:TT + 2], b_xi[:, 0:TT], gcs[:], ALU.mult, reads=[k_xi, gkey], writes=[(ukey, "m")])
                P.copy("pool", carry[:, cc, :], u[:, TT:TT + 2], reads=[(ukey, "m")], writes=[("carry", cc)])
                cv, ckey = cv_r.next()
                w0c = par[:, PC_ECW + cc:PC_ECW + cc + 1]
                w1c = par[:, PC_ECW + 4 + cc:PC_ECW + 4 + cc + 1]
                w2c = par[:, PC_ECW + 8 + cc:PC_ECW + 8 + cc + 1]
                P.ts("dve", cv[:], u[:, 2:TT + 2], w2c, None, ALU.mult, reads=[(ukey, "m"), "par"], writes=[ckey])
                P.stt(cv[:], u[:, 1:TT + 1], w1c, cv[:], ALU.mult, ALU.add, reads=[(ukey, "m"), (ukey, "c"), ckey], writes=[ckey])
                P.stt(cv[:], u[:, 0:TT], w0c, cv[:], ALU.mult, ALU.add, reads=[(ukey, "m"), (ukey, "c"), ckey], writes=[ckey])
                P.tt("dve", catT[:, cc, :], b_gb[:, 0:TT], cv[:], ALU.mult, reads=[k_gb, ckey], writes=[("catT", cc)])
            for cc in range(4):
                for (col0, colsw, dst, dkeyw) in ((1536, 3072, QT[:, cc, :], ("QT", cc)),
                                                  (2048, 3584, KT[:, cc, tsl], ("KT", tile, cc))):
                    b1, k1 = ibanks.next()
                    proj(b1, k1, col0 + cc * 128, xb, xbkey)
                    b2, k2 = ibanks.next()
                    proj(b2, k2, colsw + cc * 128, xb, xbkey)
                    t1, t1k = t1_r.next()
                    t2, t2k = t2_r.next()
                    P.tt("dve", t1[:], b1[:, 0:TT], rc[:], ALU.mult, reads=[k1, rckey], writes=[t1k])
                    P.tt("dve", t2[:], b2[:, 0:TT], rs[:], ALU.mult, reads=[k2, rskey], writes=[t2k])
                    P.tt("pool", dst, t1[:], t2[:], ALU.add, reads=[t1k, t2k], writes=[dkeyw])
            for sub in range(TT // 128):
                kb = tile * (TT // 128) + sub
                bv, kv = ibanks.next()
                for c in range(8):
                    P.mm(bv[:], xb[:, c, sub * 128:(sub + 1) * 128], win[:, c, 2560:3072], start=(c == 0), stop=(c == 7),
                         reads=["win", xbkey], writes=[kv])
                P.copy("act", VP[:, kb, :, 0:128], bv[:].rearrange("p (h d) -> p h d", h=4), reads=[kv], writes=[("VP", kb)])
            for sub in range(TT // 128):
                qb = tile * (TT // 128) + sub
                qsl = slice(sub * 128, (sub + 1) * 128)
                for h in range(4):
                    accs = [abanks.next(), abanks.next()]
                    kbs = list(range(qb + 1))
                    for g0 in range(0, qb + 1, 4):
                        grp = kbs[g0:g0 + 4]
                        ng = len(grp)
                        for m in range(2):
                            sbk, skey = ibanks.next()
                            for g, kb in enumerate(grp):
                                P.mm(sbk[:, g * 128:(g + 1) * 128], KT[64 * m:64 * m + 64, h, kb * 128:(kb + 1) * 128],
                                     QT[64 * m:64 * m + 64, h, qsl], start=True, stop=True,
                                     reads=[("KT", kb // (TT // 128), h), ("QT", h)], writes=[skey])
                            pT, pkey = pT_r.next()
                            P.act(pT[:, 0:ng, :], sbk[:, 0:ng * 128].rearrange("p (g q) -> p g q", g=ng), AF.Exp, scale=0.125,
                                  reads=[skey], writes=[pkey])
                            if qb in grp:
                                gd = grp.index(qb)
                                P.tt("pool", pT[:, gd, :], pT[:, gd, :], maskUI, ALU.mult, reads=[pkey, "consts"], writes=[pkey])
                            acc, akey = accs[m]
                            for g, kb in enumerate(grp):
                                P.mm(acc[:, 0:129], pT[:, g, :], VP[:, kb, h, :], start=(kb == 0), stop=(kb == qb),
                                     reads=[pkey, ("VP", kb)], writes=[akey])
                    (acc0, ak0), (acc1, ak1) = accs
                    sm, smk = sm_r.next()
                    P.recip(sm[:, 0:1], acc0[:, 128:129], reads=[ak0], writes=[(smk, 0)])
                    P.recip(sm[:, 1:2], acc1[:, 128:129], reads=[ak1], writes=[(smk, 1)])
                    P.tt("dve", sm[:, 2:3], sm[:, 1:2], neglam, ALU.mult, reads=[(smk, 1), "neglam"], writes=[(smk, 2)])
                    tn, tnk = tn_r.next()
                    P.ts("dve", tn[:], acc1[:, 0:128], sm[:, 2:3], None, ALU.mult, reads=[ak1, (smk, 2)], writes=[tnk])
                    o, ok = o_r.next()
                    P.stt(o[:], acc0[:, 0:128], sm[:, 0:1], tn[:], ALU.mult, ALU.add, reads=[ak0, (smk, 0), tnk], writes=[ok])
                    P.act(junk[:], o[:], AF.Square, accum_out=sm[:, 3:4], reads=[ok], writes=["junk", (smk, 3)])
                    P.ts("dve", sm[:, 4:5], sm[:, 3:4], 1.0 / 128.0, 1e-5, ALU.mult, ALU.add, reads=[(smk, 3)], writes=[(smk, 4)])
                    P.act(sm[:, 4:5], sm[:, 4:5], AF.Sqrt, reads=[(smk, 4)], writes=[(smk, 4)])
                    P.recip(sm[:, 5:6], sm[:, 4:5], reads=[(smk, 4)], writes=[(smk, 5)])
                    on, onk = on_r.next()
                    P.stt(on[:], o[:], sm[:, 5:6], gsub[:], ALU.mult, ALU.mult, reads=[ok, (smk, 5), "gsub"], writes=[onk])
                    trb, trk = ibanks.next()
                    P.tr(trb[:, 0:128], on[:], ident, reads=[onk, "consts"], writes=[trk])
                    P.copy("act", catT[:, 4 + h, qsl], trb[:, 0:128], reads=[trk], writes=[("catT", 4 + h, sub)])
            if tile + 1 < NT:
                nxt = load_xtile(P, E, xres_r, xb_r, tile + 1)

            def emit_y(m, out_ap, okey):
                for c in range(8):
                    rd = [("catT", c)] if c < 4 else [("catT", c, sb_) for sb_ in range(TT // 128)]
                    P.mm(out_ap, wout[:, c, m * 128:(m + 1) * 128], catT[:, c, :], start=(c == 0), stop=(c == 7),
                         reads=["wout"] + rd, writes=[okey])
            ln_tail(P, E, lnb, xres, xkey, PC_LN1G, PC_LN1B, emit_y, tile, ybanks, sbanks)
        P.flush()


def phase_rwkv_a(P, E, l):
    nc = E.nc
    j = l // 2
    has_vres = j > 0
    NS = TT // 128
    with contextlib.ExitStack() as st:
        load_par(P, E, st, l)
        par = E.par
        wr = st.enter_context(nc.sbuf_tensor(U("wr"), [128, 8, D], BF16))
        wk = st.enter_context(nc.sbuf_tensor(U("wk"), [128, 8, D], BF16))
        wv = st.enter_context(nc.sbuf_tensor(U("wv"), [128, 8, D], BF16))
        load_weight(P, E, wr, E.d_rw_wr[j], "wr", nsplit=2)
        load_weight(P, E, wk, E.d_rw_wk[j], "wk", nsplit=2)
        load_weight(P, E, wv, E.d_rw_wv[j], "wv", nsplit=2)
        wl = st.enter_context(nc.sbuf_tensor(U("wl"), [128, 8, 320], BF16))
        P.dma("pool", wl[:, :, 0:64], E.d_rw_w1[j].rearrange("(c p) n -> p c n", p=128), "w_l", writes=["wl"])
        P.dma("pool", wl[:, :, 64:128], E.d_rw_a1[j].rearrange("(c p) n -> p c n", p=128), "w_l", writes=["wl"])
        P.dma("pool", wl[:, :, 128:288], E.d_rw_g1[j].rearrange("(c p) n -> p c n", p=128), "w_l", writes=["wl"])
        if has_vres:
            P.dma("pool", wl[:, :, 288:320], E.d_rw_v1[j - 1].rearrange("(c p) n -> p c n", p=128), "w_l", writes=["wl"])
        w2s = st.enter_context(nc.sbuf_tensor(U("w2s"), [64, D], BF16))
        a2s = st.enter_context(nc.sbuf_tensor(U("a2s"), [64, D], BF16))
        g2s = st.enter_context(nc.sbuf_tensor(U("g2s"), [128, 2, D], BF16))
        v2s = st.enter_context(nc.sbuf_tensor(U("v2s"), [32, D], BF16))
        P.dma("pool", w2s[:], E.d_rw_w2[j], "w_l", writes=["wl"], max_dma_last_dim=4096)
        P.dma("pool", a2s[:], E.d_rw_a2[j], "w_l", writes=["wl"], max_dma_last_dim=4096)
        P.dma("pool", g2s[:, 0, :], E.d_rw_g2[j, 0:128, :], "w_l", writes=["wl"], max_dma_last_dim=4096)
        P.dma("pool", g2s[0:32, 1, :], E.d_rw_g2[j, 128:160, :], "w_l", writes=["wl"], max_dma_last_dim=4096)
        if has_vres:
            P.dma("pool", v2s[:], E.d_rw_v2[j - 1], "w_l", writes=["wl"], max_dma_last_dim=4096)
        rowf = st.enter_context(nc.sbuf_tensor(U("rowf"), [1, 2, D], F32))
        P.dma("sp", rowf[:], E.d_rowp[j:j + 1, :, :], "rowf", writes=["rowf"])
        ones1 = st.enter_context(nc.sbuf_tensor(U("ones1"), [1, 128], F32))
        P.memset("pool", ones1[:], 1.0, writes=["ones1"])
        omka = st.enter_context(nc.sbuf_tensor(U("omka"), [128, 8], F32))
        P.ts("dve", omka[:], par[:, PC_KA:PC_KA + 8], -1.0, 1.0, ALU.mult, ALU.add, reads=["par"], writes=["omka"])
        xres_r = Ring(st, nc, "xres", [128, 8, TT], F32, 2)
        xx = st.enter_context(nc.sbuf_tensor(U("xx"), [128, 8, TT], F32))
        xprev = st.enter_context(nc.sbuf_tensor(U("xprev"), [128, 8, 1], F32))
        P.memset("pool", xprev[:], 0.0, writes=["xprev"])
        mx_r = Ring(st, nc, "mx", [128, 8, TT], BF16, 2)
        rbuf = st.enter_context(nc.sbuf_tensor(U("rbuf"), [128, 8, TT], F32))
        kbuf = st.enter_context(nc.sbuf_tensor(U("kbuf"), [128, 8, TT], F32))
        abuf = st.enter_context(nc.sbuf_tensor(U("abuf"), [128, 8, TT], F32))
        gbuf = st.enter_context(nc.sbuf_tensor(U("gbuf"), [128, 8, TT], BF16))
        kkbuf = st.enter_context(nc.sbuf_tensor(U("kkbuf"), [128, 8, TT], F32))
        k2buf = st.enter_context(nc.sbuf_tensor(U("k2buf"), [128, 8, TT], F32))
        prbuf = st.enter_context(nc.sbuf_tensor(U("prbuf"), [128, 8, TT], BF16))
        h1 = st.enter_context(nc.sbuf_tensor(U("h1"), [64, TT], BF16))
        ha = st.enter_context(nc.sbuf_tensor(U("ha"), [64, TT], BF16))
        hv = st.enter_context(nc.sbuf_tensor(U("hv"), [32, TT], BF16))
        hgb = st.enter_context(nc.sbuf_tensor(U("hgb"), [128, 2, TT], BF16))
        sqb_r = Ring(st, nc, "sqb", [128, TT], BF16, 2)
        sd_r = Ring(st, nc, "sd", [128, TT], F32, 2)
        f_r = Ring(st, nc, "f", [128, TT], F32, 2)
        sgt_r = Ring(st, nc, "sgt", [128, D], F32, 1)
        vt_r = Ring(st, nc, "vt", [128, D], F32, 1)
        if has_vres:
            sgv = st.enter_context(nc.sbuf_tensor(U("sgv"), [128, D], F32))
            vf = st.enter_context(nc.sbuf_tensor(U("vf"), [128, D], F32))
            dd = st.enter_context(nc.sbuf_tensor(U("dd"), [128, D], F32))
        banks = bank_ring(E, [0, 1, 2, 3, 4, 5, 6, 7])

        def mixed(q, xres, xkey):
            mx, mk = mx_r.next()
            for c in range(8):
                P.stt(mx[:, c, :], xx[:, c, :], par[:, PC_MIX + q * 8 + c:PC_MIX + q * 8 + c + 1], xres[:, c, :],
                      ALU.mult, ALU.add, reads=["xx", xkey, "par"], writes=[(mk, c)])
            return mx, [(mk, c) for c in range(8)]

        def proj_fm(w, m, mx, mkeys, bk, bkey, M=128, col0=None):
            cs = slice(m * 128, (m + 1) * 128) if col0 is None else slice(col0, col0 + M)
            for c in range(8):
                P.mm(bk[0:M, 0:TT], w[:, c, cs], mx[:, c, :], start=(c == 0), stop=(c == 7),
                     reads=["wr", "wk", "wv", "wl", mkeys[c]], writes=[bkey])

        for tile in range(int(_os.environ.get("RW_NT", NT))):
            xres, xkey = xres_r.next()
            tsl = slice(tile * TT, (tile + 1) * TT)
            P.dma("sp", xres[:], E.xs[:, :, tsl].rearrange("c p t -> p c t"), "xs_ld" + str(xkey[1]),
                  reads=[("xs", tile)], writes=[xkey])
            P.tt("pool", xx[:, :, 1:TT], xres[:, :, 0:TT - 1], xres[:, :, 1:TT], ALU.subtract, reads=[xkey], writes=["xx"])
            P.tt("pool", xx[:, :, 0:1], xprev[:], xres[:, :, 0:1], ALU.subtract, reads=[xkey, "xprev"], writes=["xx"])
            P.copy("pool", xprev[:], xres[:, :, TT - 1:TT], reads=[xkey], writes=["xprev"])
            mx, mkeys = mixed(0, xres, xkey)
            for m in range(8):
                bk, bkey = banks.next()
                proj_fm(wr, m, mx, mkeys, bk, bkey)
                P.copy("act", rbuf[:, m, :], bk[:, 0:TT], reads=[bkey], writes=[("rbuf", m)])
            mx, mkeys = mixed(1, xres, xkey)
            bk, bkey = banks.next()
            proj_fm(wl, 0, mx, mkeys, bk, bkey, M=64, col0=0)
            P.act(h1[:], bk[0:64, 0:TT], AF.Tanh, reads=[bkey], writes=["h1"])
            for sub in range(NS):
                sgt, sgk = sgt_r.next()
                for half in range(2):
                    bk, bkey = banks.next()
                    hs_ = slice(half * 512, (half + 1) * 512)
                    P.mm(bk[:], h1[0:64, sub * 128:(sub + 1) * 128], w2s[0:64, hs_], start=True, stop=False,
                         reads=["h1", "wr", "wk", "wv", "wl"], writes=[bkey])
                    P.mm(bk[:], ones1[0:1, :], rowf[0:1, 0, hs_], start=False, stop=True, reads=["ones1", "rowf"], writes=[bkey])
                    P.act(sgt[:, hs_], bk[:], AF.Sigmoid, reads=[bkey], writes=[(sgk, half)])
                r0 = tile * TT + sub * 128
                P.dma("sp", E.rwsg[r0:r0 + 128, :], sgt[:], "st_sg", reads=[(sgk, 0), (sgk, 1)], writes=[("rwsg", r0)])
            mx, mkeys = mixed(2, xres, xkey)
            for m in range(8):
                bk, bkey = banks.next()
                proj_fm(wk, m, mx, mkeys, bk, bkey)
                P.copy("act", kbuf[:, m, :], bk[:, 0:TT], reads=[bkey], writes=[("kbuf", m)])
            mx, mkeys = mixed(3, xres, xkey)
            if has_vres:
                bk, bkey = banks.next()
                proj_fm(wl, 0, mx, mkeys, bk, bkey, M=32, col0=288)
                P.copy("act", hv[:], bk[0:32, 0:TT], reads=[bkey], writes=["hv"])
            for sub in range(NS):
                r0 = tile * TT + sub * 128
                vt, vk = vt_r.next()
                vbk = []
                for half in range(2):
                    bk, bkey = banks.next()
                    hs_ = slice(half * 512, (half + 1) * 512)
                    for c in range(8):
                        P.mm(bk[:], mx[:, c, sub * 128:(sub + 1) * 128], wv[:, c, hs_], start=(c == 0), stop=(c == 7),
                             reads=["wr", "wk", "wv", "wl", mkeys[c]], writes=[bkey])
                    vbk.append((bk, bkey))
                if has_vres:
                    P.dma("sp", vf[:], E.vfirst[r0:r0 + 128, :], "ld_vf", reads=[("vfirst", r0)], writes=["vf"])
                    for half in range(2):
                        hs_ = slice(half * 512, (half + 1) * 512)
                        bk, bkey = banks.next()
                        P.mm(bk[:], hv[0:32, sub * 128:(sub + 1) * 128], v2s[0:32, hs_], start=True, stop=False,
                             reads=["hv", "wr", "wk", "wv", "wl"], writes=[bkey])
                        P.mm(bk[:], ones1[0:1, :], rowf[0:1, 1, hs_], start=False, stop=True, reads=["ones1", "rowf"], writes=[bkey])
                        P.act(sgv[:, hs_], bk[:], AF.Sigmoid, reads=[bkey], writes=[("sgv", half)])
                        vb_, vbk_ = vbk[half]
                        P.tt("dve", dd[:, hs_], vf[:, hs_], vb_[:], ALU.subtract, reads=["vf", vbk_], writes=[("dd", half)])
                        P.tt("pool", dd[:, hs_], dd[:, hs_], sgv[:, hs_], ALU.mult, reads=[("dd", half), ("sgv", half)], writes=[("dd", half)])
                        P.tt("dve", vt[:, hs_], dd[:, hs_], vb_[:], ALU.add, reads=[("dd", half), vbk_], writes=[(vk, half)])
                else:
                    for half in range(2):
                        hs_ = slice(half * 512, (half + 1) * 512)
                        vb_, vbk_ = vbk[half]
                        P.copy("act", vt[:, hs_], vb_[:], reads=[vbk_], writes=[(vk, half)])
                    P.dma("sp", E.vfirst[r0:r0 + 128, :], vt[:], "st_vf", reads=[(vk, 0), (vk, 1)], writes=[("vfirst", r0)])
                P.dma("sp", E.rwv[r0:r0 + 128, :], vt[:], "st_v", reads=[(vk, 0), (vk, 1)], writes=[("rwv", r0)])
            mx, mkeys = mixed(4, xres, xkey)
            bk, bkey = banks.next()
            proj_fm(wl, 0, mx, mkeys, bk, bkey, M=64, col0=64)
            P.copy("act", ha[:], bk[0:64, 0:TT], reads=[bkey], writes=["ha"])
            for m in range(8):
                bk, bkey = banks.next()
                P.mm(bk[:, 0:TT], a2s[0:64, m * 128:(m + 1) * 128], ha[0:64, :], start=True, stop=True,
                     reads=["ha", "wr", "wk", "wv", "wl"], writes=[bkey])
                P.act(abuf[:, m, :], bk[:, 0:TT], AF.Sigmoid, bias=par[:, PC_A0 + m:PC_A0 + m + 1],
                      reads=[bkey, "par"], writes=[("abuf", m)])
            mx, mkeys = mixed(5, xres, xkey)
            bk, bkey = banks.next()
            proj_fm(wl, 0, mx, mkeys, bk, bkey, M=128, col0=128)
            P.act(hgb[:, 0, :], bk[:, 0:TT], AF.Sigmoid, reads=[bkey], writes=[("hgb", 0)])
            bk, bkey = banks.next()
            proj_fm(wl, 0, mx, mkeys, bk, bkey, M=32, col0=256)
            P.act(hgb[0:32, 1, :], bk[0:32, 0:TT], AF.Sigmoid, reads=[bkey], writes=[("hgb", 1)])
            for m in range(8):
                bk, bkey = banks.next()
                P.mm(bk[:, 0:TT], g2s[:, 0, m * 128:(m + 1) * 128], hgb[:, 0, :], start=True, stop=False,
                     reads=[("hgb", 0), "wr", "wk", "wv", "wl"], writes=[bkey])
                P.mm(bk[:, 0:TT], g2s[0:32, 1, m * 128:(m + 1) * 128], hgb[0:32, 1, :], start=False, stop=True,
                     reads=[("hgb", 1), "wr", "wk", "wv", "wl"], writes=[bkey])
                P.copy("act", gbuf[:, m, :], bk[:, 0:TT], reads=[bkey], writes=[("gbuf", m)])
            for m in range(8):
                pc = lambda base: par[:, base + m:base + m + 1]
                P.ts("dve", kkbuf[:, m, :], kbuf[:, m, :], pc(PC_KK), None, ALU.mult, reads=[("kbuf", m), "par"], writes=[("kk", m)])
                sqb, sqk = sqb_r.next()
                P.act(sqb[:], kkbuf[:, m, :], AF.Square, reads=[("kk", m)], writes=[sqk])
                bk, bkey = banks.next()
                P.mm(bk[:, 0:TT], E.bones[:], sqb[:], start=True, stop=True, reads=[sqk, "consts2"], writes=[bkey])
                sd, sdk = sd_r.next()
                P.act(sd[:], bk[:, 0:TT], AF.Sqrt, reads=[bkey], writes=[sdk])
                P.ts("pool", sd[:], sd[:], 1e-12, None, ALU.max, reads=[sdk], writes=[sdk])
                P.recip(sd[:], sd[:], reads=[sdk], writes=[sdk])
                P.tt("dve", kkbuf[:, m, :], kkbuf[:, m, :], sd[:], ALU.mult, reads=[("kk", m), sdk], writes=[("kk", m)])
                f, fk = f_r.next()
                P.ts("dve", f[:], abuf[:, m, :], pc(PC_KA), None, ALU.mult, reads=[("abuf", m), "par"], writes=[fk])
                P.stt(k2buf[:, m, :], f[:], omka[:, m:m + 1], kbuf[:, m, :], ALU.add, ALU.mult,
                      reads=[fk, "omka", ("kbuf", m)], writes=[("k2", m)])
                P.tt("pool", abuf[:, m, :], abuf[:, m, :], kkbuf[:, m, :], ALU.mult, reads=[("abuf", m), ("kk", m)], writes=[("abuf", m)])
                P.stt(prbuf[:, m, :], rbuf[:, m, :], pc(PC_RK), k2buf[:, m, :], ALU.mult, ALU.mult,
                      reads=[("rbuf", m), ("k2", m), "par"], writes=[("pr", m)])
            for idx, (buf, kn) in enumerate(((rbuf, "rbuf"), (k2buf, "k2"), (kkbuf, "kk"), (abuf, "abuf"))):
                P.dma("sp", E.rwd[idx, :, :, tsl].rearrange("c p t -> p c t"), buf[:], "st_rwd%d" % idx,
                      reads=[(kn, m) for m in range(8)], writes=[("rwd", idx, tile)])
            for idx, (buf, kn) in enumerate(((gbuf, "gbuf"), (prbuf, "pr"))):
                P.dma("sp", E.rwdb[idx, :, :, tsl].rearrange("c p t -> p c t"), buf[:], "st_rwdb%d" % idx,
                      reads=[(kn, m) for m in range(8)], writes=[("rwdb", idx, tile)])
        P.flush()


def phase_rwkv_b(P, E, l):
    nc = E.nc
    j = l // 2
    GN_EPS = 64e-5
    with contextlib.ExitStack() as st:
        load_par(P, E, st, l)
        par = E.par
        wo = st.enter_context(nc.sbuf_tensor(U("wo"), [128, 8, D], BF16))
        load_weight(P, E, wo, E.d_rw_wo[j], "wo", nsplit=2)
        mSU4 = st.enter_context(nc.sbuf_tensor(U("mSU4"), [128, 4, 128], F32))
        mUI4 = st.enter_context(nc.sbuf_tensor(U("mUI4"), [128, 4, 128], F32))
        mSL4 = st.enter_context(nc.sbuf_tensor(U("mSL4"), [128, 4, 128], F32))
        id4 = st.enter_context(nc.sbuf_tensor(U("id4"), [128, 4, 128], F32))
        for q in range(4):
            P.copy("pool", mSU4[:, q, :], E.cst[:, C_SU:C_SU + 128], reads=["consts"], writes=["m4"])
            P.copy("pool", mUI4[:, q, :], E.cst[:, C_UI:C_UI + 128], reads=["consts"], writes=["m4"])
            P.copy("pool", mSL4[:, q, :], E.cst[:, C_SL:C_SL + 128], reads=["consts"], writes=["m4"])
            P.copy("pool", id4[:, q, :], E.cst[:, C_ID:C_ID + 128], reads=["consts"], writes=["m4"])
        bones64 = st.enter_context(nc.sbuf_tensor(U("bones64"), [128, 128], BF16))
        P.ts("dve", bones64[:], E.cst[:, C_BO:C_BO + 128], 1.0 / 64.0, None, ALU.mult, reads=["consts"], writes=["m4"])
        maskUI = E.cst[:, C_UI:C_UI + 128]
        maskSU = E.cst[:, C_SU:C_SU + 128]
        ident = E.cst[:, C_ID:C_ID + 128]
        Pf = st.enter_context(nc.sbuf_tensor(U("Pf"), [128, 512], F32))
        Pb = st.enter_context(nc.sbuf_tensor(U("Pb"), [128, 512], BF16))
        P.memset("pool", Pf[:], 0.0, writes=["Pf"])
        P.memset("pool", Pb[:], 0.0, writes=["Pb"])
        lnb = alloc_ln(st, nc)
        xres_r = Ring(st, nc, "xres", [128, 8, TT], F32, 1)
        zT = st.enter_context(nc.sbuf_tensor(U("zT"), [128, 8, TT], BF16))
        fm_r = [Ring(st, nc, "fm%d" % i, [128, 8, 128], F32, 2) for i in range(4)]
        fb_r = [Ring(st, nc, "fb%d" % i, [128, 8, 128], BF16, 2) for i in range(2)]
        sg_r = Ring(st, nc, "sgl", [128, D], F32, 2)
        vt_r = Ring(st, nc, "vtl", [128, D], F32, 2)
        vb = st.enter_context(nc.sbuf_tensor(U("vb"), [128, D], BF16))
        gam = st.enter_context(nc.sbuf_tensor(U("gam"), [128, 4, 128], F32))
        ginv = st.enter_context(nc.sbuf_tensor(U("ginv"), [128, 4, 128], F32))
        game = st.enter_context(nc.sbuf_tensor(U("game"), [128, 4, 128], F32))
        ghat = st.enter_context(nc.sbuf_tensor(U("ghat"), [128, 4, 128], F32))
        gsm = st.enter_context(nc.sbuf_tensor(U("gsm"), [128, 16], F32))
        Rt = st.enter_context(nc.sbuf_tensor(U("Rt"), [128, 8, 128], BF16))
        At = st.enter_context(nc.sbuf_tensor(U("At"), [128, 8, 128], BF16))
        Bt = st.enter_context(nc.sbuf_tensor(U("Bt"), [128, 8, 128], BF16))
        Kt = st.enter_context(nc.sbuf_tensor(U("Kt"), [128, 8, 128], BF16))
        Atf = st.enter_context(nc.sbuf_tensor(U("Atf"), [128, 4, 128], F32))
        Bhf = st.enter_context(nc.sbuf_tensor(U("Bhf"), [128, 4, 128], F32))
        Khf = st.enter_context(nc.sbuf_tensor(U("Khf"), [128, 4, 128], F32))
        Atm = st.enter_context(nc.sbuf_tensor(U("Atm"), [128, 8, 128], BF16))
        Bhm = st.enter_context(nc.sbuf_tensor(U("Bhm"), [128, 8, 128], BF16))
        Khm = st.enter_context(nc.sbuf_tensor(U("Khm"), [128, 8, 128], BF16))
        LT_r = Ring(st, nc, "LTs", [128, 4, 128], BF16, 2)
        L_r = Ring(st, nc, "Ls", [128, 4, 128], BF16, 2)
        TT_r = Ring(st, nc, "TTm", [128, 4, 128], BF16, 2)
        TTall = st.enter_context(nc.sbuf_tensor(U("TTall"), [128, 16, 128], BF16))
        Lak_r = Ring(st, nc, "LakTs", [128, 4, 128], BF16, 2)
        Y2s = st.enter_context(nc.sbuf_tensor(U("Y2s"), [128, 16, 64], BF16))
        WTs = st.enter_context(nc.sbuf_tensor(U("WTs"), [128, 8, 128], BF16))
        MrbT = st.enter_context(nc.sbuf_tensor(U("MrbT"), [128, 16, 128], BF16))
        MrkT = st.enter_context(nc.sbuf_tensor(U("MrkT"), [128, 16, 128], BF16))
        Us = st.enter_context(nc.sbuf_tensor(U("Us"), [128, 16, 64], BF16))
        Os = st.enter_context(nc.sbuf_tensor(U("Os"), [128, D], F32))
        ob = st.enter_context(nc.sbuf_tensor(U("ob"), [128, 8, 128], BF16))
        osq = st.enter_context(nc.sbuf_tensor(U("osq"), [128, 8, 128], BF16))
        OTf = st.enter_context(nc.sbuf_tensor(U("OTf"), [128, 4, 128], F32))
        vTs = st.enter_context(nc.sbuf_tensor(U("vTs"), [128, 4, 128], F32))
        gmean = st.enter_context(nc.sbuf_tensor(U("gmean"), [128, 4, 128], F32))
        gm2 = st.enter_context(nc.sbuf_tensor(U("gm2"), [128, 4, 128], F32))
        grstd = st.enter_context(nc.sbuf_tensor(U("grstd"), [128, 4, 128], F32))
        gon = st.enter_context(nc.sbuf_tensor(U("gon"), [128, 4, 128], F32))
        gt = st.enter_context(nc.sbuf_tensor(U("gtt"), [128, 4, 128], F32))
        tb = bank_ring(E, [0, 1, 2, 3])
        ybanks = bank_ring(E, [4, 5])
        sbanks = bank_ring(E, [6, 7])
        UB = [(E.PB[4], ("pb", 4)), (E.PB[5], ("pb", 5))]
        OB = [(E.PB[6], ("pb", 6)), (E.PB[7], ("pb", 7))]

        def v4(bank):
            return bank[:].rearrange("p (q t) -> p q t", q=4)

        xcur = None
        for ch in range(int(_os.environ.get("RW_NCH", T // 128))):
            tile, sub = ch // 2, ch % 2
            csl = slice(ch * 128, (ch + 1) * 128)
            if sub == 0:
                xres, xkey = xres_r.next()
                P.dma("sp", xres[:], E.xs[:, :, tile * TT:(tile + 1) * TT].rearrange("c p t -> p c t"), "xs_ld" + str(xkey[1]),
                      reads=[("xs", tile)], writes=[xkey])
                xcur = (xres, xkey)
            fm, fmk = [], []
            for i in range(4):
                t_, k_ = fm_r[i].next()
                P.dma("sp", t_[:], E.rwd[i, :, :, csl].rearrange("c p t -> p c t"), "ld_fm%d_%d" % (i, k_[1]),
                      reads=[("rwd", i, tile)], writes=[k_])
                fm.append(t_)
                fmk.append(k_)
            fb, fbk = [], []
            for i in range(2):
                t_, k_ = fb_r[i].next()
                P.dma("sp", t_[:], E.rwdb[i, :, :, csl].rearrange("c p t -> p c t"), "ld_fb%d_%d" % (i, k_[1]),
                      reads=[("rwdb", i, tile)], writes=[k_])
                fb.append(t_)
                fbk.append(k_)
            sg, sgk = sg_r.next()
            P.dma("sp", sg[:], E.rwsg[csl, :], "ld_sg%d" % sgk[1], reads=[("rwsg", ch * 128)], writes=[sgk])
            vt, vtk = vt_r.next()
            P.dma("sp", vt[:], E.rwv[csl, :], "ld_vt%d" % vtk[1], reads=[("rwv", ch * 128)], writes=[vtk])
            P.copy("pool", vb[:], vt[:], reads=[vtk], writes=["vb"])
            r_f, k2_f, kk_f, bv_f = fm
            rk_, k2k_, kkk_, bvk_ = fmk
            g_b, pr_b = fb
            gk_, prk_ = fbk
            for grp in range(2):
                gi, gik = tb.next()
                ge, gek = tb.next()
                for p4 in range(4):
                    c = grp * 4 + p4
                    P.mm(gi[:, p4 * 128:(p4 + 1) * 128], sg[:, c * 128:(c + 1) * 128], maskUI, start=True, stop=True,
                         reads=[sgk, "consts"], writes=[gik])
                for p4 in range(4):
                    c = grp * 4 + p4
                    P.mm(ge[:, p4 * 128:(p4 + 1) * 128], sg[:, c * 128:(c + 1) * 128], maskSU, start=True, stop=True,
                         reads=[sgk, "consts"], writes=[gek])
                P.act(gam[:], v4(gi), AF.Exp, scale=-C0, reads=[gik], writes=["gam"])
                P.act(ginv[:], v4(gi), AF.Exp, scale=C0, reads=[gik], writes=["ginv"])
                P.act(game[:], v4(ge), AF.Exp, scale=-C0, reads=[gek], writes=["game"])
                for p4 in range(4):
                    c = grp * 4 + p4
                    last = gi[:, p4 * 128 + 127:p4 * 128 + 128]
                    P.act(gsm[:, c:c + 1], last, AF.Identity, scale=-C0, reads=[gik], writes=[("nb", c)])
                    P.act(gsm[:, 8 + c:9 + c], last, AF.Exp, scale=-C0, reads=[gik], writes=[("gC", c)])
                    P.act(ghat[:, p4, :], gi[:, p4 * 128:(p4 + 1) * 128], AF.Exp, scale=C0, bias=gsm[:, c:c + 1],
                          reads=[gik, ("nb", c)], writes=[("ghat", p4)])
                cs = slice(grp * 4, grp * 4 + 4)
                P.tt("dve", Rt[:, cs, :], r_f[:, cs, :], gam[:], ALU.mult, reads=[rk_, "gam"], writes=[("Rt", grp)])
                P.stt(Atf[:], kk_f[:, cs, :], -1.0, game[:], ALU.mult, ALU.mult, reads=[kkk_, "game"], writes=["Atf"])
                P.copy("pool", At[:, cs, :], Atf[:], reads=["Atf"], writes=[("At", grp)])
                P.tt("dve", Bt[:, cs, :], bv_f[:, cs, :], ginv[:], ALU.mult, reads=[bvk_, "ginv"], writes=[("Bt", grp)])
                P.tt("pool", Kt[:, cs, :], k2_f[:, cs, :], ginv[:], ALU.mult, reads=[k2k_, "ginv"], writes=[("Kt", grp)])
                P.tt("dve", Bhf[:], bv_f[:, cs, :], ghat[:], ALU.mult, reads=[bvk_] + [("ghat", q) for q in range(4)], writes=["Bhf"])
                P.tt("pool", Khf[:], k2_f[:, cs, :], ghat[:], ALU.mult, reads=[k2k_] + [("ghat", q) for q in range(4)], writes=["Khf"])
                for (src, skey, dst, dname, eng) in ((Atf, "Atf", Atm, "Atm", "act"), (Bhf, "Bhf", Bhm, "Bhm", "dve"), (Khf, "Khf", Khm, "Khm", "act")):
                    bk, bkey = tb.next()
                    for p4 in range(4):
                        P.tr(bk[:, p4 * 128:(p4 + 1) * 128], src[:, p4, :], ident, reads=[skey, "consts"], writes=[bkey])
                    P.copy(eng, dst[:, cs, :], v4(bk), reads=[bkey], writes=[(dname, grp)])
            if _DBG_STOP <= 1:
                continue
            for hg in range(4):
                par_i, pblk = hg % 2, hg // 2

                def hop(q):
                    c = 4 * pblk + q
                    h = 2 * c + par_i
                    return h, c, slice(64 * par_i, 64 * par_i + 64), pblk
                ltb, ltk = tb.next()
                lb, lk = tb.next()
                for q in range(4):
                    h, c, rs, g_ = hop(q)
                    P.mm(ltb[:, q * 128:(q + 1) * 128], Bt[rs, c, :], At[rs, c, :], start=True, stop=True,
                         reads=[("Bt", g_), ("At", g_)], writes=[ltk])
                for q in range(4):
                    h, c, rs, g_ = hop(q)
                    P.mm(lb[:, q * 128:(q + 1) * 128], At[rs, c, :], Bt[rs, c, :], start=True, stop=True,
                         reads=[("Bt", g_), ("At", g_)], writes=[lk])
                LTs, LTk = LT_r.next()
                Ls, Lk = L_r.next()
                if not _os.environ.get("X1"):
                    P.tt("dve", LTs[:], v4(ltb), mSU4[:], ALU.mult, reads=[ltk, "m4"], writes=[LTk])
                if not _os.environ.get("X2"):
                    P.tt("dve", Ls[:], v4(lb), mSL4[:], ALU.mult, reads=[lk, "m4"], writes=[Lk])
                TTm, TTk = TT_r.next()
                if not _os.environ.get("X3"):
                    P.tt("pool", TTm[:], LTs[:], id4[:], ALU.add, reads=[LTk, "m4"], writes=[TTk])
                if _DBG_SUB <= 1:
                    continue
                n = 2
                while n <= _DBG_NMAX:
                    lnb_, lnk = tb.next()
                    for q in range(4):
                        P.mm(lnb_[:, q * 128:(q + 1) * 128], LTs[:, q, :], Ls[:, q, :], start=True, stop=True,
                             reads=[LTk, Lk], writes=[lnk])
                    if n < 64:
                        ltnb, ltnk = tb.next()
                        for q in range(4):
                            P.mm(ltnb[:, q * 128:(q + 1) * 128], Ls[:, q, :], LTs[:, q, :], start=True, stop=True,
                                 reads=[LTk, Lk], writes=[ltnk])
                    Ls2, Lk2 = L_r.next()
                    P.copy("act", Ls2[:], v4(lnb_), reads=[lnk], writes=[Lk2])
                    if n < 64:
                        LTs2, LTk2 = LT_r.next()
                        P.copy("act", LTs2[:], v4(ltnb), reads=[ltnk], writes=[LTk2])
                    pb_, pk_ = tb.next()
                    for q in range(4):
                        P.mm(pb_[:, q * 128:(q + 1) * 128], Ls2[:, q, :], TTm[:, q, :], start=True, stop=True,
                             reads=[Lk2, TTk], writes=[pk_])
                    if n < 64:
                        TTm2, TTk2 = TT_r.next()
                        P.tt("dve", TTm2[:], v4(pb_), TTm[:], ALU.add, reads=[pk_, TTk], writes=[TTk2])
                        TTm, TTk = TTm2, TTk2
                        LTs, LTk = LTs2, LTk2
                    else:
                        P.tt("dve", TTall[:, 4 * hg:4 * hg + 4, :], v4(pb_), TTm[:], ALU.add, reads=[pk_, TTk], writes=[("TTall", hg)])
                    Ls, Lk = Ls2, Lk2
                    n *= 2
                if _DBG_SUB <= 2:
                    continue
                lab, lak = tb.next()
                for q in range(4):
                    h, c, rs, g_ = hop(q)
                    P.mm(lab[:, q * 128:(q + 1) * 128], Kt[rs, c, :], At[rs, c, :], start=True, stop=True,
                         reads=[("Kt", g_), ("At", g_)], writes=[lak])
                LakTs, Lakk = Lak_r.next()
                P.tt("dve", LakTs[:], v4(lab), mSU4[:], ALU.mult, reads=[lak, "m4"], writes=[Lakk])
                y2b, y2k = tb.next()
                for q in range(4):
                    h, c, rs, g_ = hop(q)
                    P.mm(y2b[:, q * 64:(q + 1) * 64], LakTs[:, q, :], vb[:, h * 64:(h + 1) * 64], start=True, stop=True,
                         reads=[Lakk, "vb"], writes=[y2k])
                P.copy("act", Y2s[:, 4 * hg:4 * hg + 4, :], y2b[:, 0:256].rearrange("p (q v) -> p q v", q=4), reads=[y2k], writes=[("Y2s", hg)])
                if _DBG_SUB <= 3:
                    continue
                wtb, wtk = tb.next()
                for q in range(4):
                    if _os.environ.get("NO_WT"):
                        continue
                    h, c, rs, g_ = hop(q)
                    P.mm(wtb[rs, q * 128:(q + 1) * 128], Atm[:, c, rs], TTall[:, 4 * hg + q, :], start=True, stop=True,
                         reads=[("Atm", g_), ("TTall", hg)], writes=[wtk])
                rs_ = slice(64 * par_i, 64 * par_i + 64)
                P.copy("act", WTs[rs_, 4 * pblk:4 * pblk + 4, :], wtb[rs_, :].rearrange("p (q t) -> p q t", q=4), reads=[wtk], writes=[("WTs", hg)])
                if _DBG_SUB <= 4:
                    continue
                mbb, mbk = tb.next()
                for q in range(4):
                    h, c, rs, g_ = hop(q)
                    P.mm(mbb[:, q * 128:(q + 1) * 128], Bt[rs, c, :], Rt[rs, c, :], start=True, stop=True,
                         reads=[("Bt", g_), ("Rt", g_)], writes=[mbk])
                P.tt("dve", MrbT[:, 4 * hg:4 * hg + 4, :], v4(mbb), mUI4[:], ALU.mult, reads=[mbk, "m4"], writes=[("MrbT", hg)])
                mkb, mkk = tb.next()
                for q in range(4):
                    h, c, rs, g_ = hop(q)
                    P.mm(mkb[:, q * 128:(q + 1) * 128], Kt[rs, c, :], Rt[rs, c, :], start=True, stop=True,
                         reads=[("Kt", g_), ("Rt", g_)], writes=[mkk])
                P.tt("dve", MrkT[:, 4 * hg:4 * hg + 4, :], v4(mkb), mUI4[:], ALU.mult, reads=[mkk, "m4"], writes=[("MrkT", hg)])
            if _DBG_STOP <= 2:
                continue
            def slot(h):
                i_, c_ = h % 2, h // 2
                return (2 * (c_ // 4) + i_) * 4 + (c_ % 4)
            for i in range(2):
                rs = slice(64 * i, 64 * i + 64)
                ub, ubk = UB[i]
                for c in range(8):
                    h = 2 * c + i
                    sl_ = slot(h)
                    P.mm(ub[:, c * 64:(c + 1) * 64], TTall[:, sl_, :], Y2s[:, sl_, :], start=True, stop=False,
                         reads=[("TTall", sl_ // 4), ("Y2s", sl_ // 4)], writes=[ubk])
                    P.mm(ub[:, c * 64:(c + 1) * 64], WTs[rs, c, :], Pb[rs, c * 64:(c + 1) * 64], start=False, stop=True,
                         reads=[("WTs", sl_ // 4), "Pb"], writes=[ubk])
            for i in range(2):
                ub, ubk = UB[i]
                P.copy("act", Us[:, 8 * i:8 * i + 8, :], ub[:].rearrange("p (q v) -> p q v", q=8), reads=[ubk], writes=[("Us", i)])
            for i in range(2):
                rs = slice(64 * i, 64 * i + 64)
                obk_, obkk = OB[i]
                for c in range(8):
                    h = 2 * c + i
                    sl_ = slot(h)
                    P.mm(obk_[:, c * 64:(c + 1) * 64], Rt[rs, c, :], Pb[rs, c * 64:(c + 1) * 64], start=True, stop=False,
                         reads=[("Rt", c // 4), "Pb"], writes=[obkk])
                    P.mm(obk_[:, c * 64:(c + 1) * 64], MrkT[:, sl_, :], vb[:, h * 64:(h + 1) * 64], start=False, stop=False,
                         reads=[("MrkT", sl_ // 4), "vb"], writes=[obkk])
                    P.mm(obk_[:, c * 64:(c + 1) * 64], MrbT[:, sl_, :], Us[:, 8 * i + c, :], start=False, stop=True,
                         reads=[("MrbT", sl_ // 4), ("Us", i)], writes=[obkk])
            pnb, pnk = tb.next()
            for h in range(16):
                c, i = h // 2, h % 2
                rs = slice(64 * i, 64 * i + 64)
                P.mm(pnb[rs, c * 64:(c + 1) * 64], Bhm[:, c, rs], Us[:, 8 * i + c, :], start=True, stop=False,
                     reads=[("Bhm", c // 4), ("Us", i)], writes=[pnk])
                P.mm(pnb[rs, c * 64:(c + 1) * 64], Khm[:, c, rs], vb[:, h * 64:(h + 1) * 64], start=False, stop=True,
                     reads=[("Khm", c // 4), "vb"], writes=[pnk])
            for c in range(8):
                P.stt(Pf[:, c * 64:(c + 1) * 64], Pf[:, c * 64:(c + 1) * 64], gsm[:, 8 + c:9 + c], pnb[:, c * 64:(c + 1) * 64],
                      ALU.mult, ALU.add, reads=["Pf", ("gC", c), pnk], writes=["Pf"])
            P.copy("pool", Pb[:], Pf[:], reads=["Pf"], writes=["Pb"])
            if _DBG_STOP <= 3:
                continue
            for i in range(2):
                obk_, obkk = OB[i]
                P.copy("act", Os[:].rearrange("p (c i v) -> p c i v", c=8, i=2)[:, :, i, :],
                       obk_[:].rearrange("p (c v) -> p c v", c=8), reads=[obkk], writes=[("Os", i)])
            for grp in range(2):
                cs = slice(grp * 4, grp * 4 + 4)
                otb, otk = tb.next()
                for p4 in range(4):
                    c = grp * 4 + p4
                    P.tr(otb[:, p4 * 128:(p4 + 1) * 128], Os[:, c * 128:(c + 1) * 128], ident, reads=[("Os", 0), ("Os", 1), "consts"], writes=[otk])
                P.act(ob[:, cs, :], v4(otb), AF.Identity, reads=[otk], writes=[("ob", grp)])
                P.act(osq[:, cs, :], v4(otb), AF.Square, reads=[otk], writes=[("osq", grp)])
                P.copy("act", OTf[:], v4(otb), reads=[otk], writes=["OTf"])
                vtb, vtbk = tb.next()
                for p4 in range(4):
                    c = grp * 4 + p4
                    P.tr(vtb[:, p4 * 128:(p4 + 1) * 128], vt[:, c * 128:(c + 1) * 128], ident, reads=[vtk, "consts"], writes=[vtbk])
                P.copy("act", vTs[:], v4(vtb), reads=[vtbk], writes=["vTs"])
                mnb, mnk = tb.next()
                for p4 in range(4):
                    c = grp * 4 + p4
                    P.mm(mnb[:, p4 * 128:(p4 + 1) * 128], bones64[:], ob[:, c, :], start=True, stop=True, reads=[("ob", grp), "m4"], writes=[mnk])
                P.copy("act", gmean[:], v4(mnb), reads=[mnk], writes=["gmean"])
                msb, msk = tb.next()
                for p4 in range(4):
                    c = grp * 4 + p4
                    P.mm(msb[:, p4 * 128:(p4 + 1) * 128], bones64[:], osq[:, c, :], start=True, stop=True, reads=[("osq", grp), "m4"], writes=[msk])
                P.tt("pool", gm2[:], gmean[:], gmean[:], ALU.mult, reads=["gmean"], writes=["gm2"])
                P.stt(grstd[:], v4(msb), GN_EPS, gm2[:], ALU.add, ALU.subtract, reads=[msk, "gm2"], writes=["grstd"])
                P.act(grstd[:], grstd[:], AF.Sqrt, reads=["grstd"], writes=["grstd"])
                P.recip(grstd[:], grstd[:], reads=["grstd"], writes=["grstd"])
                P.tt("pool", OTf[:], OTf[:], gmean[:], ALU.subtract, reads=["OTf", "gmean"], writes=["OTf"])
                P.tt("dve", OTf[:], OTf[:], grstd[:], ALU.mult, reads=["OTf", "grstd"], writes=["OTf"])
                for p4 in range(4):
                    c = grp * 4 + p4
                    P.act(gon[:, p4, :], OTf[:, p4, :], AF.Identity, scale=par[:, PC_GNG + c:PC_GNG + c + 1],
                          bias=par[:, PC_GNB + c:PC_GNB + c + 1], reads=["OTf", "par"], writes=[("gon", p4)])
                bnb, bnk = tb.next()
                for p4 in range(4):
                    c = grp * 4 + p4
                    P.mm(bnb[:, p4 * 128:(p4 + 1) * 128], E.bones[:], pr_b[:, c, :], start=True, stop=True, reads=[prk_, "consts2"], writes=[bnk])
                P.tt("dve", gt[:], v4(bnb), vTs[:], ALU.mult, reads=[bnk, "vTs"], writes=["gt"])
                P.tt("pool", gt[:], gt[:], gon[:], ALU.add, reads=["gt"] + [("gon", q) for q in range(4)], writes=["gt"])
                P.tt("dve", zT[:, cs, sub * 128:(sub + 1) * 128], gt[:], g_b[:, cs, :], ALU.mult, reads=["gt", gk_], writes=[("zT", grp, sub)])
            if sub == 1:
                xres, xkey = xcur

                def emit_y(m, out_ap, okey):
                    for c in range(8):
                        P.mm(out_ap, wo[:, c, m * 128:(m + 1) * 128], zT[:, c, :], start=(c == 0), stop=(c == 7),
                             reads=["wo"] + [("zT", c // 4, s_) for s_ in range(2)], writes=[okey])
                ln_tail(P, E, lnb, xres, xkey, PC_LN1G, PC_LN1B, emit_y, tile, ybanks, sbanks)
        P.flush()


def build(plan, debug_xs=False):
    nc = bass.Bass("TRN2", target_bir_lowering=False)
    E = Env()
    E.nc = nc

    def din(name, shape):
        return nc.dram_tensor(name, list(shape), F32, kind="ExternalInput").ap()
    E.d_x = din("x", [T, D])
    E.d_cst = din("cst", [128, 640])
    E.d_ropec = din("ropec", [128, T])
    E.d_ropes = din("ropes", [128, T])
    E.d_par = din("par", [4, 128, NPAR])
    E.d_rowp = din("rowp", [2, 2, D])
    E.d_lamp = din("lamp", [2, 128, 256])
    E.d_gsub = din("gsub", [2, 128, 128])
    E.d_ev_win = din("ev_win", [2, D, 4096])
    E.d_ev_wout = din("ev_wout", [2, D, D])
    for n in ("rw_wr", "rw_wk", "rw_wv", "rw_wo"):
        setattr(E, "d_" + n, din(n, [2, D, D]))
    E.d_rw_w1 = din("rw_w1", [2, D, 64])
    E.d_rw_w2 = din("rw_w2", [2, 64, D])
    E.d_rw_a1 = din("rw_a1", [2, D, 64])
    E.d_rw_a2 = din("rw_a2", [2, 64, D])
    E.d_rw_g1 = din("rw_g1", [2, D, 160])
    E.d_rw_g2 = din("rw_g2", [2, 160, D])
    E.d_rw_v1 = din("rw_v1", [1, D, 32])
    E.d_rw_v2 = din("rw_v2", [1, 32, D])
    E.d_ffn_up = din("ffn_up", [4, D, 2 * DFF])
    E.d_ffn_dn = din("ffn_dn", [4, DFF, D])
    E.d_out = nc.dram_tensor("out", [T, D], F32, kind="ExternalOutput").ap()
    E.xs = nc.dram_tensor("xs_scratch", [8, 128, T], F32).ap()
    E.vfirst = nc.dram_tensor("vfirst_scratch", [T, D], F32).ap()
    E.rwd = nc.dram_tensor("rwd_scratch", [4, 8, 128, T], F32).ap()
    E.rwdb = nc.dram_tensor("rwdb_scratch", [2, 8, 128, T], BF16).ap()
    E.rwsg = nc.dram_tensor("rwsg_scratch", [T, D], F32).ap()
    E.rwv = nc.dram_tensor("rwv_scratch", [T, D], F32).ap()
    with contextlib.ExitStack() as st:
        P = Prog(nc, st)
        E.PB = [st.enter_context(nc.psum_tensor("pb%d" % i, [128, 512], F32)) for i in range(8)]
        E.cst = st.enter_context(nc.sbuf_tensor(U("cst_sb"), [128, 640], F32))
        E.onesD = st.enter_context(nc.sbuf_tensor(U("onesD"), [128, 128], BF16))
        E.identb = st.enter_context(nc.sbuf_tensor(U("identb"), [128, 128], BF16))
        E.bones = st.enter_context(nc.sbuf_tensor(U("bones"), [128, 128], BF16))
        P.dma("sp", E.cst[:], E.d_cst[:, :], "cst", writes=["consts"])
        P.memset("pool", E.onesD[:], 1.0 / D, writes=["consts"])
        P.copy("dve", E.identb[:], E.cst[:, C_ID:C_ID + 128], reads=["consts"], writes=["consts2"])
        P.copy("dve", E.bones[:], E.cst[:, C_BO:C_BO + 128], reads=["consts"], writes=["consts2"])
        P.flush()
        for ph in plan:
            if ph[0] == "in":
                phase_in(P, E)
            elif ph[0] == "out":
                phase_out(P, E)
            elif ph[0] == "ffn":
                phase_ffn(P, E, ph[1])
            elif ph[0] == "even":
                phase_even(P, E, ph[1])
            elif ph[0] == "rwkv":
                phase_rwkv_a(P, E, ph[1])
                if len(ph) < 3:
                    phase_rwkv_b(P, E, ph[1])
        E.n_instr = P.n_instr
    return nc, E


FULL_PLAN = [("in",), ("even", 0), ("ffn", 0), ("rwkv", 1), ("ffn", 1), ("even", 2), ("ffn", 2), ("rwkv", 3), ("ffn", 3), ("out",)]


def fm(v):
    return np.ascontiguousarray(np.asarray(v, np.float32).reshape(-1, 128).T)


def host_prepare(inp):
    sh = {}
    idx = np.arange(128)
    ident = (idx[:, None] == idx[None, :]).astype(np.float32)
    su = (idx[:, None] < idx[None, :]).astype(np.float32)
    ui = (idx[:, None] <= idx[None, :]).astype(np.float32)
    sl = (idx[:, None] > idx[None, :]).astype(np.float32)
    bo = ((idx[:, None] // 64) == (idx[None, :] // 64)).astype(np.float32)
    sh["cst"] = np.ascontiguousarray(np.concatenate([ident, su, ui, sl, bo], axis=1))
    inv = (1.0 / (np.float32(10000.0) ** (np.arange(0, 64, 2, dtype=np.float32) / np.float32(64)))).astype(np.float32)
    ang = (np.arange(T, dtype=np.float32)[:, None] * inv[None, :]).astype(np.float32)
    cos = np.cos(ang).astype(np.float32).T
    sin = np.sin(ang).astype(np.float32).T
    cos64 = np.concatenate([cos, cos], 0)
    sin64 = np.concatenate([-sin, sin], 0)
    sh["ropec"] = np.ascontiguousarray(np.concatenate([cos64, cos64], 0))
    sh["ropes"] = np.ascontiguousarray(np.concatenate([sin64, sin64], 0))
    par = np.zeros((4, 128, NPAR), np.float32)
    for l in range(4):
        par[l, :, PC_LN1G:PC_LN1G + 8] = fm(inp["ln1_g"][l])
        par[l, :, PC_LN1B:PC_LN1B + 8] = fm(inp["ln1_b"][l])
        par[l, :, PC_LN2G:PC_LN2G + 8] = fm(inp["ln2_g"][l])
        par[l, :, PC_LN2B:PC_LN2B + 8] = fm(inp["ln2_b"][l])
        for k in range(3):
            par[l, :, PC_FCW + 44 * k:PC_FCW + 44 * k + 44] = fm(inp["ffn_conv_w"][l, k])
        par[l, :, PC_FCB:PC_FCB + 44] = fm(inp["ffn_conv_b"][l])
        if l % 2 == 0:
            i = l // 2
            for k in range(3):
                par[l, :, PC_ECW + 4 * k:PC_ECW + 4 * k + 4] = fm(inp["ev_conv_w"][i, k])
        else:
            j = l // 2
            for q in range(6):
                par[l, :, PC_MIX + 8 * q:PC_MIX + 8 * q + 8] = fm(inp["rw_mix"][j, q])
            par[l, :, PC_W0:PC_W0 + 8] = fm(inp["rw_w0"][j])
            par[l, :, PC_A0:PC_A0 + 8] = fm(inp["rw_a0"][j])
            par[l, :, PC_KK:PC_KK + 8] = fm(inp["rw_k_k"][j])
            par[l, :, PC_KA:PC_KA + 8] = fm(inp["rw_k_a"][j])
            par[l, :, PC_RK:PC_RK + 8] = fm(inp["rw_r_k"][j].reshape(-1))
            par[l, :, PC_GNG:PC_GNG + 8] = fm(inp["rw_gn_g"][j])
            par[l, :, PC_GNB:PC_GNB + 8] = fm(inp["rw_gn_b"][j])
    sh["par"] = par
    rowp = np.zeros((2, 2, D), np.float32)
    rowp[:, 0, :] = inp["rw_w0"]
    rowp[1, 1, :] = inp["rw_v0"][0]
    sh["rowp"] = rowp
    lamp = np.stack([np.concatenate([inp["ev_lam_q1"][i], inp["ev_lam_k1"][i], inp["ev_lam_q2"][i], inp["ev_lam_k2"][i]])
                     for i in range(2)])
    sh["lamp"] = np.ascontiguousarray(np.broadcast_to(lamp[:, None, :], (2, 128, 256))).astype(np.float32)
    sh["gsub"] = np.ascontiguousarray(np.broadcast_to(inp["ev_subln_g"][:, None, :], (2, 128, 128))).astype(np.float32)
    w = np.asarray(inp["ev_w_in"], np.float32)
    perm = np.concatenate([np.concatenate([np.arange(m * 64 + 32, m * 64 + 64), np.arange(m * 64, m * 64 + 32)]) for m in range(8)])
    sh["ev_win"] = np.ascontiguousarray(np.concatenate([w, w[:, :, 1536 + perm], w[:, :, 2048 + perm]], axis=2))
    sh["ev_wout"] = inp["ev_w_out"]
    sh["rw_wr"], sh["rw_wk"], sh["rw_wv"], sh["rw_wo"] = inp["rw_w_r"], inp["rw_w_k"], inp["rw_w_v"], inp["rw_w_o"]
    for n in ("rw_w1", "rw_w2", "rw_a1", "rw_a2", "rw_g1", "rw_g2", "rw_v1", "rw_v2"):
        sh[n] = inp[n]
    sh["ffn_up"] = inp["ffn_w_up"]
    sh["ffn_dn"] = inp["ffn_w_down"]
    return {k: np.ascontiguousarray(np.asarray(v, np.float32)) for k, v in sh.items()}


_CACHE = {}


def run_plan(inputs, plan, x_override=None, n_cores=8):
    key = tuple(plan)
    if key not in _CACHE:
        _CACHE[key] = build(plan)
    nc, E = _CACHE[key]
    shared = host_prepare(inputs)
    x = np.asarray(inputs["x"] if x_override is None else x_override, np.float32)
    in_maps = []
    for b in range(n_cores):
        m = dict(shared)
        m["x"] = np.ascontiguousarray(x[b])
        in_maps.append(m)
    res = run_bass_kernel_spmd(nc, in_maps, core_ids=list(range(n_cores)))
    return np.stack([r["out"] for r in res.results], axis=0)


def kernel(**inputs):
    return run_plan(inputs, FULL_PLAN).astype(np.float32)
```

```python
import contextlib
import math
import numpy as np
import concourse.bass as bass
import concourse.mybir as mybir
from concourse.bass_utils import run_bass_kernel_spmd

F32 = mybir.dt.float32
BF16 = mybir.dt.bfloat16
AF = mybir.ActivationFunctionType
ALU = mybir.AluOpType
AX = mybir.AxisListType

SEM_LIMIT = 30000
T = 4096
D = 1024
TT = 256
NT = T // TT
DFF = 2816
NFC = DFF // 128
ALPHA = float(8 ** 0.25)
LN_EPS = 1e-5
C0 = float(math.exp(-0.5))
NPAR = 320


import os as _os
_DBG_STOP = int(_os.environ.get("RWB_STOP", "9"))
_DBG_SUB = int(_os.environ.get("RWB_SUB", "9"))
_DBG_NMAX = int(_os.environ.get("RWB_NMAX", "64"))


class _Op:
    __slots__ = ("eng", "fn", "deps", "sig", "is_dma", "dkey", "sigval", "idx")


class Prog:
    ENGS = ("pe", "act", "dve", "pool", "sp")
    N_EPOCH = 4
    N_DMA_SEMS = 60

    def __init__(self, nc, stack):
        self.nc = nc
        self.ops = []
        self.last_w = {}
        self.readers = {}
        self.sems = {}
        for e in self.ENGS:
            for i in range(self.N_EPOCH):
                self.sems[(e, i)] = stack.enter_context(nc.semaphore("s_%s_%d" % (e, i)))
        self.dma_pool = [(stack.enter_context(nc.semaphore("s_dma_%d" % i)), 0) for i in range(self.N_DMA_SEMS)]
        self.eng_cnt = {e: 0 for e in self.ENGS}
        self.dma_cnt = {}
        self.waited = {e: {} for e in self.ENGS}
        self.emitted = 0
        self.n_instr = 0

    def add(self, eng, fn, reads=(), writes=(), dkey=None):
        op = _Op()
        op.eng = eng
        op.fn = fn
        op.is_dma = dkey is not None
        op.dkey = dkey
        op.sig = False
        op.sigval = None
        op.idx = len(self.ops)
        deps = {}

        def adddep(d):
            if d is None or d.idx < self.emitted:
                return
            if (not d.is_dma) and d.eng == "pe" and eng == "pe" and not op.is_dma:
                return
            k = ("dma", d.dkey) if d.is_dma else d.eng
            o = deps.get(k)
            if o is None or o.idx < d.idx:
                deps[k] = d
        for k in reads:
            adddep(self.last_w.get(k))
        for k in writes:
            adddep(self.last_w.get(k))
            for r in self.readers.get(k, {}).values():
                adddep(r)
        for k in writes:
            self.last_w[k] = op
            self.readers[k] = {}
        for k in reads:
            rk = ("dma", dkey) if op.is_dma else eng
            self.readers.setdefault(k, {})[rk] = op
        op.deps = list(deps.values())
        for d in op.deps:
            d.sig = True
        self.ops.append(op)
        return op

    def mm(self, out, lhsT, rhs, start=True, stop=True, reads=(), writes=()):
        return self.add("pe", lambda e: e.matmul(out, lhsT, rhs, start=start, stop=stop), reads, writes)

    def tr(self, out, in_, ident, reads=(), writes=()):
        return self.add("pe", lambda e: e.transpose(out, in_, ident), reads, writes)

    def act(self, out, in_, func, bias=None, scale=None, accum_out=None, reads=(), writes=()):
        kw = {}
        if bias is not None:
            kw["bias"] = bias
        if scale is not None:
            kw["scale"] = scale
        if accum_out is not None:
            kw["accum_out"] = accum_out
        return self.add("act", lambda e: e.activation(out, in_, func, **kw), reads, writes)

    def tt(self, eng, out, in0, in1, op, reads=(), writes=()):
        return self.add(eng, lambda e: e.tensor_tensor(out, in0, in1, op), reads, writes)

    def ts(self, eng, out, in0, s1, s2, op0, op1=None, reads=(), writes=()):
        if op1 is None:
            return self.add(eng, lambda e: e.tensor_scalar(out, in0, s1, None, op0), reads, writes)
        return self.add(eng, lambda e: e.tensor_scalar(out, in0, s1, s2, op0, op1), reads, writes)

    def stt(self, out, in0, scalar, in1, op0, op1, reads=(), writes=()):
        return self.add("dve", lambda e: e.scalar_tensor_tensor(out, in0, scalar, in1, op0, op1), reads, writes)

    def copy(self, eng, out, in_, reads=(), writes=()):
        if eng == "act":
            return self.add(eng, lambda e: e.copy(out, in_), reads, writes)
        return self.add(eng, lambda e: e.tensor_copy(out, in_), reads, writes)

    def recip(self, out, in_, reads=(), writes=()):
        return self.add("dve", lambda e: e.reciprocal(out, in_), reads, writes)

    def memset(self, eng, ap, val, writes=()):
        return self.add(eng, lambda e: e.memset(ap, val), (), writes)

    def dma(self, q, out, in_, dkey, reads=(), writes=(), **kw):
        return self.add(q, lambda e: e.dma_start(out, in_, **kw), reads, writes, dkey=dkey)

    def flush(self):
        nc = self.nc
        ops = self.ops[self.emitted:]
        self.emitted = len(self.ops)
        if not ops:
            return
        last = {}
        for op in ops:
            last[op.eng] = op
        for e, op in last.items():
            op.sig = True
        eng_cnt = self.eng_cnt
        dma_cnt = self.dma_cnt
        for op in ops:
            if op.is_dma:
                c = dma_cnt.get(op.dkey, 0) + 16
                dma_cnt[op.dkey] = c
                sn = ("dma", op.dkey)
                if sn not in self.sems:
                    self.dma_pool.sort(key=lambda t: -t[1])
                    sem_, base_ = self.dma_pool.pop()
                    self.sems[sn] = sem_
                    c = base_ + 16
                    dma_cnt[op.dkey] = c
                op.sigval = (sn, c)
            elif op.sig:
                c = eng_cnt[op.eng] + 1
                eng_cnt[op.eng] = c
                op.sigval = ((op.eng, (c - 1) // SEM_LIMIT), (c - 1) % SEM_LIMIT + 1)
        sems = self.sems
        per_eng = {e: [] for e in self.ENGS}
        for op in ops:
            per_eng[op.eng].append(op)
        finals = {}
        for e, op in last.items():
            if not op.is_dma:
                finals[op.sigval[0]] = max(finals.get(op.sigval[0], 0), op.sigval[1])
        for dk, c in dma_cnt.items():
            finals[("dma", dk)] = c
        N_EPOCH = self.N_EPOCH
        with nc.Block() as block:
            def make(ename):
                eops = per_eng[ename]
                waited = self.waited[ename]

                def do_wait(e, sn, v):
                    if waited.get(sn, 0) >= v:
                        return
                    if sn[0] != "dma":
                        if any(waited.get((sn[0], j), 0) > 0 for j in range(sn[1] + 1, N_EPOCH)):
                            return
                    e.wait_ge(sems[sn], v)
                    waited[sn] = v
                    self.n_instr += 1

                def body(e):
                    for op in eops:
                        need = {}
                        for d in op.deps:
                            sn, v = d.sigval
                            if need.get(sn, 0) < v:
                                need[sn] = v
                        for sn, v in need.items():
                            do_wait(e, sn, v)
                        ins = op.fn(e)
                        self.n_instr += 1
                        if op.is_dma:
                            ins.then_inc(sems[op.sigval[0]], 16)
                        elif op.sig:
                            ins.then_inc(sems[op.sigval[0]], 1)
                    for sn, v in finals.items():
                        do_wait(e, sn, v)
                return body

            block.tensor(make("pe"))
            block.scalar(make("act"))
            block.vector(make("dve"))
            block.gpsimd(make("pool"))
            block.sync(make("sp"))
        for op in ops:
            op.fn = None
        for dk in list(dma_cnt.keys()):
            sn = ("dma", dk)
            self.dma_pool.append((self.sems.pop(sn), dma_cnt.pop(dk)))
            for e in self.ENGS:
                self.waited[e].pop(sn, None)


_UID = [0]


def U(name):
    _UID[0] += 1
    return "sb%d_%s" % (_UID[0], name)


class Ring:
    def __init__(self, st, nc, name, shape, dtype, n):
        self.t = [st.enter_context(nc.sbuf_tensor(U("%s_%d" % (name, i)), shape, dtype)) for i in range(n)]
        self.i = 0
        self.name = name
        self.n = n

    def next(self):
        j = self.i % self.n
        self.i += 1
        return self.t[j], (self.name, j)


class Env:
    pass


C_ID, C_SU, C_UI, C_SL, C_BO = 0, 128, 256, 384, 512

PC_LN1G, PC_LN1B, PC_LN2G, PC_LN2B = 0, 8, 16, 24
PC_FCW, PC_FCB = 32, 164
PC_ECW = 208
PC_MIX, PC_W0, PC_A0, PC_KK, PC_KA, PC_RK, PC_GNG, PC_GNB = 208, 256, 264, 272, 280, 288, 296, 304


def bank_ring(E, idxs):
    r = Env()
    r.idxs = list(idxs)
    r.i = 0

    def nxt():
        j = r.idxs[r.i % len(r.idxs)]
        r.i += 1
        return E.PB[j], ("pb", j)
    r.next = nxt
    return r


def load_weight(P, E, dst, src_ap, key, nsplit=1):
    C = dst.shape[1]
    step = max(1, C // nsplit)
    for c0 in range(0, C, step):
        c1 = min(C, c0 + step)
        P.dma("pool", dst[:, c0:c1, :], src_ap[c0 * 128:c1 * 128, :].rearrange("(c p) n -> p c n", p=128),
              "w_" + key, writes=[key], max_dma_last_dim=4096)


def ln_tail(P, E, st_bufs, xres, xres_key, gcol, bcol, emit_y, tile, ybanks, sbanks):
    nc = E.nc
    s, sb, sq = st_bufs["s"], st_bufs["sb"], st_bufs["sq"]
    xo = xres
    par = E.par
    mb, mkey = sbanks.next()
    qb, qkey = sbanks.next()
    pend = []

    def stats(m, sbm, sbk, sqm, sqk):
        P.mm(mb[:, 0:TT], E.onesD[:], sbm[:], start=(m == 0), stop=(m == 7), reads=[sbk, "consts"], writes=[mkey])
        P.mm(qb[:, 0:TT], E.onesD[:], sqm[:], start=(m == 0), stop=(m == 7), reads=[sqk, "consts"], writes=[qkey])
    for m in range(8):
        bk, bkey = ybanks.next()
        emit_y(m, bk[:, 0:TT], bkey)
        P.stt(s[:, m, :], xres[:, m, :], ALPHA, bk[:, 0:TT], ALU.mult, ALU.add,
              reads=[xres_key, bkey], writes=[("ln_s", m)])
        sbm, sbk = sb.next()
        sqm, sqk = sq.next()
        P.act(sbm[:], s[:, m, :], AF.Identity, reads=[("ln_s", m)], writes=[sbk])
        P.act(sqm[:], s[:, m, :], AF.Square, reads=[("ln_s", m)], writes=[sqk])
        if pend:
            stats(*pend.pop())
        pend.append((m, sbm, sbk, sqm, sqk))
    stats(*pend.pop())
    mean, m2, rstd = st_bufs["mean"], st_bufs["m2"], st_bufs["rstd"]
    P.copy("act", mean[:], mb[:, 0:TT], reads=[mkey], writes=["ln_mean"])
    P.tt("pool", m2[:], mean[:], mean[:], ALU.mult, reads=["ln_mean"], writes=["ln_m2"])
    P.stt(rstd[:], qb[:, 0:TT], LN_EPS, m2[:], ALU.add, ALU.subtract, reads=[qkey, "ln_m2"], writes=["ln_rstd"])
    P.act(rstd[:], rstd[:], AF.Sqrt, reads=["ln_rstd"], writes=["ln_rstd"])
    P.recip(rstd[:], rstd[:], reads=["ln_rstd"], writes=["ln_rstd"])
    for m in range(8):
        P.tt("pool", s[:, m, :], s[:, m, :], mean[:], ALU.subtract, reads=[("ln_s", m), "ln_mean"], writes=[("ln_s", m)])
        P.tt("dve", s[:, m, :], s[:, m, :], rstd[:], ALU.mult, reads=[("ln_s", m), "ln_rstd"], writes=[("ln_s", m)])
        P.act(xo[:, m, :], s[:, m, :], AF.Identity, scale=par[:, gcol + m:gcol + m + 1], bias=par[:, bcol + m:bcol + m + 1],
              reads=[("ln_s", m), "par"], writes=[xres_key])
    P.dma("sp", E.xs[:, :, tile * TT:(tile + 1) * TT].rearrange("c p t -> p c t"), xo[:], "xs_st",
          reads=[xres_key], writes=[("xs", tile)])


def alloc_ln(st, nc):
    b = {}
    b["s"] = st.enter_context(nc.sbuf_tensor(U("ln_s"), [128, 8, TT], F32))
    b["sb"] = Ring(st, nc, "ln_sb", [128, TT], BF16, 3)
    b["sq"] = Ring(st, nc, "ln_sq", [128, TT], BF16, 3)
    b["mean"] = st.enter_context(nc.sbuf_tensor(U("ln_mean"), [128, TT], F32))
    b["m2"] = st.enter_context(nc.sbuf_tensor(U("ln_m2"), [128, TT], F32))
    b["rstd"] = st.enter_context(nc.sbuf_tensor(U("ln_rstd"), [128, TT], F32))
    return b


def load_par(P, E, st, l):
    nc = E.nc
    par = st.enter_context(nc.sbuf_tensor(U("par_sb"), [128, NPAR], F32))
    P.dma("sp", par[:], E.d_par[l], "par", writes=["par"])
    E.par = par


def load_xtile(P, E, xres_ring, xb_ring, tile):
    xres, xkey = xres_ring.next()
    P.dma("sp", xres[:], E.xs[:, :, tile * TT:(tile + 1) * TT].rearrange("c p t -> p c t"), "xs_ld" + str(xkey[1]),
          reads=[("xs", tile)], writes=[xkey])
    xb, xbkey = xb_ring.next()
    P.copy("pool", xb[:], xres[:], reads=[xkey], writes=[xbkey])
    return xres, xkey, xb, xbkey


def phase_in(P, E):
    nc = E.nc
    with contextlib.ExitStack() as st:
        xt_r = Ring(st, nc, "pi_xt", [128, D], F32, 2)
        xo_r = Ring(st, nc, "pi_xo", [128, 8, 128], F32, 2)
        banks = bank_ring(E, [0, 1, 2, 3])
        for blk in range(T // 128):
            xt, xk = xt_r.next()
            P.dma("sp", xt[:], E.d_x[blk * 128:(blk + 1) * 128, :], "pi_ld" + str(xk[1]), writes=[xk])
            xo, ok = xo_r.next()
            for half in range(2):
                bk, bkey = banks.next()
                for j in range(4):
                    c = half * 4 + j
                    P.tr(bk[:, j * 128:(j + 1) * 128], xt[:, c * 128:(c + 1) * 128], E.cst[:, C_ID:C_ID + 128],
                         reads=[xk, "consts"], writes=[bkey])
                eng = "act" if half == 0 else "dve"
                P.copy(eng, xo[:, half * 4:half * 4 + 4, :], bk[:].rearrange("p (j t) -> p j t", j=4),
                       reads=[bkey], writes=[(ok, half)])
            P.dma("sp", E.xs[:, :, blk * 128:(blk + 1) * 128].rearrange("c p t -> p c t"), xo[:], "pi_st" + str(ok[1]),
                  reads=[(ok, 0), (ok, 1)], writes=[("xs", blk // 2)])
        P.flush()


def phase_out(P, E):
    nc = E.nc
    with contextlib.ExitStack() as st:
        xi_r = Ring(st, nc, "po_xi", [128, 8, 128], F32, 2)
        xo_r = Ring(st, nc, "po_xo", [128, D], F32, 2)
        banks = bank_ring(E, [0, 1, 2, 3])
        for blk in range(T // 128):
            xi, ik = xi_r.next()
            P.dma("sp", xi[:], E.xs[:, :, blk * 128:(blk + 1) * 128].rearrange("c p t -> p c t"), "po_ld" + str(ik[1]),
                  reads=[("xs", blk // 2)], writes=[ik])
            xo, ok = xo_r.next()
            for half in range(2):
                bk, bkey = banks.next()
                for j in range(4):
                    c = half * 4 + j
                    P.tr(bk[:, j * 128:(j + 1) * 128], xi[:, c, :], E.cst[:, C_ID:C_ID + 128],
                         reads=[ik, "consts"], writes=[bkey])
                eng = "act" if half == 0 else "dve"
                P.copy(eng, xo[:, half * 512:(half + 1) * 512], bk[:], reads=[bkey], writes=[(ok, half)])
            P.dma("sp", E.d_out[blk * 128:(blk + 1) * 128, :], xo[:], "po_st" + str(ok[1]),
                  reads=[(ok, 0), (ok, 1)], writes=[("out", blk)])
        P.flush()


def phase_ffn(P, E, l):
    nc = E.nc
    with contextlib.ExitStack() as st:
        load_par(P, E, st, l)
        par = E.par
        wup = st.enter_context(nc.sbuf_tensor(U("wup"), [128, 8, 2 * DFF], BF16))
        wdn = st.enter_context(nc.sbuf_tensor(U("wdn"), [128, NFC, D], BF16))
        GW = 11 * 128
        for gq in (0, 2, 1, 3):
            for c in range(8):
                P.dma("pool", wup[:, c, gq * GW:(gq + 1) * GW], E.d_ffn_up[l, c * 128:(c + 1) * 128, gq * GW:(gq + 1) * GW],
                      "w_up%d" % gq, writes=[("wup", gq)], max_dma_last_dim=4096)
        load_weight(P, E, wdn, E.d_ffn_dn[l], "wdn", nsplit=2)
        lnb = alloc_ln(st, nc)
        xres_r = Ring(st, nc, "xres", [128, 8, TT], F32, 2)
        xb_r = Ring(st, nc, "xb", [128, 8, TT], BF16, 1)
        gT = st.enter_context(nc.sbuf_tensor(U("gT"), [128, NFC, TT], BF16))
        carry = st.enter_context(nc.sbuf_tensor(U("carry"), [128, 2 * NFC, 2], F32))
        hs_r = Ring(st, nc, "hs", [128, TT + 2], F32, 4)
        cv_r = Ring(st, nc, "cv", [128, TT], F32, 4)
        sg_r = Ring(st, nc, "sg", [128, TT], F32, 2)
        P.memset("pool", carry[:], 0.0, writes=[("carry", ch) for ch in range(2 * NFC)])
        hbanks = bank_ring(E, [0, 1, 2, 3])
        ybanks = bank_ring(E, [4, 5])
        sbanks = bank_ring(E, [6, 7])
        nxt = load_xtile(P, E, xres_r, xb_r, 0)
        for tile in range(NT):
            xres, xkey, xb, xbkey = nxt
            cvs = {}
            for j in range(NFC):
                for which in range(2):
                    ch = j + which * NFC
                    bk, bkey = hbanks.next()
                    for c in range(8):
                        P.mm(bk[:, 0:TT], wup[:, c, ch * 128:(ch + 1) * 128], xb[:, c, :], start=(c == 0), stop=(c == 7),
                             reads=[("wup", ch // 11), xbkey], writes=[bkey])
                    hs, hkey = hs_r.next()
                    P.copy("pool", hs[:, 0:2], carry[:, ch, :], reads=[("carry", ch)], writes=[(hkey, "c")])
                    P.copy("act", hs[:, 2:TT + 2], bk[:, 0:TT], reads=[bkey], writes=[(hkey, "m")])
                    P.copy("pool", carry[:, ch, :], hs[:, TT:TT + 2], reads=[(hkey, "m")], writes=[("carry", ch)])
                    cv, ckey = cv_r.next()
                    w0c = par[:, PC_FCW + ch:PC_FCW + ch + 1]
                    w1c = par[:, PC_FCW + 44 + ch:PC_FCW + 44 + ch + 1]
                    w2c = par[:, PC_FCW + 88 + ch:PC_FCW + 88 + ch + 1]
                    bc = par[:, PC_FCB + ch:PC_FCB + ch + 1]
                    P.act(cv[:], bk[:, 0:TT], AF.Identity, scale=w2c, bias=bc, reads=[bkey, "par"], writes=[ckey])
                    P.stt(cv[:], hs[:, 1:TT + 1], w1c, cv[:], ALU.mult, ALU.add,
                          reads=[(hkey, "m"), (hkey, "c"), ckey, "par"], writes=[ckey])
                    P.stt(cv[:], hs[:, 0:TT], w0c, cv[:], ALU.mult, ALU.add,
                          reads=[(hkey, "m"), (hkey, "c"), ckey, "par"], writes=[ckey])
                    cvs[which] = (cv, ckey)
                sg, skey = sg_r.next()
                P.act(sg[:], cvs[0][0][:], AF.Silu, reads=[cvs[0][1]], writes=[skey])
                P.tt("dve", gT[:, j, :], sg[:], cvs[1][0][:], ALU.mult, reads=[skey, cvs[1][1]], writes=[("gT", j)])
            if tile + 1 < NT:
                nxt = load_xtile(P, E, xres_r, xb_r, tile + 1)

            def emit_y(m, out_ap, okey):
                for j in range(NFC):
                    P.mm(out_ap, wdn[:, j, m * 128:(m + 1) * 128], gT[:, j, :], start=(j == 0), stop=(j == NFC - 1),
                         reads=["wdn", ("gT", j)], writes=[okey])
            ln_tail(P, E, lnb, xres, xkey, PC_LN2G, PC_LN2B, emit_y, tile, ybanks, sbanks)
        P.flush()


def phase_even(P, E, l):
    nc = E.nc
    i = l // 2
    lam_init = 0.8 - 0.6 * math.exp(-0.3 * l)
    with contextlib.ExitStack() as st:
        load_par(P, E, st, l)
        par = E.par
        win = st.enter_context(nc.sbuf_tensor(U("win"), [128, 8, 4096], BF16))
        wout = st.enter_context(nc.sbuf_tensor(U("wout"), [128, 8, D], BF16))
        for c in range(8):
            P.dma("pool", win[:, c, :], E.d_ev_win[i, c * 128:(c + 1) * 128, :], "w_in", writes=["win"],
                  max_dma_last_dim=4096)
        load_weight(P, E, wout, E.d_ev_wout[i], "wout", nsplit=2)
        KT = st.enter_context(nc.sbuf_tensor(U("KT"), [128, 4, T], BF16))
        VP = st.enter_context(nc.sbuf_tensor(U("VP"), [128, T // 128, 4, 129], BF16))
        P.memset("pool", VP[:], 1.0, writes=[("VP", kb) for kb in range(T // 128)])
        lamp = st.enter_context(nc.sbuf_tensor(U("lamp"), [128, 256], F32))
        gsub = st.enter_context(nc.sbuf_tensor(U("gsub_sb"), [128, 128], F32))
        lsm = st.enter_context(nc.sbuf_tensor(U("lsm"), [128, 8], F32))
        lpr = st.enter_context(nc.sbuf_tensor(U("lpr"), [128, 2, 64], F32))
        P.dma("sp", lamp[:], E.d_lamp[i], "lamp", writes=["lamp"])
        P.dma("sp", gsub[:], E.d_gsub[i], "gsubd", writes=["gsub"])
        P.ts("dve", gsub[:], gsub[:], float(1.0 - lam_init), None, ALU.mult, reads=["gsub"], writes=["gsub"])
        P.tt("dve", lpr[:, 0, :], lamp[:, 0:64], lamp[:, 64:128], ALU.mult, reads=["lamp"], writes=["lpr"])
        P.tt("dve", lpr[:, 1, :], lamp[:, 128:192], lamp[:, 192:256], ALU.mult, reads=["lamp"], writes=["lpr"])
        P.add("dve", lambda e: e.reduce_sum(lsm[:, 0:1], lpr[:, 0, :], AX.X), ["lpr"], ["lsm"])
        P.add("dve", lambda e: e.reduce_sum(lsm[:, 1:2], lpr[:, 1, :], AX.X), ["lpr"], ["lsm"])
        P.act(lsm[:, 2:4], lsm[:, 0:2], AF.Exp, reads=["lsm"], writes=["lsm"])
        P.tt("dve", lsm[:, 4:5], lsm[:, 3:4], lsm[:, 2:3], ALU.subtract, reads=["lsm"], writes=["lsm"])
        P.ts("dve", lsm[:, 5:6], lsm[:, 4:5], float(-lam_init), None, ALU.add, reads=["lsm"], writes=["neglam"])
        neglam = lsm[:, 5:6]
        lnb = alloc_ln(st, nc)
        xres_r = Ring(st, nc, "xres", [128, 8, TT], F32, 2)
        xb_r = Ring(st, nc, "xb", [128, 8, TT], BF16, 1)
        catT = st.enter_context(nc.sbuf_tensor(U("catT"), [128, 8, TT], BF16))
        QT = st.enter_context(nc.sbuf_tensor(U("QT"), [128, 4, TT], BF16))
        carry = st.enter_context(nc.sbuf_tensor(U("carry_u"), [128, 4, 2], F32))
        P.memset("pool", carry[:], 0.0, writes=[("carry", c) for c in range(4)])
        rc_r = Ring(st, nc, "rc", [128, TT], F32, 1)
        rs_r = Ring(st, nc, "rs", [128, TT], F32, 1)
        gcs_r = Ring(st, nc, "gcs", [128, TT], F32, 1)
        u_r = Ring(st, nc, "u", [128, TT + 2], F32, 2)
        cv_r = Ring(st, nc, "cv", [128, TT], F32, 1)
        t1_r = Ring(st, nc, "t1", [128, TT], F32, 1)
        t2_r = Ring(st, nc, "t2", [128, TT], F32, 1)
        pT_r = Ring(st, nc, "pT", [128, 4, 128], BF16, 3)
        sm_r = Ring(st, nc, "sm", [128, 8], F32, 2)
        tn_r = Ring(st, nc, "tn", [128, 128], F32, 2)
        o_r = Ring(st, nc, "o", [128, 128], F32, 2)
        on_r = Ring(st, nc, "on", [128, 128], F32, 2)
        junk = st.enter_context(nc.sbuf_tensor(U("junk"), [128, 128], BF16))
        ibanks = bank_ring(E, [0, 1, 2, 3])
        abanks = bank_ring(E, [4, 5, 6, 7])
        ybanks = bank_ring(E, [4, 5])
        sbanks = bank_ring(E, [6, 7])
        maskUI = E.cst[:, C_UI:C_UI + 128]
        ident = E.cst[:, C_ID:C_ID + 128]

        def proj(bk, bkey, col0, xb, xbkey):
            for c in range(8):
                P.mm(bk[:, 0:TT], win[:, c, col0:col0 + 128], xb[:, c, :], start=(c == 0), stop=(c == 7),
                     reads=["win", xbkey], writes=[bkey])

        nxt = load_xtile(P, E, xres_r, xb_r, 0)
        for tile in range(NT):
            xres, xkey, xb, xbkey = nxt
            tsl = slice(tile * TT, (tile + 1) * TT)
            rc, rckey = rc_r.next()
            rs, rskey = rs_r.next()
            P.dma("sp", rc[:], E.d_ropec[:, tsl], "rc" + str(rckey[1]), writes=[rckey])
            P.dma("sp", rs[:], E.d_ropes[:, tsl], "rs" + str(rskey[1]), writes=[rskey])
            for cc in range(4):
                b_gc, k_gc = ibanks.next()
                proj(b_gc, k_gc, 512 + cc * 128, xb, xbkey)
                b_xi, k_xi = ibanks.next()
                proj(b_xi, k_xi, 1024 + cc * 128, xb, xbkey)
                b_gb, k_gb = ibanks.next()
                proj(b_gb, k_gb, cc * 128, xb, xbkey)
                gcs, gkey = gcs_r.next()
                P.copy("act", gcs[:], b_gc[:, 0:TT], reads=[k_gc], writes=[gkey])
                u, ukey = u_r.next()
                P.copy("pool", u[:, 0:2], carry[:, cc, :], reads=[("carry", cc)], writes=[(ukey, "c")])
                P.tt("dve", u[:, 2:TT + 2], b_xi[:, 0:TT], gcs[:], ALU.mult, reads=[k_xi, gkey], writes=[(ukey, "m")])
                P.copy("pool", carry[:, cc, :], u[:, TT:TT + 2], reads=[(ukey, "m")], writes=[("carry", cc)])
                cv, ckey = cv_r.next()
                w0c = par[:, PC_ECW + cc:PC_ECW + cc + 1]
                w1c = par[:, PC_ECW + 4 + cc:PC_ECW + 4 + cc + 1]
                w2c = par[:, PC_ECW + 8 + cc:PC_ECW + 8 + cc + 1]
                P.ts("dve", cv[:], u[:, 2:TT + 2], w2c, None, ALU.mult, reads=[(ukey, "m"), "par"], writes=[ckey])
                P.stt(cv[:], u[:, 1:TT + 1], w1c, cv[:], ALU.mult, ALU.add, reads=[(ukey, "m"), (ukey, "c"), ckey], writes=[ckey])
                P.stt(cv[:], u[:, 0:TT], w0c, cv[:], ALU.mult, ALU.add, reads=[(ukey, "m"), (ukey, "c"), ckey], writes=[ckey])
                P.tt("dve", catT[:, cc, :], b_gb[:, 0:TT], cv[:], ALU.mult, reads=[k_gb, ckey], writes=[("catT", cc)])
            for cc in range(4):
                for (col0, colsw, dst, dkeyw) in ((1536, 3072, QT[:, cc, :], ("QT", cc)),
                                                  (2048, 3584, KT[:, cc, tsl], ("KT", tile, cc))):
                    b1, k1 = ibanks.next()
                    proj(b1, k1, col0 + cc * 128, xb, xbkey)
                    b2, k2 = ibanks.next()
                    proj(b2, k2, colsw + cc * 128, xb, xbkey)
                    t1, t1k = t1_r.next()
                    t2, t2k = t2_r.next()
                    P.tt("dve", t1[:], b1[:, 0:TT], rc[:], ALU.mult, reads=[k1, rckey], writes=[t1k])
                    P.tt("dve", t2[:], b2[:, 0:TT], rs[:], ALU.mult, reads=[k2, rskey], writes=[t2k])
                    P.tt("pool", dst, t1[:], t2[:], ALU.add, reads=[t1k, t2k], writes=[dkeyw])
            for sub in range(TT // 128):
                kb = tile * (TT // 128) + sub
                bv, kv = ibanks.next()
                for c in range(8):
                    P.mm(bv[:], xb[:, c, sub * 128:(sub + 1) * 128], win[:, c, 2560:3072], start=(c == 0), stop=(c == 7),
                         reads=["win", xbkey], writes=[kv])
                P.copy("act", VP[:, kb, :, 0:128], bv[:].rearrange("p (h d) -> p h d", h=4), reads=[kv], writes=[("VP", kb)])
            pend = []

            def drain():
                while pend:
                    pend.pop(0)()
            for sub in range(TT // 128):
                qb = tile * (TT // 128) + sub
                qsl = slice(sub * 128, (sub + 1) * 128)
                for h in range(4):
                    accs = [abanks.next(), abanks.next()]
                    kbs = list(range(qb + 1))
                    for g0 in range(0, qb + 1, 4):
                        grp = kbs[g0:g0 + 4]
                        ng = len(grp)
                        for m in range(2):
                            sbk, skey = ibanks.next()
                            for g, kb in enumerate(grp):
                                P.mm(sbk[:, g * 128:(g + 1) * 128], KT[64 * m:64 * m + 64, h, kb * 128:(kb + 1) * 128],
                                     QT[64 * m:64 * m + 64, h, qsl], start=True, stop=True,
                                     reads=[("KT", kb // (TT // 128), h), ("QT", h)], writes=[skey])
                            pT, pkey = pT_r.next()
                            P.act(pT[:, 0:ng, :], sbk[:, 0:ng * 128].rearrange("p (g q) -> p g q", g=ng), AF.Exp, scale=0.125,
                                  reads=[skey], writes=[pkey])
                            if qb in grp:
                                gd = grp.index(qb)
                                P.tt("pool", pT[:, gd, :], pT[:, gd, :], maskUI, ALU.mult, reads=[pkey, "consts"], writes=[pkey])
                            drain()

                            def pv(grp=grp, pT=pT, pkey=pkey, acc=accs[m][0], akey=accs[m][1], h=h, qb=qb):
                                for g, kb in enumerate(grp):
                                    P.mm(acc[:, 0:129], pT[:, g, :], VP[:, kb, h, :], start=(kb == 0), stop=(kb == qb),
                                         reads=[pkey, ("VP", kb)], writes=[akey])
                            pend.append(pv)

                    def norm(accs=accs, h=h, qsl=qsl, sub=sub):
                        (acc0, ak0), (acc1, ak1) = accs
                        sm, smk = sm_r.next()
                        P.recip(sm[:, 0:1], acc0[:, 128:129], reads=[ak0], writes=[(smk, 0)])
                        P.recip(sm[:, 1:2], acc1[:, 128:129], reads=[ak1], writes=[(smk, 1)])
                        P.tt("dve", sm[:, 2:3], sm[:, 1:2], neglam, ALU.mult, reads=[(smk, 1), "neglam"], writes=[(smk, 2)])
                        tn, tnk = tn_r.next()
                        P.ts("dve", tn[:], acc1[:, 0:128], sm[:, 2:3], None, ALU.mult, reads=[ak1, (smk, 2)], writes=[tnk])
                        o, ok = o_r.next()
                        P.stt(o[:], acc0[:, 0:128], sm[:, 0:1], tn[:], ALU.mult, ALU.add, reads=[ak0, (smk, 0), tnk], writes=[ok])
                        P.act(junk[:], o[:], AF.Square, accum_out=sm[:, 3:4], reads=[ok], writes=["junk", (smk, 3)])
                        P.ts("dve", sm[:, 4:5], sm[:, 3:4], 1.0 / 128.0, 1e-5, ALU.mult, ALU.add, reads=[(smk, 3)], writes=[(smk, 4)])
                        P.act(sm[:, 4:5], sm[:, 4:5], AF.Sqrt, reads=[(smk, 4)], writes=[(smk, 4)])
                        P.recip(sm[:, 5:6], sm[:, 4:5], reads=[(smk, 4)], writes=[(smk, 5)])
                        on, onk = on_r.next()
                        P.stt(on[:], o[:], sm[:, 5:6], gsub[:], ALU.mult, ALU.mult, reads=[ok, (smk, 5), "gsub"], writes=[onk])
                        trb, trk = ibanks.next()
                        P.tr(trb[:, 0:128], on[:], ident, reads=[onk, "consts"], writes=[trk])
                        P.copy("act", catT[:, 4 + h, qsl], trb[:, 0:128], reads=[trk], writes=[("catT", 4 + h, sub)])
                    pend.append(norm)
            drain()
            if tile + 1 < NT:
                nxt = load_xtile(P, E, xres_r, xb_r, tile + 1)

            def emit_y(m, out_ap, okey):
                for c in range(8):
                    rd = [("catT", c)] if c < 4 else [("catT", c, sb_) for sb_ in range(TT // 128)]
                    P.mm(out_ap, wout[:, c, m * 128:(m + 1) * 128], catT[:, c, :], start=(c == 0), stop=(c == 7),
                         reads=["wout"] + rd, writes=[okey])
            ln_tail(P, E, lnb, xres, xkey, PC_LN1G, PC_LN1B, emit_y, tile, ybanks, sbanks)
        P.flush()


def phase_rwkv_a(P, E, l):
    nc = E.nc
    j = l // 2
    has_vres = j > 0
    NS = TT // 128
    with contextlib.ExitStack() as st:
        load_par(P, E, st, l)
        par = E.par
        wr = st.enter_context(nc.sbuf_tensor(U("wr"), [128, 8, D], BF16))
        wk = st.enter_context(nc.sbuf_tensor(U("wk"), [128, 8, D], BF16))
        wv = st.enter_context(nc.sbuf_tensor(U("wv"), [128, 8, D], BF16))
        load_weight(P, E, wr, E.d_rw_wr[j], "wr", nsplit=2)
        wl = st.enter_context(nc.sbuf_tensor(U("wl"), [128, 8, 320], BF16))
        P.dma("pool", wl[:, :, 0:64], E.d_rw_w1[j].rearrange("(c p) n -> p c n", p=128), "w_l", writes=["wl"])
        P.dma("pool", wl[:, :, 64:128], E.d_rw_a1[j].rearrange("(c p) n -> p c n", p=128), "w_l", writes=["wl"])
        P.dma("pool", wl[:, :, 128:288], E.d_rw_g1[j].rearrange("(c p) n -> p c n", p=128), "w_l", writes=["wl"])
        if has_vres:
            P.dma("pool", wl[:, :, 288:320], E.d_rw_v1[j - 1].rearrange("(c p) n -> p c n", p=128), "w_l", writes=["wl"])
        w2s = st.enter_context(nc.sbuf_tensor(U("w2s"), [64, D], BF16))
        a2s = st.enter_context(nc.sbuf_tensor(U("a2s"), [64, D], BF16))
        g2s = st.enter_context(nc.sbuf_tensor(U("g2s"), [128, 2, D], BF16))
        v2s = st.enter_context(nc.sbuf_tensor(U("v2s"), [32, D], BF16))
        P.dma("pool", w2s[:], E.d_rw_w2[j], "w_l", writes=["wl"], max_dma_last_dim=4096)
        P.dma("pool", a2s[:], E.d_rw_a2[j], "w_l", writes=["wl"], max_dma_last_dim=4096)
        P.dma("pool", g2s[:, 0, :], E.d_rw_g2[j, 0:128, :], "w_l", writes=["wl"], max_dma_last_dim=4096)
        P.dma("pool", g2s[0:32, 1, :], E.d_rw_g2[j, 128:160, :], "w_l", writes=["wl"], max_dma_last_dim=4096)
        if has_vres:
            P.dma("pool", v2s[:], E.d_rw_v2[j - 1], "w_l", writes=["wl"], max_dma_last_dim=4096)
        load_weight(P, E, wk, E.d_rw_wk[j], "wk", nsplit=2)
        load_weight(P, E, wv, E.d_rw_wv[j], "wv", nsplit=2)
        rowf = st.enter_context(nc.sbuf_tensor(U("rowf"), [1, 2, D], F32))
        P.dma("sp", rowf[:], E.d_rowp[j:j + 1, :, :], "rowf", writes=["rowf"])
        ones1 = st.enter_context(nc.sbuf_tensor(U("ones1"), [1, 128], F32))
        P.memset("pool", ones1[:], 1.0, writes=["ones1"])
        omka = st.enter_context(nc.sbuf_tensor(U("omka"), [128, 8], F32))
        P.ts("dve", omka[:], par[:, PC_KA:PC_KA + 8], -1.0, 1.0, ALU.mult, ALU.add, reads=["par"], writes=["omka"])
        xres_r = Ring(st, nc, "xres", [128, 8, TT], F32, 2)
        xx_r = Ring(st, nc, "xx", [128, 8, TT], F32, 1)
        xprev = st.enter_context(nc.sbuf_tensor(U("xprev"), [128, 8, 1], F32))
        P.memset("pool", xprev[:], 0.0, writes=["xprev"])
        mx_r = Ring(st, nc, "mx", [128, 8, TT], BF16, 2)
        rbuf_r = Ring(st, nc, "rbuf", [128, 8, TT], F32, 1)
        kbuf_r = Ring(st, nc, "kbuf", [128, 8, TT], F32, 1)
        abuf_r = Ring(st, nc, "abuf", [128, 8, TT], F32, 1)
        gbuf = st.enter_context(nc.sbuf_tensor(U("gbuf"), [128, 8, TT], BF16))
        kkbuf = st.enter_context(nc.sbuf_tensor(U("kkbuf"), [128, 8, TT], F32))
        k2buf = st.enter_context(nc.sbuf_tensor(U("k2buf"), [128, 8, TT], F32))
        prbuf = st.enter_context(nc.sbuf_tensor(U("prbuf"), [128, 8, TT], BF16))
        h1 = st.enter_context(nc.sbuf_tensor(U("h1"), [64, TT], BF16))
        ha = st.enter_context(nc.sbuf_tensor(U("ha"), [64, TT], BF16))
        hv = st.enter_context(nc.sbuf_tensor(U("hv"), [32, TT], BF16))
        hgb = st.enter_context(nc.sbuf_tensor(U("hgb"), [128, 2, TT], BF16))
        sqb_r = Ring(st, nc, "sqb", [128, TT], BF16, 2)
        sd_r = Ring(st, nc, "sd", [128, TT], F32, 2)
        f_r = Ring(st, nc, "f", [128, TT], F32, 2)
        sgt_r = Ring(st, nc, "sgt", [128, D], F32, 1)
        vt_r = Ring(st, nc, "vt", [128, D], F32, 1)
        if has_vres:
            sgv = st.enter_context(nc.sbuf_tensor(U("sgv"), [128, D], F32))
            vf = st.enter_context(nc.sbuf_tensor(U("vf"), [128, D], F32))
            dd = st.enter_context(nc.sbuf_tensor(U("dd"), [128, D], F32))
        banks = bank_ring(E, [0, 1, 2, 3, 4, 5, 6, 7])

        cur_xx = [None, None]

        def mixed(q, xres, xkey, xx=None, xxk=None):
            xx, xxk = cur_xx
            mx, mk = mx_r.next()
            for c in range(8):
                P.stt(mx[:, c, :], xx[:, c, :], par[:, PC_MIX + q * 8 + c:PC_MIX + q * 8 + c + 1], xres[:, c, :],
                      ALU.mult, ALU.add, reads=[xxk, xkey, "par"], writes=[(mk, c)])
            return mx, [(mk, c) for c in range(8)]

        def proj_fm(w, m, mx, mkeys, bk, bkey, M=128, col0=None):
            cs = slice(m * 128, (m + 1) * 128) if col0 is None else slice(col0, col0 + M)
            wkey = {id(wr): "wr", id(wk): "wk", id(wv): "wv", id(wl): "wl"}[id(w)]
            for c in range(8):
                P.mm(bk[0:M, 0:TT], w[:, c, cs], mx[:, c, :], start=(c == 0), stop=(c == 7),
                     reads=[wkey, mkeys[c]], writes=[bkey])

        for tile in range(int(_os.environ.get("RW_NT", NT))):
            xres, xkey = xres_r.next()
            xx, xxk = xx_r.next()
            rbuf, rbk = rbuf_r.next()
            kbuf, kbk = kbuf_r.next()
            abuf, abk = abuf_r.next()
            tsl = slice(tile * TT, (tile + 1) * TT)
            P.dma("sp", xres[:], E.xs[:, :, tsl].rearrange("c p t -> p c t"), "xs_ld" + str(xkey[1]),
                  reads=[("xs", tile)], writes=[xkey])
            cur_xx[:] = [xx, xxk]
            P.tt("pool", xx[:, :, 1:TT], xres[:, :, 0:TT - 1], xres[:, :, 1:TT], ALU.subtract, reads=[xkey], writes=[xxk])
            P.tt("pool", xx[:, :, 0:1], xprev[:], xres[:, :, 0:1], ALU.subtract, reads=[xkey, "xprev"], writes=[xxk])
            P.copy("pool", xprev[:], xres[:, :, TT - 1:TT], reads=[xkey], writes=["xprev"])
            mx, mkeys = mixed(0, xres, xkey)
            for m in range(8):
                bk, bkey = banks.next()
                proj_fm(wr, m, mx, mkeys, bk, bkey)
                P.copy("act", rbuf[:, m, :], bk[:, 0:TT], reads=[bkey], writes=[(rbk, m)])
            mx, mkeys = mixed(1, xres, xkey)
            bk, bkey = banks.next()
            proj_fm(wl, 0, mx, mkeys, bk, bkey, M=64, col0=0)
            P.act(h1[:], bk[0:64, 0:TT], AF.Tanh, reads=[bkey], writes=["h1"])
            for sub in range(NS):
                sgt, sgk = sgt_r.next()
                for half in range(2):
                    bk, bkey = banks.next()
                    hs_ = slice(half * 512, (half + 1) * 512)
                    P.mm(bk[:], h1[0:64, sub * 128:(sub + 1) * 128], w2s[0:64, hs_], start=True, stop=False,
                         reads=["h1", "wl"], writes=[bkey])
                    P.mm(bk[:], ones1[0:1, :], rowf[0:1, 0, hs_], start=False, stop=True, reads=["ones1", "rowf"], writes=[bkey])
                    P.act(sgt[:, hs_], bk[:], AF.Sigmoid, reads=[bkey], writes=[(sgk, half)])
                r0 = tile * TT + sub * 128
                P.dma("sp", E.rwsg[r0:r0 + 128, :], sgt[:], "st_sg", reads=[(sgk, 0), (sgk, 1)], writes=[("rwsg", r0)])
            mx, mkeys = mixed(2, xres, xkey)
            for m in range(8):
                bk, bkey = banks.next()
                proj_fm(wk, m, mx, mkeys, bk, bkey)
                P.copy("act", kbuf[:, m, :], bk[:, 0:TT], reads=[bkey], writes=[(kbk, m)])
            mx, mkeys = mixed(3, xres, xkey)
            if has_vres:
                bk, bkey = banks.next()
                proj_fm(wl, 0, mx, mkeys, bk, bkey, M=32, col0=288)
                P.copy("act", hv[:], bk[0:32, 0:TT], reads=[bkey], writes=["hv"])
            for sub in range(NS):
                r0 = tile * TT + sub * 128
                vt, vk = vt_r.next()
                vbk = []
                for half in range(2):
                    bk, bkey = banks.next()
                    hs_ = slice(half * 512, (half + 1) * 512)
                    for c in range(8):
                        P.mm(bk[:], mx[:, c, sub * 128:(sub + 1) * 128], wv[:, c, hs_], start=(c == 0), stop=(c == 7),
                             reads=["wv", mkeys[c]], writes=[bkey])
                    vbk.append((bk, bkey))
                if has_vres:
                    P.dma("sp", vf[:], E.vfirst[r0:r0 + 128, :], "ld_vf", reads=[("vfirst", r0)], writes=["vf"])
                    for half in range(2):
                        hs_ = slice(half * 512, (half + 1) * 512)
                        bk, bkey = banks.next()
                        P.mm(bk[:], hv[0:32, sub * 128:(sub + 1) * 128], v2s[0:32, hs_], start=True, stop=False,
                             reads=["hv", "wl"], writes=[bkey])
                        P.mm(bk[:], ones1[0:1, :], rowf[0:1, 1, hs_], start=False, stop=True, reads=["ones1", "rowf"], writes=[bkey])
                        P.act(sgv[:, hs_], bk[:], AF.Sigmoid, reads=[bkey], writes=[("sgv", half)])
                        vb_, vbk_ = vbk[half]
                        P.tt("dve", dd[:, hs_], vf[:, hs_], vb_[:], ALU.subtract, reads=["vf", vbk_], writes=[("dd", half)])
                        P.tt("pool", dd[:, hs_], dd[:, hs_], sgv[:, hs_], ALU.mult, reads=[("dd", half), ("sgv", half)], writes=[("dd", half)])
                        P.tt("dve", vt[:, hs_], dd[:, hs_], vb_[:], ALU.add, reads=[("dd", half), vbk_], writes=[(vk, half)])
                else:
                    for half in range(2):
                        hs_ = slice(half * 512, (half + 1) * 512)
                        vb_, vbk_ = vbk[half]
                        P.copy("act", vt[:, hs_], vb_[:], reads=[vbk_], writes=[(vk, half)])
                    P.dma("sp", E.vfirst[r0:r0 + 128, :], vt[:], "st_vf", reads=[(vk, 0), (vk, 1)], writes=[("vfirst", r0)])
                P.dma("sp", E.rwv[r0:r0 + 128, :], vt[:], "st_v", reads=[(vk, 0), (vk, 1)], writes=[("rwv", r0)])
            mx, mkeys = mixed(4, xres, xkey)
            bk, bkey = banks.next()
            proj_fm(wl, 0, mx, mkeys, bk, bkey, M=64, col0=64)
            P.copy("act", ha[:], bk[0:64, 0:TT], reads=[bkey], writes=["ha"])
            for m in range(8):
                bk, bkey = banks.next()
                P.mm(bk[:, 0:TT], a2s[0:64, m * 128:(m + 1) * 128], ha[0:64, :], start=True, stop=True,
                     reads=["ha", "wl"], writes=[bkey])
                P.act(abuf[:, m, :], bk[:, 0:TT], AF.Sigmoid, bias=par[:, PC_A0 + m:PC_A0 + m + 1],
                      reads=[bkey, "par"], writes=[(abk, m)])
            mx, mkeys = mixed(5, xres, xkey)
            bk, bkey = banks.next()
            proj_fm(wl, 0, mx, mkeys, bk, bkey, M=128, col0=128)
            P.act(hgb[:, 0, :], bk[:, 0:TT], AF.Sigmoid, reads=[bkey], writes=[("hgb", 0)])
            bk, bkey = banks.next()
            proj_fm(wl, 0, mx, mkeys, bk, bkey, M=32, col0=256)
            P.act(hgb[0:32, 1, :], bk[0:32, 0:TT], AF.Sigmoid, reads=[bkey], writes=[("hgb", 1)])
            for m in range(8):
                bk, bkey = banks.next()
                P.mm(bk[:, 0:TT], g2s[:, 0, m * 128:(m + 1) * 128], hgb[:, 0, :], start=True, stop=False,
                     reads=[("hgb", 0), "wl"], writes=[bkey])
                P.mm(bk[:, 0:TT], g2s[0:32, 1, m * 128:(m + 1) * 128], hgb[0:32, 1, :], start=False, stop=True,
                     reads=[("hgb", 1), "wl"], writes=[bkey])
                P.copy("act", gbuf[:, m, :], bk[:, 0:TT], reads=[bkey], writes=[("gbuf", m)])
            for m in range(8):
                pc = lambda base: par[:, base + m:base + m + 1]
                P.ts("dve", kkbuf[:, m, :], kbuf[:, m, :], pc(PC_KK), None, ALU.mult, reads=[(kbk, m), "par"], writes=[("kk", m)])
                sqb, sqk = sqb_r.next()
                P.act(sqb[:], kkbuf[:, m, :], AF.Square, reads=[("kk", m)], writes=[sqk])
                bk, bkey = banks.next()
                P.mm(bk[:, 0:TT], E.bones[:], sqb[:], start=True, stop=True, reads=[sqk, "consts2"], writes=[bkey])
                sd, sdk = sd_r.next()
                P.act(sd[:], bk[:, 0:TT], AF.Sqrt, reads=[bkey], writes=[sdk])
                P.ts("pool", sd[:], sd[:], 1e-12, None, ALU.max, reads=[sdk], writes=[sdk])
                P.recip(sd[:], sd[:], reads=[sdk], writes=[sdk])
                P.tt("dve", kkbuf[:, m, :], kkbuf[:, m, :], sd[:], ALU.mult, reads=[("kk", m), sdk], writes=[("kk", m)])
                f, fk = f_r.next()
                P.ts("dve", f[:], abuf[:, m, :], pc(PC_KA), None, ALU.mult, reads=[(abk, m), "par"], writes=[fk])
                P.stt(k2buf[:, m, :], f[:], omka[:, m:m + 1], kbuf[:, m, :], ALU.add, ALU.mult,
                      reads=[fk, "omka", (kbk, m)], writes=[("k2", m)])
                P.tt("pool", abuf[:, m, :], abuf[:, m, :], kkbuf[:, m, :], ALU.mult, reads=[(abk, m), ("kk", m)], writes=[(abk, m)])
                P.stt(prbuf[:, m, :], rbuf[:, m, :], pc(PC_RK), k2buf[:, m, :], ALU.mult, ALU.mult,
                      reads=[(rbk, m), ("k2", m), "par"], writes=[("pr", m)])
            for idx, (buf, kn) in enumerate(((rbuf, rbk), (k2buf, "k2"), (kkbuf, "kk"), (abuf, abk))):
                P.dma("sp", E.rwd[idx, :, :, tsl].rearrange("c p t -> p c t"), buf[:], "st_rwd%d" % idx,
                      reads=[(kn, m) for m in range(8)], writes=[("rwd", idx, tile)])
            for idx, (buf, kn) in enumerate(((gbuf, "gbuf"), (prbuf, "pr"))):
                P.dma("sp", E.rwdb[idx, :, :, tsl].rearrange("c p t -> p c t"), buf[:], "st_rwdb%d" % idx,
                      reads=[(kn, m) for m in range(8)], writes=[("rwdb", idx, tile)])
        P.flush()


def phase_rwkv_b(P, E, l):
    nc = E.nc
    j = l // 2
    GN_EPS = 64e-5
    with contextlib.ExitStack() as st:
        load_par(P, E, st, l)
        par = E.par
        wo = st.enter_context(nc.sbuf_tensor(U("wo"), [128, 8, D], BF16))
        load_weight(P, E, wo, E.d_rw_wo[j], "wo", nsplit=2)
        mSU4 = st.enter_context(nc.sbuf_tensor(U("mSU4"), [128, 4, 128], F32))
        mUI4 = st.enter_context(nc.sbuf_tensor(U("mUI4"), [128, 4, 128], F32))
        mSL4 = st.enter_context(nc.sbuf_tensor(U("mSL4"), [128, 4, 128], F32))
        id4 = st.enter_context(nc.sbuf_tensor(U("id4"), [128, 4, 128], F32))
        for q in range(4):
            P.copy("pool", mSU4[:, q, :], E.cst[:, C_SU:C_SU + 128], reads=["consts"], writes=["m4"])
            P.copy("pool", mUI4[:, q, :], E.cst[:, C_UI:C_UI + 128], reads=["consts"], writes=["m4"])
            P.copy("pool", mSL4[:, q, :], E.cst[:, C_SL:C_SL + 128], reads=["consts"], writes=["m4"])
            P.copy("pool", id4[:, q, :], E.cst[:, C_ID:C_ID + 128], reads=["consts"], writes=["m4"])
        bones64 = st.enter_context(nc.sbuf_tensor(U("bones64"), [128, 128], BF16))
        P.ts("dve", bones64[:], E.cst[:, C_BO:C_BO + 128], 1.0 / 64.0, None, ALU.mult, reads=["consts"], writes=["m4"])
        maskUI = E.cst[:, C_UI:C_UI + 128]
        maskSU = E.cst[:, C_SU:C_SU + 128]
        ident = E.cst[:, C_ID:C_ID + 128]
        Pf = st.enter_context(nc.sbuf_tensor(U("Pf"), [128, 512], F32))
        Pb = st.enter_context(nc.sbuf_tensor(U("Pb"), [128, 512], BF16))
        P.memset("pool", Pf[:], 0.0, writes=["Pf"])
        P.memset("pool", Pb[:], 0.0, writes=["Pb"])
        lnb = alloc_ln(st, nc)
        xres_r = Ring(st, nc, "xres", [128, 8, TT], F32, 1)
        zT = st.enter_context(nc.sbuf_tensor(U("zT"), [128, 8, TT], BF16))
        fm_r = [Ring(st, nc, "fm%d" % i, [128, 8, 128], F32, 2) for i in range(4)]
        fb_r = [Ring(st, nc, "fb%d" % i, [128, 8, 128], BF16, 2) for i in range(2)]
        sg_r = Ring(st, nc, "sgl", [128, D], F32, 2)
        vt_r = Ring(st, nc, "vtl", [128, D], F32, 2)
        vb = st.enter_context(nc.sbuf_tensor(U("vb"), [128, D], BF16))
        gam = st.enter_context(nc.sbuf_tensor(U("gam"), [128, 4, 128], F32))
        ginv = st.enter_context(nc.sbuf_tensor(U("ginv"), [128, 4, 128], F32))
        game = st.enter_context(nc.sbuf_tensor(U("game"), [128, 4, 128], F32))
        ghat = st.enter_context(nc.sbuf_tensor(U("ghat"), [128, 4, 128], F32))
        gsm = st.enter_context(nc.sbuf_tensor(U("gsm"), [128, 16], F32))
        Rt = st.enter_context(nc.sbuf_tensor(U("Rt"), [128, 8, 128], BF16))
        At = st.enter_context(nc.sbuf_tensor(U("At"), [128, 8, 128], BF16))
        Bt = st.enter_context(nc.sbuf_tensor(U("Bt"), [128, 8, 128], BF16))
        Kt = st.enter_context(nc.sbuf_tensor(U("Kt"), [128, 8, 128], BF16))
        Atf = st.enter_context(nc.sbuf_tensor(U("Atf"), [128, 4, 128], F32))
        Bhf = st.enter_context(nc.sbuf_tensor(U("Bhf"), [128, 4, 128], F32))
        Khf = st.enter_context(nc.sbuf_tensor(U("Khf"), [128, 4, 128], F32))
        Atm = st.enter_context(nc.sbuf_tensor(U("Atm"), [128, 8, 128], BF16))
        Bhm = st.enter_context(nc.sbuf_tensor(U("Bhm"), [128, 8, 128], BF16))
        Khm = st.enter_context(nc.sbuf_tensor(U("Khm"), [128, 8, 128], BF16))
        LT_r = [Ring(st, nc, "LTs%d" % i, [128, 4, 128], BF16, 2) for i in range(4)]
        L_r = [Ring(st, nc, "Ls%d" % i, [128, 4, 128], BF16, 2) for i in range(4)]
        TT_r = [Ring(st, nc, "TTm%d" % i, [128, 4, 128], BF16, 2) for i in range(4)]
        TTall = st.enter_context(nc.sbuf_tensor(U("TTall"), [128, 16, 128], BF16))
        Lak_r = Ring(st, nc, "LakTs", [128, 4, 128], BF16, 2)
        Y2s = st.enter_context(nc.sbuf_tensor(U("Y2s"), [128, 16, 64], BF16))
        WTs = st.enter_context(nc.sbuf_tensor(U("WTs"), [128, 8, 128], BF16))
        MrbT = st.enter_context(nc.sbuf_tensor(U("MrbT"), [128, 16, 128], BF16))
        MrkT = st.enter_context(nc.sbuf_tensor(U("MrkT"), [128, 16, 128], BF16))
        Us = st.enter_context(nc.sbuf_tensor(U("Us"), [128, 16, 64], BF16))
        Os = st.enter_context(nc.sbuf_tensor(U("Os"), [128, D], F32))
        ob = st.enter_context(nc.sbuf_tensor(U("ob"), [128, 8, 128], BF16))
        osq = st.enter_context(nc.sbuf_tensor(U("osq"), [128, 8, 128], BF16))
        tb = bank_ring(E, [0, 1, 2, 3])
        tb8 = bank_ring(E, [0, 1, 2, 3, 4, 5, 6, 7])
        ybanks = bank_ring(E, [4, 5])
        sbanks = bank_ring(E, [6, 7])
        UB = [(E.PB[4], ("pb", 4)), (E.PB[5], ("pb", 5))]
        OB = [(E.PB[6], ("pb", 6)), (E.PB[7], ("pb", 7))]

        def v4(bank):
            return bank[:].rearrange("p (q t) -> p q t", q=4)

        xcur = None
        for ch in range(int(_os.environ.get("RW_NCH", T // 128))):
            tile, sub = ch // 2, ch % 2
            csl = slice(ch * 128, (ch + 1) * 128)
            if sub == 0:
                xres, xkey = xres_r.next()
                P.dma("sp", xres[:], E.xs[:, :, tile * TT:(tile + 1) * TT].rearrange("c p t -> p c t"), "xs_ld" + str(xkey[1]),
                      reads=[("xs", tile)], writes=[xkey])
                xcur = (xres, xkey)
            fm, fmk = [], []
            for i in range(4):
                t_, k_ = fm_r[i].next()
                P.dma("sp", t_[:], E.rwd[i, :, :, csl].rearrange("c p t -> p c t"), "ld_fm%d_%d" % (i, k_[1]),
                      reads=[("rwd", i, tile)], writes=[k_])
                fm.append(t_)
                fmk.append(k_)
            fb, fbk = [], []
            for i in range(2):
                t_, k_ = fb_r[i].next()
                P.dma("sp", t_[:], E.rwdb[i, :, :, csl].rearrange("c p t -> p c t"), "ld_fb%d_%d" % (i, k_[1]),
                      reads=[("rwdb", i, tile)], writes=[k_])
                fb.append(t_)
                fbk.append(k_)
            sg, sgk = sg_r.next()
            P.dma("sp", sg[:], E.rwsg[csl, :], "ld_sg%d" % sgk[1], reads=[("rwsg", ch * 128)], writes=[sgk])
            vt, vtk = vt_r.next()
            P.dma("sp", vt[:], E.rwv[csl, :], "ld_vt%d" % vtk[1], reads=[("rwv", ch * 128)], writes=[vtk])
            P.copy("pool", vb[:], vt[:], reads=[vtk], writes=["vb"])
            r_f, k2_f, kk_f, bv_f = fm
            rk_, k2k_, kkk_, bvk_ = fmk
            g_b, pr_b = fb
            gk_, prk_ = fbk
            for grp in range(2):
                gi, gik = tb.next()
                ge, gek = tb.next()
                for p4 in range(4):
                    c = grp * 4 + p4
                    P.mm(gi[:, p4 * 128:(p4 + 1) * 128], sg[:, c * 128:(c + 1) * 128], maskUI, start=True, stop=True,
                         reads=[sgk, "consts"], writes=[gik])
                for p4 in range(4):
                    c = grp * 4 + p4
                    P.mm(ge[:, p4 * 128:(p4 + 1) * 128], sg[:, c * 128:(c + 1) * 128], maskSU, start=True, stop=True,
                         reads=[sgk, "consts"], writes=[gek])
                P.act(gam[:], v4(gi), AF.Exp, scale=-C0, reads=[gik], writes=["gam"])
                P.act(ginv[:], v4(gi), AF.Exp, scale=C0, reads=[gik], writes=["ginv"])
                P.act(game[:], v4(ge), AF.Exp, scale=-C0, reads=[gek], writes=["game"])
                for p4 in range(4):
                    c = grp * 4 + p4
                    last = gi[:, p4 * 128 + 127:p4 * 128 + 128]
                    P.act(gsm[:, c:c + 1], last, AF.Identity, scale=-C0, reads=[gik], writes=[("nb", c)])
                    P.act(gsm[:, 8 + c:9 + c], last, AF.Exp, scale=-C0, reads=[gik], writes=[("gC", c)])
                    P.act(ghat[:, p4, :], gi[:, p4 * 128:(p4 + 1) * 128], AF.Exp, scale=C0, bias=gsm[:, c:c + 1],
                          reads=[gik, ("nb", c)], writes=[("ghat", p4)])
                cs = slice(grp * 4, grp * 4 + 4)
                P.tt("dve", Rt[:, cs, :], r_f[:, cs, :], gam[:], ALU.mult, reads=[rk_, "gam"], writes=[("Rt", grp)])
                P.stt(Atf[:], kk_f[:, cs, :], -1.0, game[:], ALU.mult, ALU.mult, reads=[kkk_, "game"], writes=["Atf"])
                P.copy("pool", At[:, cs, :], Atf[:], reads=["Atf"], writes=[("At", grp)])
                P.tt("dve", Bt[:, cs, :], bv_f[:, cs, :], ginv[:], ALU.mult, reads=[bvk_, "ginv"], writes=[("Bt", grp)])
                P.tt("pool", Kt[:, cs, :], k2_f[:, cs, :], ginv[:], ALU.mult, reads=[k2k_, "ginv"], writes=[("Kt", grp)])
                P.tt("dve", Bhf[:], bv_f[:, cs, :], ghat[:], ALU.mult, reads=[bvk_] + [("ghat", q) for q in range(4)], writes=["Bhf"])
                P.tt("pool", Khf[:], k2_f[:, cs, :], ghat[:], ALU.mult, reads=[k2k_] + [("ghat", q) for q in range(4)], writes=["Khf"])
                for (src, skey, dst, dname, eng) in ((Atf, "Atf", Atm, "Atm", "act"), (Bhf, "Bhf", Bhm, "Bhm", "dve"), (Khf, "Khf", Khm, "Khm", "act")):
                    bk, bkey = tb.next()
                    for p4 in range(4):
                        P.tr(bk[:, p4 * 128:(p4 + 1) * 128], src[:, p4, :], ident, reads=[skey, "consts"], writes=[bkey])
                    P.copy(eng, dst[:, cs, :], v4(bk), reads=[bkey], writes=[(dname, grp)])
            if _DBG_STOP <= 1:
                continue
            def hop(hg, q):
                par_i, pblk = hg % 2, hg // 2
                c = 4 * pblk + q
                return 2 * c + par_i, c, slice(64 * par_i, 64 * par_i + 64), pblk
            G = [dict() for _ in range(4)]
            for hg in range(4):
                g = G[hg]
                ltb, ltk = tb8.next()
                lb, lk = tb8.next()
                for q in range(4):
                    h, c, rs, g_ = hop(hg, q)
                    P.mm(ltb[:, q * 128:(q + 1) * 128], Bt[rs, c, :], At[rs, c, :], start=True, stop=True,
                         reads=[("Bt", g_), ("At", g_)], writes=[ltk])
                for q in range(4):
                    h, c, rs, g_ = hop(hg, q)
                    P.mm(lb[:, q * 128:(q + 1) * 128], At[rs, c, :], Bt[rs, c, :], start=True, stop=True,
                         reads=[("Bt", g_), ("At", g_)], writes=[lk])
                g["LTs"], g["LTk"] = LT_r[hg].next()
                g["Ls"], g["Lk"] = L_r[hg].next()
                P.tt("dve", g["LTs"][:], v4(ltb), mSU4[:], ALU.mult, reads=[ltk, "m4"], writes=[g["LTk"]])
                P.tt("dve", g["Ls"][:], v4(lb), mSL4[:], ALU.mult, reads=[lk, "m4"], writes=[g["Lk"]])
                g["TTm"], g["TTk"] = TT_r[hg].next()
                P.tt("pool", g["TTm"][:], g["LTs"][:], id4[:], ALU.add, reads=[g["LTk"], "m4"], writes=[g["TTk"]])
            for hg in range(4):
                g = G[hg]
                lab, lak = tb8.next()
                for q in range(4):
                    h, c, rs, g_ = hop(hg, q)
                    P.mm(lab[:, q * 128:(q + 1) * 128], Kt[rs, c, :], At[rs, c, :], start=True, stop=True,
                         reads=[("Kt", g_), ("At", g_)], writes=[lak])
                g["LakTs"], g["Lakk"] = Lak_r.next()
                P.tt("dve", g["LakTs"][:], v4(lab), mSU4[:], ALU.mult, reads=[lak, "m4"], writes=[g["Lakk"]])
                y2b, y2k = tb8.next()
                for q in range(4):
                    h, c, rs, g_ = hop(hg, q)
                    P.mm(y2b[:, q * 64:(q + 1) * 64], g["LakTs"][:, q, :], vb[:, h * 64:(h + 1) * 64], start=True, stop=True,
                         reads=[g["Lakk"], "vb"], writes=[y2k])
                P.copy("act", Y2s[:, 4 * hg:4 * hg + 4, :], y2b[:, 0:256].rearrange("p (q v) -> p q v", q=4), reads=[y2k], writes=[("Y2s", hg)])
                mbb, mbk = tb8.next()
                for q in range(4):
                    h, c, rs, g_ = hop(hg, q)
                    P.mm(mbb[:, q * 128:(q + 1) * 128], Bt[rs, c, :], Rt[rs, c, :], start=True, stop=True,
                         reads=[("Bt", g_), ("Rt", g_)], writes=[mbk])
                P.tt("dve", MrbT[:, 4 * hg:4 * hg + 4, :], v4(mbb), mUI4[:], ALU.mult, reads=[mbk, "m4"], writes=[("MrbT", hg)])
                mkb, mkk = tb8.next()
                for q in range(4):
                    h, c, rs, g_ = hop(hg, q)
                    P.mm(mkb[:, q * 128:(q + 1) * 128], Kt[rs, c, :], Rt[rs, c, :], start=True, stop=True,
                         reads=[("Kt", g_), ("Rt", g_)], writes=[mkk])
                P.tt("dve", MrkT[:, 4 * hg:4 * hg + 4, :], v4(mkb), mUI4[:], ALU.mult, reads=[mkk, "m4"], writes=[("MrkT", hg)])
            n = 2
            while n <= 64:
                for hg in range(4):
                    g = G[hg]
                    g["lnb"], g["lnk"] = tb8.next()
                    for q in range(4):
                        P.mm(g["lnb"][:, q * 128:(q + 1) * 128], g["LTs"][:, q, :], g["Ls"][:, q, :], start=True, stop=True,
                             reads=[g["LTk"], g["Lk"]], writes=[g["lnk"]])
                    if n < 64:
                        g["ltnb"], g["ltnk"] = tb8.next()
                        for q in range(4):
                            P.mm(g["ltnb"][:, q * 128:(q + 1) * 128], g["Ls"][:, q, :], g["LTs"][:, q, :], start=True, stop=True,
                                 reads=[g["LTk"], g["Lk"]], writes=[g["ltnk"]])
                    g["Ls2"], g["Lk2"] = L_r[hg].next()
                    P.copy("act", g["Ls2"][:], v4(g["lnb"]), reads=[g["lnk"]], writes=[g["Lk2"]])
                    if n < 64:
                        g["LTs2"], g["LTk2"] = LT_r[hg].next()
                        P.copy("act", g["LTs2"][:], v4(g["ltnb"]), reads=[g["ltnk"]], writes=[g["LTk2"]])
                for hg in range(4):
                    g = G[hg]
                    pb_, pk_ = tb8.next()
                    for q in range(4):
                        P.mm(pb_[:, q * 128:(q + 1) * 128], g["Ls2"][:, q, :], g["TTm"][:, q, :], start=True, stop=True,
                             reads=[g["Lk2"], g["TTk"]], writes=[pk_])
                    if n < 64:
                        TTm2, TTk2 = TT_r[hg].next()
                        P.tt("dve", TTm2[:], v4(pb_), g["TTm"][:], ALU.add, reads=[pk_, g["TTk"]], writes=[TTk2])
                        g["TTm"], g["TTk"] = TTm2, TTk2
                        g["LTs"], g["LTk"] = g["LTs2"], g["LTk2"]
                    else:
                        P.tt("dve", TTall[:, 4 * hg:4 * hg + 4, :], v4(pb_), g["TTm"][:], ALU.add, reads=[pk_, g["TTk"]], writes=[("TTall", hg)])
                    g["Ls"], g["Lk"] = g["Ls2"], g["Lk2"]
                n *= 2
            for hg in range(4):
                par_i, pblk = hg % 2, hg // 2
                wtb, wtk = tb8.next()
                for q in range(4):
                    h, c, rs, g_ = hop(hg, q)
                    P.mm(wtb[rs, q * 128:(q + 1) * 128], Atm[:, c, rs], TTall[:, 4 * hg + q, :], start=True, stop=True,
                         reads=[("Atm", g_), ("TTall", hg)], writes=[wtk])
                rs_ = slice(64 * par_i, 64 * par_i + 64)
                P.copy("act", WTs[rs_, 4 * pblk:4 * pblk + 4, :], wtb[rs_, :].rearrange("p (q t) -> p q t", q=4), reads=[wtk], writes=[("WTs", hg)])
            if _DBG_STOP <= 2:
                continue
            def slot(h):
                i_, c_ = h % 2, h // 2
                return (2 * (c_ // 4) + i_) * 4 + (c_ % 4)
            for i in range(2):
                rs = slice(64 * i, 64 * i + 64)
                ub, ubk = UB[i]
                for c in range(8):
                    h = 2 * c + i
                    sl_ = slot(h)
                    P.mm(ub[:, c * 64:(c + 1) * 64], TTall[:, sl_, :], Y2s[:, sl_, :], start=True, stop=False,
                         reads=[("TTall", sl_ // 4), ("Y2s", sl_ // 4)], writes=[ubk])
                    P.mm(ub[:, c * 64:(c + 1) * 64], WTs[rs, c, :], Pb[rs, c * 64:(c + 1) * 64], start=False, stop=True,
                         reads=[("WTs", sl_ // 4), "Pb"], writes=[ubk])
            for i in range(2):
                ub, ubk = UB[i]
                P.copy("act", Us[:, 8 * i:8 * i + 8, :], ub[:].rearrange("p (q v) -> p q v", q=8), reads=[ubk], writes=[("Us", i)])
            for i in range(2):
                rs = slice(64 * i, 64 * i + 64)
                obk_, obkk = OB[i]
                for c in range(8):
                    h = 2 * c + i
                    sl_ = slot(h)
                    P.mm(obk_[:, c * 64:(c + 1) * 64], Rt[rs, c, :], Pb[rs, c * 64:(c + 1) * 64], start=True, stop=False,
                         reads=[("Rt", c // 4), "Pb"], writes=[obkk])
                    P.mm(obk_[:, c * 64:(c + 1) * 64], MrkT[:, sl_, :], vb[:, h * 64:(h + 1) * 64], start=False, stop=False,
                         reads=[("MrkT", sl_ // 4), "vb"], writes=[obkk])
                    P.mm(obk_[:, c * 64:(c + 1) * 64], MrbT[:, sl_, :], Us[:, 8 * i + c, :], start=False, stop=True,
                         reads=[("MrbT", sl_ // 4), ("Us", i)], writes=[obkk])
            pnb, pnk = tb.next()
            for h in range(16):
                c, i = h // 2, h % 2
                rs = slice(64 * i, 64 * i + 64)
                P.mm(pnb[rs, c * 64:(c + 1) * 64], Bhm[:, c, rs], Us[:, 8 * i + c, :], start=True, stop=False,
                     reads=[("Bhm", c // 4), ("Us", i)], writes=[pnk])
                P.mm(pnb[rs, c * 64:(c + 1) * 64], Khm[:, c, rs], vb[:, h * 64:(h + 1) * 64], start=False, stop=True,
                     reads=[("Khm", c // 4), "vb"], writes=[pnk])
            for c in range(8):
                P.stt(Pf[:, c * 64:(c + 1) * 64], Pf[:, c * 64:(c + 1) * 64], gsm[:, 8 + c:9 + c], pnb[:, c * 64:(c + 1) * 64],
                      ALU.mult, ALU.add, reads=["Pf", ("gC", c), pnk], writes=["Pf"])
            P.copy("pool", Pb[:], Pf[:], reads=["Pf"], writes=["Pb"])
            if _DBG_STOP <= 3:
                continue
            for i in range(2):
                obk_, obkk = OB[i]
                P.copy("act", Os[:].rearrange("p (c i v) -> p c i v", c=8, i=2)[:, :, i, :],
                       obk_[:].rearrange("p (c v) -> p c v", c=8), reads=[obkk], writes=[("Os", i)])
            for grp in range(2):
                cs = slice(grp * 4, grp * 4 + 4)
                otb, otk = tb.next()
                for p4 in range(4):
                    c = grp * 4 + p4
                    P.tr(otb[:, p4 * 128:(p4 + 1) * 128], Os[:, c * 128:(c + 1) * 128], ident, reads=[("Os", 0), ("Os", 1), "consts"], writes=[otk])
                P.act(ob[:, cs, :], v4(otb), AF.Identity, reads=[otk], writes=[("ob", grp)])
                P.act(osq[:, cs, :], v4(otb), AF.Square, reads=[otk], writes=[("osq", grp)])
                P.copy("act", gam[:], v4(otb), reads=[otk], writes=["gam"])
                vtb, vtbk = tb.next()
                for p4 in range(4):
                    c = grp * 4 + p4
                    P.tr(vtb[:, p4 * 128:(p4 + 1) * 128], vt[:, c * 128:(c + 1) * 128], ident, reads=[vtk, "consts"], writes=[vtbk])
                P.copy("act", ginv[:], v4(vtb), reads=[vtbk], writes=["ginv"])
                mnb, mnk = tb.next()
                for p4 in range(4):
                    c = grp * 4 + p4
                    P.mm(mnb[:, p4 * 128:(p4 + 1) * 128], bones64[:], ob[:, c, :], start=True, stop=True, reads=[("ob", grp), "m4"], writes=[mnk])
                P.copy("act", game[:], v4(mnb), reads=[mnk], writes=["game"])
                msb, msk = tb.next()
                for p4 in range(4):
                    c = grp * 4 + p4
                    P.mm(msb[:, p4 * 128:(p4 + 1) * 128], bones64[:], osq[:, c, :], start=True, stop=True, reads=[("osq", grp), "m4"], writes=[msk])
                P.tt("pool", Atf[:], game[:], game[:], ALU.mult, reads=["game"], writes=["Atf"])
                P.stt(Bhf[:], v4(msb), GN_EPS, Atf[:], ALU.add, ALU.subtract, reads=[msk, "Atf"], writes=["Bhf"])
                P.act(Bhf[:], Bhf[:], AF.Sqrt, reads=["Bhf"], writes=["Bhf"])
                P.recip(Bhf[:], Bhf[:], reads=["Bhf"], writes=["Bhf"])
                P.tt("pool", gam[:], gam[:], game[:], ALU.subtract, reads=["gam", "game"], writes=["gam"])
                P.tt("dve", gam[:], gam[:], Bhf[:], ALU.mult, reads=["gam", "Bhf"], writes=["gam"])
                for p4 in range(4):
                    c = grp * 4 + p4
                    P.act(ghat[:, p4, :], gam[:, p4, :], AF.Identity, scale=par[:, PC_GNG + c:PC_GNG + c + 1],
                          bias=par[:, PC_GNB + c:PC_GNB + c + 1], reads=["gam", "par"], writes=[("ghat", p4)])
                bnb, bnk = tb.next()
                for p4 in range(4):
                    c = grp * 4 + p4
                    P.mm(bnb[:, p4 * 128:(p4 + 1) * 128], E.bones[:], pr_b[:, c, :], start=True, stop=True, reads=[prk_, "consts2"], writes=[bnk])
                P.tt("dve", Khf[:], v4(bnb), ginv[:], ALU.mult, reads=[bnk, "ginv"], writes=["Khf"])
                P.tt("pool", Khf[:], Khf[:], ghat[:], ALU.add, reads=["Khf"] + [("ghat", q) for q in range(4)], writes=["Khf"])
                P.tt("dve", zT[:, cs, sub * 128:(sub + 1) * 128], Khf[:], g_b[:, cs, :], ALU.mult, reads=["Khf", gk_], writes=[("zT", grp, sub)])
            if sub == 1:
                xres, xkey = xcur

                def emit_y(m, out_ap, okey):
                    for c in range(8):
                        P.mm(out_ap, wo[:, c, m * 128:(m + 1) * 128], zT[:, c, :], start=(c == 0), stop=(c == 7),
                             reads=["wo"] + [("zT", c // 4, s_) for s_ in range(2)], writes=[okey])
                ln_tail(P, E, lnb, xres, xkey, PC_LN1G, PC_LN1B, emit_y, tile, ybanks, sbanks)
        P.flush()


def build(plan, debug_xs=False):
    nc = bass.Bass("TRN2", target_bir_lowering=False)
    E = Env()
    E.nc = nc

    def din(name, shape):
        return nc.dram_tensor(name, list(shape), F32, kind="ExternalInput").ap()
    E.d_x = din("x", [T, D])
    E.d_cst = din("cst", [128, 640])
    E.d_ropec = din("ropec", [128, T])
    E.d_ropes = din("ropes", [128, T])
    E.d_par = din("par", [4, 128, NPAR])
    E.d_rowp = din("rowp", [2, 2, D])
    E.d_lamp = din("lamp", [2, 128, 256])
    E.d_gsub = din("gsub", [2, 128, 128])
    E.d_ev_win = din("ev_win", [2, D, 4096])
    E.d_ev_wout = din("ev_wout", [2, D, D])
    for n in ("rw_wr", "rw_wk", "rw_wv", "rw_wo"):
        setattr(E, "d_" + n, din(n, [2, D, D]))
    E.d_rw_w1 = din("rw_w1", [2, D, 64])
    E.d_rw_w2 = din("rw_w2", [2, 64, D])
    E.d_rw_a1 = din("rw_a1", [2, D, 64])
    E.d_rw_a2 = din("rw_a2", [2, 64, D])
    E.d_rw_g1 = din("rw_g1", [2, D, 160])
    E.d_rw_g2 = din("rw_g2", [2, 160, D])
    E.d_rw_v1 = din("rw_v1", [1, D, 32])
    E.d_rw_v2 = din("rw_v2", [1, 32, D])
    E.d_ffn_up = din("ffn_up", [4, D, 2 * DFF])
    E.d_ffn_dn = din("ffn_dn", [4, DFF, D])
    E.d_out = nc.dram_tensor("out", [T, D], F32, kind="ExternalOutput").ap()
    E.xs = nc.dram_tensor("xs_scratch", [8, 128, T], F32).ap()
    E.vfirst = nc.dram_tensor("vfirst_scratch", [T, D], F32).ap()
    E.rwd = nc.dram_tensor("rwd_scratch", [4, 8, 128, T], F32).ap()
    E.rwdb = nc.dram_tensor("rwdb_scratch", [2, 8, 128, T], BF16).ap()
    E.rwsg = nc.dram_tensor("rwsg_scratch", [T, D], F32).ap()
    E.rwv = nc.dram_tensor("rwv_scratch", [T, D], F32).ap()
    with contextlib.ExitStack() as st:
        P = Prog(nc, st)
        E.PB = [st.enter_context(nc.psum_tensor("pb%d" % i, [128, 512], F32)) for i in range(8)]
        E.cst = st.enter_context(nc.sbuf_tensor(U("cst_sb"), [128, 640], F32))
        E.onesD = st.enter_context(nc.sbuf_tensor(U("onesD"), [128, 128], BF16))
        E.identb = st.enter_context(nc.sbuf_tensor(U("identb"), [128, 128], BF16))
        E.bones = st.enter_context(nc.sbuf_tensor(U("bones"), [128, 128], BF16))
        P.dma("sp", E.cst[:], E.d_cst[:, :], "cst", writes=["consts"])
        P.memset("pool", E.onesD[:], 1.0 / D, writes=["consts"])
        P.copy("dve", E.identb[:], E.cst[:, C_ID:C_ID + 128], reads=["consts"], writes=["consts2"])
        P.copy("dve", E.bones[:], E.cst[:, C_BO:C_BO + 128], reads=["consts"], writes=["consts2"])
        P.flush()
        for ph in plan:
            if ph[0] == "in":
                phase_in(P, E)
            elif ph[0] == "out":
                phase_out(P, E)
            elif ph[0] == "ffn":
                phase_ffn(P, E, ph[1])
            elif ph[0] == "even":
                phase_even(P, E, ph[1])
            elif ph[0] == "rwkv":
                phase_rwkv_a(P, E, ph[1])
                if len(ph) < 3:
                    phase_rwkv_b(P, E, ph[1])
        E.n_instr = P.n_instr
    return nc, E


FULL_PLAN = [("in",), ("even", 0), ("ffn", 0), ("rwkv", 1), ("ffn", 1), ("even", 2), ("ffn", 2), ("rwkv", 3), ("ffn", 3), ("out",)]


def fm(v):
    return np.ascontiguousarray(np.asarray(v, np.float32).reshape(-1, 128).T)


def host_prepare(inp):
    sh = {}
    idx = np.arange(128)
    ident = (idx[:, None] == idx[None, :]).astype(np.float32)
    su = (idx[:, None] < idx[None, :]).astype(np.float32)
    ui = (idx[:, None] <= idx[None, :]).astype(np.float32)
    sl = (idx[:, None] > idx[None, :]).astype(np.float32)
    bo = ((idx[:, None] // 64) == (idx[None, :] // 64)).astype(np.float32)
    sh["cst"] = np.ascontiguousarray(np.concatenate([ident, su, ui, sl, bo], axis=1))
    inv = (1.0 / (np.float32(10000.0) ** (np.arange(0, 64, 2, dtype=np.float32) / np.float32(64)))).astype(np.float32)
    ang = (np.arange(T, dtype=np.float32)[:, None] * inv[None, :]).astype(np.float32)
    cos = np.cos(ang).astype(np.float32).T
    sin = np.sin(ang).astype(np.float32).T
    cos64 = np.concatenate([cos, cos], 0)
    sin64 = np.concatenate([-sin, sin], 0)
    sh["ropec"] = np.ascontiguousarray(np.concatenate([cos64, cos64], 0))
    sh["ropes"] = np.ascontiguousarray(np.concatenate([sin64, sin64], 0))
    par = np.zeros((4, 128, NPAR), np.float32)
    for l in range(4):
        par[l, :, PC_LN1G:PC_LN1G + 8] = fm(inp["ln1_g"][l])
        par[l, :, PC_LN1B:PC_LN1B + 8] = fm(inp["ln1_b"][l])
        par[l, :, PC_LN2G:PC_LN2G + 8] = fm(inp["ln2_g"][l])
        par[l, :, PC_LN2B:PC_LN2B + 8] = fm(inp["ln2_b"][l])
        for k in range(3):
            par[l, :, PC_FCW + 44 * k:PC_FCW + 44 * k + 44] = fm(inp["ffn_conv_w"][l, k])
        par[l, :, PC_FCB:PC_FCB + 44] = fm(inp["ffn_conv_b"][l])
        if l % 2 == 0:
            i = l // 2
            for k in range(3):
                par[l, :, PC_ECW + 4 * k:PC_ECW + 4 * k + 4] = fm(inp["ev_conv_w"][i, k])
        else:
            j = l // 2
            for q in range(6):
                par[l, :, PC_MIX + 8 * q:PC_MIX + 8 * q + 8] = fm(inp["rw_mix"][j, q])
            par[l, :, PC_W0:PC_W0 + 8] = fm(inp["rw_w0"][j])
            par[l, :, PC_A0:PC_A0 + 8] = fm(inp["rw_a0"][j])
            par[l, :, PC_KK:PC_KK + 8] = fm(inp["rw_k_k"][j])
            par[l, :, PC_KA:PC_KA + 8] = fm(inp["rw_k_a"][j])
            par[l, :, PC_RK:PC_RK + 8] = fm(inp["rw_r_k"][j].reshape(-1))
            par[l, :, PC_GNG:PC_GNG + 8] = fm(inp["rw_gn_g"][j])
            par[l, :, PC_GNB:PC_GNB + 8] = fm(inp["rw_gn_b"][j])
    sh["par"] = par
    rowp = np.zeros((2, 2, D), np.float32)
    rowp[:, 0, :] = inp["rw_w0"]
    rowp[1, 1, :] = inp["rw_v0"][0]
    sh["rowp"] = rowp
    lamp = np.stack([np.concatenate([inp["ev_lam_q1"][i], inp["ev_lam_k1"][i], inp["ev_lam_q2"][i], inp["ev_lam_k2"][i]])
                     for i in range(2)])
    sh["lamp"] = np.ascontiguousarray(np.broadcast_to(lamp[:, None, :], (2, 128, 256))).astype(np.float32)
    sh["gsub"] = np.ascontiguousarray(np.broadcast_to(inp["ev_subln_g"][:, None, :], (2, 128, 128))).astype(np.float32)
    w = np.asarray(inp["ev_w_in"], np.float32)
    perm = np.concatenate([np.concatenate([np.arange(m * 64 + 32, m * 64 + 64), np.arange(m * 64, m * 64 + 32)]) for m in range(8)])
    sh["ev_win"] = np.ascontiguousarray(np.concatenate([w, w[:, :, 1536 + perm], w[:, :, 2048 + perm]], axis=2))
    sh["ev_wout"] = inp["ev_w_out"]
    sh["rw_wr"], sh["rw_wk"], sh["rw_wv"], sh["rw_wo"] = inp["rw_w_r"], inp["rw_w_k"], inp["rw_w_v"], inp["rw_w_o"]
    for n in ("rw_w1", "rw_w2", "rw_a1", "rw_a2", "rw_g1", "rw_g2", "rw_v1", "rw_v2"):
        sh[n] = inp[n]
    sh["ffn_up"] = inp["ffn_w_up"]
    sh["ffn_dn"] = inp["ffn_w_down"]
    return {k: np.ascontiguousarray(np.asarray(v, np.float32)) for k, v in sh.items()}


_CACHE = {}


def run_plan(inputs, plan, x_override=None, n_cores=8):
    key = tuple(plan)
    if key not in _CACHE:
        _CACHE[key] = build(plan)
    nc, E = _CACHE[key]
    shared = host_prepare(inputs)
    x = np.asarray(inputs["x"] if x_override is None else x_override, np.float32)
    in_maps = []
    for b in range(n_cores):
        m = dict(shared)
        m["x"] = np.ascontiguousarray(x[b])
        in_maps.append(m)
    res = run_bass_kernel_spmd(nc, in_maps, core_ids=list(range(n_cores)))
    return np.stack([r["out"] for r in res.results], axis=0)


def kernel(**inputs):
    return run_plan(inputs, FULL_PLAN).astype(np.float32)
```

```python
import contextlib
import math
import numpy as np
import concourse.bass as bass
import concourse.mybir as mybir
from concourse.bass_utils import run_bass_kernel_spmd

F32 = mybir.dt.float32
BF16 = mybir.dt.bfloat16
AF = mybir.ActivationFunctionType
ALU = mybir.AluOpType
AX = mybir.AxisListType

SEM_LIMIT = 30000
T = 4096
D = 1024
TT = 256
NT = T // TT
DFF = 2816
NFC = DFF // 128
ALPHA = float(8 ** 0.25)
LN_EPS = 1e-5
C0 = float(math.exp(-0.5))
NPAR = 320


import os as _os
_DBG_STOP = int(_os.environ.get("RWB_STOP", "9"))
_DBG_SUB = int(_os.environ.get("RWB_SUB", "9"))
_DBG_NMAX = int(_os.environ.get("RWB_NMAX", "64"))


class _Op:
    __slots__ = ("eng", "fn", "deps", "sig", "is_dma", "dkey", "sigval", "idx")


class Prog:
    ENGS = ("pe", "act", "dve", "pool", "sp")
    N_EPOCH = 4
    N_DMA_SEMS = 60

    def __init__(self, nc, stack):
        self.nc = nc
        self.ops = []
        self.last_w = {}
        self.readers = {}
        self.sems = {}
        for e in self.ENGS:
            for i in range(self.N_EPOCH):
                self.sems[(e, i)] = stack.enter_context(nc.semaphore("s_%s_%d" % (e, i)))
        self.dma_pool = [(stack.enter_context(nc.semaphore("s_dma_%d" % i)), 0) for i in range(self.N_DMA_SEMS)]
        self.eng_cnt = {e: 0 for e in self.ENGS}
        self.dma_cnt = {}
        self.waited = {e: {} for e in self.ENGS}
        self.emitted = 0
        self.n_instr = 0

    def add(self, eng, fn, reads=(), writes=(), dkey=None):
        op = _Op()
        op.eng = eng
        op.fn = fn
        op.is_dma = dkey is not None
        op.dkey = dkey
        op.sig = False
        op.sigval = None
        op.idx = len(self.ops)
        deps = {}

        def adddep(d):
            if d is None or d.idx < self.emitted:
                return
            if (not d.is_dma) and d.eng == "pe" and eng == "pe" and not op.is_dma:
                return
            k = ("dma", d.dkey) if d.is_dma else d.eng
            o = deps.get(k)
            if o is None or o.idx < d.idx:
                deps[k] = d
        for k in reads:
            adddep(self.last_w.get(k))
        for k in writes:
            adddep(self.last_w.get(k))
            for r in self.readers.get(k, {}).values():
                adddep(r)
        for k in writes:
            self.last_w[k] = op
            self.readers[k] = {}
        for k in reads:
            rk = ("dma", dkey) if op.is_dma else eng
            self.readers.setdefault(k, {})[rk] = op
        op.deps = list(deps.values())
        for d in op.deps:
            d.sig = True
        self.ops.append(op)
        return op

    def mm(self, out, lhsT, rhs, start=True, stop=True, reads=(), writes=()):
        return self.add("pe", lambda e: e.matmul(out, lhsT, rhs, start=start, stop=stop), reads, writes)

    def tr(self, out, in_, ident, reads=(), writes=()):
        return self.add("pe", lambda e: e.transpose(out, in_, ident), reads, writes)

    def act(self, out, in_, func, bias=None, scale=None, accum_out=None, reads=(), writes=()):
        kw = {}
        if bias is not None:
            kw["bias"] = bias
        if scale is not None:
            kw["scale"] = scale
        if accum_out is not None:
            kw["accum_out"] = accum_out
        return self.add("act", lambda e: e.activation(out, in_, func, **kw), reads, writes)

    def tt(self, eng, out, in0, in1, op, reads=(), writes=()):
        return self.add(eng, lambda e: e.tensor_tensor(out, in0, in1, op), reads, writes)

    def ts(self, eng, out, in0, s1, s2, op0, op1=None, reads=(), writes=()):
        if op1 is None:
            return self.add(eng, lambda e: e.tensor_scalar(out, in0, s1, None, op0), reads, writes)
        return self.add(eng, lambda e: e.tensor_scalar(out, in0, s1, s2, op0, op1), reads, writes)

    def stt(self, out, in0, scalar, in1, op0, op1, reads=(), writes=()):
        return self.add("dve", lambda e: e.scalar_tensor_tensor(out, in0, scalar, in1, op0, op1), reads, writes)

    def copy(self, eng, out, in_, reads=(), writes=()):
        if eng == "act":
            return self.add(eng, lambda e: e.copy(out, in_), reads, writes)
        return self.add(eng, lambda e: e.tensor_copy(out, in_), reads, writes)

    def recip(self, out, in_, reads=(), writes=()):
        return self.add("dve", lambda e: e.reciprocal(out, in_), reads, writes)

    def memset(self, eng, ap, val, writes=()):
        return self.add(eng, lambda e: e.memset(ap, val), (), writes)

    def dma(self, q, out, in_, dkey, reads=(), writes=(), **kw):
        return self.add(q, lambda e: e.dma_start(out, in_, **kw), reads, writes, dkey=dkey)

    def flush(self):
        nc = self.nc
        ops = self.ops[self.emitted:]
        self.emitted = len(self.ops)
        if not ops:
            return
        last = {}
        for op in ops:
            last[op.eng] = op
        for e, op in last.items():
            op.sig = True
        eng_cnt = self.eng_cnt
        dma_cnt = self.dma_cnt
        for op in ops:
            if op.is_dma:
                c = dma_cnt.get(op.dkey, 0) + 16
                dma_cnt[op.dkey] = c
                sn = ("dma", op.dkey)
                if sn not in self.sems:
                    self.dma_pool.sort(key=lambda t: -t[1])
                    sem_, base_ = self.dma_pool.pop()
                    self.sems[sn] = sem_
                    c = base_ + 16
                    dma_cnt[op.dkey] = c
                op.sigval = (sn, c)
            elif op.sig:
                c = eng_cnt[op.eng] + 1
                eng_cnt[op.eng] = c
                op.sigval = ((op.eng, (c - 1) // SEM_LIMIT), (c - 1) % SEM_LIMIT + 1)
        sems = self.sems
        per_eng = {e: [] for e in self.ENGS}
        for op in ops:
            per_eng[op.eng].append(op)
        finals = {}
        for e, op in last.items():
            if not op.is_dma:
                finals[op.sigval[0]] = max(finals.get(op.sigval[0], 0), op.sigval[1])
        for dk, c in dma_cnt.items():
            finals[("dma", dk)] = c
        N_EPOCH = self.N_EPOCH
        with nc.Block() as block:
            def make(ename):
                eops = per_eng[ename]
                waited = self.waited[ename]

                def do_wait(e, sn, v):
                    if waited.get(sn, 0) >= v:
                        return
                    if sn[0] != "dma":
                        if any(waited.get((sn[0], j), 0) > 0 for j in range(sn[1] + 1, N_EPOCH)):
                            return
                    e.wait_ge(sems[sn], v)
                    waited[sn] = v
                    self.n_instr += 1

                def body(e):
                    for op in eops:
                        need = {}
                        for d in op.deps:
                            sn, v = d.sigval
                            if need.get(sn, 0) < v:
                                need[sn] = v
                        for sn, v in need.items():
                            do_wait(e, sn, v)
                        ins = op.fn(e)
                        self.n_instr += 1
                        if op.is_dma:
                            ins.then_inc(sems[op.sigval[0]], 16)
                        elif op.sig:
                            ins.then_inc(sems[op.sigval[0]], 1)
                    for sn, v in finals.items():
                        do_wait(e, sn, v)
                return body

            block.tensor(make("pe"))
            block.scalar(make("act"))
            block.vector(make("dve"))
            block.gpsimd(make("pool"))
            block.sync(make("sp"))
        for op in ops:
            op.fn = None
        for dk in list(dma_cnt.keys()):
            sn = ("dma", dk)
            self.dma_pool.append((self.sems.pop(sn), dma_cnt.pop(dk)))
            for e in self.ENGS:
                self.waited[e].pop(sn, None)


_UID = [0]


def U(name):
    _UID[0] += 1
    return "sb%d_%s" % (_UID[0], name)


class Ring:
    def __init__(self, st, nc, name, shape, dtype, n):
        self.t = [st.enter_context(nc.sbuf_tensor(U("%s_%d" % (name, i)), shape, dtype)) for i in range(n)]
        self.i = 0
        self.name = name
        self.n = n

    def next(self):
        j = self.i % self.n
        self.i += 1
        return self.t[j], (self.name, j)


class Env:
    pass


C_ID, C_SU, C_UI, C_SL, C_BO = 0, 128, 256, 384, 512

PC_LN1G, PC_LN1B, PC_LN2G, PC_LN2B = 0, 8, 16, 24
PC_FCW, PC_FCB = 32, 164
PC_ECW = 208
PC_MIX, PC_W0, PC_A0, PC_KK, PC_KA, PC_RK, PC_GNG, PC_GNB = 208, 256, 264, 272, 280, 288, 296, 304


def bank_ring(E, idxs):
    r = Env()
    r.idxs = list(idxs)
    r.i = 0

    def nxt():
        j = r.idxs[r.i % len(r.idxs)]
        r.i += 1
        return E.PB[j], ("pb", j)
    r.next = nxt
    return r


def load_weight(P, E, dst, src_ap, key, nsplit=1):
    C = dst.shape[1]
    step = max(1, C // nsplit)
    for c0 in range(0, C, step):
        c1 = min(C, c0 + step)
        P.dma("pool", dst[:, c0:c1, :], src_ap[c0 * 128:c1 * 128, :].rearrange("(c p) n -> p c n", p=128),
              "w_" + key, writes=[key], max_dma_last_dim=4096)


def ln_tail(P, E, st_bufs, xres, xres_key, gcol, bcol, emit_y, tile, ybanks, sbanks):
    nc = E.nc
    s, sb, sq = st_bufs["s"], st_bufs["sb"], st_bufs["sq"]
    xo = xres
    par = E.par
    mb, mkey = sbanks.next()
    qb, qkey = sbanks.next()
    pend = []

    def stats(m, sbm, sbk, sqm, sqk):
        P.mm(mb[:, 0:TT], E.onesD[:], sbm[:], start=(m == 0), stop=(m == 7), reads=[sbk, "consts"], writes=[mkey])
        P.mm(qb[:, 0:TT], E.onesD[:], sqm[:], start=(m == 0), stop=(m == 7), reads=[sqk, "consts"], writes=[qkey])
    for m in range(8):
        bk, bkey = ybanks.next()
        emit_y(m, bk[:, 0:TT], bkey)
        P.stt(s[:, m, :], xres[:, m, :], ALPHA, bk[:, 0:TT], ALU.mult, ALU.add,
              reads=[xres_key, bkey], writes=[("ln_s", m)])
        sbm, sbk = sb.next()
        sqm, sqk = sq.next()
        P.act(sbm[:], s[:, m, :], AF.Identity, reads=[("ln_s", m)], writes=[sbk])
        P.act(sqm[:], s[:, m, :], AF.Square, reads=[("ln_s", m)], writes=[sqk])
        if pend:
            stats(*pend.pop())
        pend.append((m, sbm, sbk, sqm, sqk))
    stats(*pend.pop())
    mean, m2, rstd = st_bufs["mean"], st_bufs["m2"], st_bufs["rstd"]
    P.copy("act", mean[:], mb[:, 0:TT], reads=[mkey], writes=["ln_mean"])
    P.tt("pool", m2[:], mean[:], mean[:], ALU.mult, reads=["ln_mean"], writes=["ln_m2"])
    P.stt(rstd[:], qb[:, 0:TT], LN_EPS, m2[:], ALU.add, ALU.subtract, reads=[qkey, "ln_m2"], writes=["ln_rstd"])
    P.act(rstd[:], rstd[:], AF.Sqrt, reads=["ln_rstd"], writes=["ln_rstd"])
    P.recip(rstd[:], rstd[:], reads=["ln_rstd"], writes=["ln_rstd"])
    for m in range(8):
        P.tt("pool", s[:, m, :], s[:, m, :], mean[:], ALU.subtract, reads=[("ln_s", m), "ln_mean"], writes=[("ln_s", m)])
        P.tt("dve", s[:, m, :], s[:, m, :], rstd[:], ALU.mult, reads=[("ln_s", m), "ln_rstd"], writes=[("ln_s", m)])
        P.act(xo[:, m, :], s[:, m, :], AF.Identity, scale=par[:, gcol + m:gcol + m + 1], bias=par[:, bcol + m:bcol + m + 1],
              reads=[("ln_s", m), "par"], writes=[xres_key])
    P.dma("sp", E.xs[:, :, tile * TT:(tile + 1) * TT].rearrange("c p t -> p c t"), xo[:], "xs_st",
          reads=[xres_key], writes=[("xs", tile)])


def alloc_ln(st, nc):
    b = {}
    b["s"] = st.enter_context(nc.sbuf_tensor(U("ln_s"), [128, 8, TT], F32))
    b["sb"] = Ring(st, nc, "ln_sb", [128, TT], BF16, 3)
    b["sq"] = Ring(st, nc, "ln_sq", [128, TT], BF16, 3)
    b["mean"] = st.enter_context(nc.sbuf_tensor(U("ln_mean"), [128, TT], F32))
    b["m2"] = st.enter_context(nc.sbuf_tensor(U("ln_m2"), [128, TT], F32))
    b["rstd"] = st.enter_context(nc.sbuf_tensor(U("ln_rstd"), [128, TT], F32))
    return b


def load_par(P, E, st, l):
    nc = E.nc
    par = st.enter_context(nc.sbuf_tensor(U("par_sb"), [128, NPAR], F32))
    P.dma("sp", par[:], E.d_par[l], "par", writes=["par"])
    E.par = par


def load_xtile(P, E, xres_ring, xb_ring, tile):
    xres, xkey = xres_ring.next()
    P.dma("sp", xres[:], E.xs[:, :, tile * TT:(tile + 1) * TT].rearrange("c p t -> p c t"), "xs_ld" + str(xkey[1]),
          reads=[("xs", tile)], writes=[xkey])
    xb, xbkey = xb_ring.next()
    P.copy("pool", xb[:], xres[:], reads=[xkey], writes=[xbkey])
    return xres, xkey, xb, xbkey


def phase_in(P, E):
    nc = E.nc
    with contextlib.ExitStack() as st:
        xt_r = Ring(st, nc, "pi_xt", [128, D], F32, 2)
        xo_r = Ring(st, nc, "pi_xo", [128, 8, 128], F32, 2)
        banks = bank_ring(E, [0, 1, 2, 3])
        for blk in range(T // 128):
            xt, xk = xt_r.next()
            P.dma("sp", xt[:], E.d_x[blk * 128:(blk + 1) * 128, :], "pi_ld" + str(xk[1]), writes=[xk])
            xo, ok = xo_r.next()
            for half in range(2):
                bk, bkey = banks.next()
                for j in range(4):
                    c = half * 4 + j
                    P.tr(bk[:, j * 128:(j + 1) * 128], xt[:, c * 128:(c + 1) * 128], E.cst[:, C_ID:C_ID + 128],
                         reads=[xk, "consts"], writes=[bkey])
                eng = "act" if half == 0 else "dve"
                P.copy(eng, xo[:, half * 4:half * 4 + 4, :], bk[:].rearrange("p (j t) -> p j t", j=4),
                       reads=[bkey], writes=[(ok, half)])
            P.dma("sp", E.xs[:, :, blk * 128:(blk + 1) * 128].rearrange("c p t -> p c t"), xo[:], "pi_st" + str(ok[1]),
                  reads=[(ok, 0), (ok, 1)], writes=[("xs", blk // 2)])
        P.flush()


def phase_out(P, E):
    nc = E.nc
    with contextlib.ExitStack() as st:
        xi_r = Ring(st, nc, "po_xi", [128, 8, 128], F32, 2)
        xo_r = Ring(st, nc, "po_xo", [128, D], F32, 2)
        banks = bank_ring(E, [0, 1, 2, 3])
        for blk in range(T // 128):
            xi, ik = xi_r.next()
            P.dma("sp", xi[:], E.xs[:, :, blk * 128:(blk + 1) * 128].rearrange("c p t -> p c t"), "po_ld" + str(ik[1]),
                  reads=[("xs", blk // 2)], writes=[ik])
            xo, ok = xo_r.next()
            for half in range(2):
                bk, bkey = banks.next()
                for j in range(4):
                    c = half * 4 + j
                    P.tr(bk[:, j * 128:(j + 1) * 128], xi[:, c, :], E.cst[:, C_ID:C_ID + 128],
                         reads=[ik, "consts"], writes=[bkey])
                eng = "act" if half == 0 else "dve"
                P.copy(eng, xo[:, half * 512:(half + 1) * 512], bk[:], reads=[bkey], writes=[(ok, half)])
            P.dma("sp", E.d_out[blk * 128:(blk + 1) * 128, :], xo[:], "po_st" + str(ok[1]),
                  reads=[(ok, 0), (ok, 1)], writes=[("out", blk)])
        P.flush()


def phase_ffn(P, E, l):
    nc = E.nc
    with contextlib.ExitStack() as st:
        load_par(P, E, st, l)
        par = E.par
        wup = st.enter_context(nc.sbuf_tensor(U("wup"), [128, 8, 2 * DFF], BF16))
        wdn = st.enter_context(nc.sbuf_tensor(U("wdn"), [128, NFC, D], BF16))
        GW = 11 * 128
        for gq in (0, 2, 1, 3):
            for c in range(8):
                P.dma("pool", wup[:, c, gq * GW:(gq + 1) * GW], E.d_ffn_up[l, c * 128:(c + 1) * 128, gq * GW:(gq + 1) * GW],
                      "w_up%d" % gq, writes=[("wup", gq)], max_dma_last_dim=4096)
        load_weight(P, E, wdn, E.d_ffn_dn[l], "wdn", nsplit=2)
        lnb = alloc_ln(st, nc)
        xres_r = Ring(st, nc, "xres", [128, 8, TT], F32, 2)
        xb_r = Ring(st, nc, "xb", [128, 8, TT], BF16, 1)
        gT = st.enter_context(nc.sbuf_tensor(U("gT"), [128, NFC, TT], BF16))
        carry = st.enter_context(nc.sbuf_tensor(U("carry"), [128, 2 * NFC, 2], F32))
        hs_r = Ring(st, nc, "hs", [128, TT + 2], F32, 4)
        cv_r = Ring(st, nc, "cv", [128, TT], F32, 4)
        sg_r = Ring(st, nc, "sg", [128, TT], F32, 2)
        P.memset("pool", carry[:], 0.0, writes=[("carry", ch) for ch in range(2 * NFC)])
        hbanks = bank_ring(E, [0, 1, 2, 3])
        ybanks = bank_ring(E, [4, 5])
        sbanks = bank_ring(E, [6, 7])
        nxt = load_xtile(P, E, xres_r, xb_r, 0)
        for tile in range(NT):
            xres, xkey, xb, xbkey = nxt
            cvs = {}
            for j in range(NFC):
                for which in range(2):
                    ch = j + which * NFC
                    bk, bkey = hbanks.next()
                    for c in range(8):
                        P.mm(bk[:, 0:TT], wup[:, c, ch * 128:(ch + 1) * 128], xb[:, c, :], start=(c == 0), stop=(c == 7),
                             reads=[("wup", ch // 11), xbkey], writes=[bkey])
                    hs, hkey = hs_r.next()
                    P.copy("pool", hs[:, 0:2], carry[:, ch, :], reads=[("carry", ch)], writes=[(hkey, "c")])
                    P.copy("act", hs[:, 2:TT + 2], bk[:, 0:TT], reads=[bkey], writes=[(hkey, "m")])
                    P.copy("pool", carry[:, ch, :], hs[:, TT:TT + 2], reads=[(hkey, "m")], writes=[("carry", ch)])
                    cv, ckey = cv_r.next()
                    w0c = par[:, PC_FCW + ch:PC_FCW + ch + 1]
                    w1c = par[:, PC_FCW + 44 + ch:PC_FCW + 44 + ch + 1]
                    w2c = par[:, PC_FCW + 88 + ch:PC_FCW + 88 + ch + 1]
                    bc = par[:, PC_FCB + ch:PC_FCB + ch + 1]
                    P.act(cv[:], bk[:, 0:TT], AF.Identity, scale=w2c, bias=bc, reads=[bkey, "par"], writes=[ckey])
                    P.stt(cv[:], hs[:, 1:TT + 1], w1c, cv[:], ALU.mult, ALU.add,
                          reads=[(hkey, "m"), (hkey, "c"), ckey, "par"], writes=[ckey])
                    P.stt(cv[:], hs[:, 0:TT], w0c, cv[:], ALU.mult, ALU.add,
                          reads=[(hkey, "m"), (hkey, "c"), ckey, "par"], writes=[ckey])
                    cvs[which] = (cv, ckey)
                sg, skey = sg_r.next()
                P.act(sg[:], cvs[0][0][:], AF.Silu, reads=[cvs[0][1]], writes=[skey])
                P.tt("dve", gT[:, j, :], sg[:], cvs[1][0][:], ALU.mult, reads=[skey, cvs[1][1]], writes=[("gT", j)])
            if tile + 1 < NT:
                nxt = load_xtile(P, E, xres_r, xb_r, tile + 1)

            def emit_y(m, out_ap, okey):
                for j in range(NFC):
                    P.mm(out_ap, wdn[:, j, m * 128:(m + 1) * 128], gT[:, j, :], start=(j == 0), stop=(j == NFC - 1),
                         reads=["wdn", ("gT", j)], writes=[okey])
            ln_tail(P, E, lnb, xres, xkey, PC_LN2G, PC_LN2B, emit_y, tile, ybanks, sbanks)
        P.flush()


def phase_even(P, E, l):
    nc = E.nc
    i = l // 2
    lam_init = 0.8 - 0.6 * math.exp(-0.3 * l)
    with contextlib.ExitStack() as st:
        load_par(P, E, st, l)
        par = E.par
        win = st.enter_context(nc.sbuf_tensor(U("win"), [128, 8, 4096], BF16))
        wout = st.enter_context(nc.sbuf_tensor(U("wout"), [128, 8, D], BF16))
        for c in range(8):
            P.dma("pool", win[:, c, :], E.d_ev_win[i, c * 128:(c + 1) * 128, :], "w_in", writes=["win"],
                  max_dma_last_dim=4096)
        load_weight(P, E, wout, E.d_ev_wout[i], "wout", nsplit=2)
        KT = st.enter_context(nc.sbuf_tensor(U("KT"), [128, 4, T], BF16))
        VP = st.enter_context(nc.sbuf_tensor(U("VP"), [128, T // 128, 4, 129], BF16))
        P.memset("pool", VP[:], 1.0, writes=[("VP", kb) for kb in range(T // 128)])
        lamp = st.enter_context(nc.sbuf_tensor(U("lamp"), [128, 256], F32))
        gsub = st.enter_context(nc.sbuf_tensor(U("gsub_sb"), [128, 128], F32))
        lsm = st.enter_context(nc.sbuf_tensor(U("lsm"), [128, 8], F32))
        lpr = st.enter_context(nc.sbuf_tensor(U("lpr"), [128, 2, 64], F32))
        P.dma("sp", lamp[:], E.d_lamp[i], "lamp", writes=["lamp"])
        P.dma("sp", gsub[:], E.d_gsub[i], "gsubd", writes=["gsub"])
        P.ts("dve", gsub[:], gsub[:], float(1.0 - lam_init), None, ALU.mult, reads=["gsub"], writes=["gsub"])
        P.tt("dve", lpr[:, 0, :], lamp[:, 0:64], lamp[:, 64:128], ALU.mult, reads=["lamp"], writes=["lpr"])
        P.tt("dve", lpr[:, 1, :], lamp[:, 128:192], lamp[:, 192:256], ALU.mult, reads=["lamp"], writes=["lpr"])
        P.add("dve", lambda e: e.reduce_sum(lsm[:, 0:1], lpr[:, 0, :], AX.X), ["lpr"], ["lsm"])
        P.add("dve", lambda e: e.reduce_sum(lsm[:, 1:2], lpr[:, 1, :], AX.X), ["lpr"], ["lsm"])
        P.act(lsm[:, 2:4], lsm[:, 0:2], AF.Exp, reads=["lsm"], writes=["lsm"])
        P.tt("dve", lsm[:, 4:5], lsm[:, 3:4], lsm[:, 2:3], ALU.subtract, reads=["lsm"], writes=["lsm"])
        P.ts("dve", lsm[:, 5:6], lsm[:, 4:5], float(-lam_init), None, ALU.add, reads=["lsm"], writes=["neglam"])
        neglam = lsm[:, 5:6]
        lnb = alloc_ln(st, nc)
        xres_r = Ring(st, nc, "xres", [128, 8, TT], F32, 2)
        xb_r = Ring(st, nc, "xb", [128, 8, TT], BF16, 1)
        catT = st.enter_context(nc.sbuf_tensor(U("catT"), [128, 8, TT], BF16))
        QT = st.enter_context(nc.sbuf_tensor(U("QT"), [128, 4, TT], BF16))
        carry = st.enter_context(nc.sbuf_tensor(U("carry_u"), [128, 4, 2], F32))
        P.memset("pool", carry[:], 0.0, writes=[("carry", c) for c in range(4)])
        rc_r = Ring(st, nc, "rc", [128, TT], F32, 1)
        rs_r = Ring(st, nc, "rs", [128, TT], F32, 1)
        gcs_r = Ring(st, nc, "gcs", [128, TT], F32, 1)
        u_r = Ring(st, nc, "u", [128, TT + 2], F32, 2)
        cv_r = Ring(st, nc, "cv", [128, TT], F32, 1)
        t1_r = Ring(st, nc, "t1", [128, TT], F32, 1)
        t2_r = Ring(st, nc, "t2", [128, TT], F32, 1)
        pT_r = Ring(st, nc, "pT", [128, 4, 128], BF16, 4)
        sm_r = Ring(st, nc, "sm", [128, 8], F32, 2)
        tn_r = Ring(st, nc, "tn", [128, 128], F32, 2)
        o_r = Ring(st, nc, "o", [128, 128], F32, 2)
        on_r = Ring(st, nc, "on", [128, 128], F32, 2)
        junk = st.enter_context(nc.sbuf_tensor(U("junk"), [128, 128], BF16))
        ibanks = bank_ring(E, [0, 1, 2, 3])
        abanks = bank_ring(E, [4, 5, 6, 7])
        ybanks = bank_ring(E, [4, 5])
        sbanks = bank_ring(E, [6, 7])
        maskUI = E.cst[:, C_UI:C_UI + 128]
        ident = E.cst[:, C_ID:C_ID + 128]

        def proj(bk, bkey, col0, xb, xbkey):
            for c in range(8):
                P.mm(bk[:, 0:TT], win[:, c, col0:col0 + 128], xb[:, c, :], start=(c == 0), stop=(c == 7),
                     reads=["win", xbkey], writes=[bkey])

        nxt = load_xtile(P, E, xres_r, xb_r, 0)
        for tile in range(NT):
            xres, xkey, xb, xbkey = nxt
            tsl = slice(tile * TT, (tile + 1) * TT)
            rc, rckey = rc_r.next()
            rs, rskey = rs_r.next()
            P.dma("sp", rc[:], E.d_ropec[:, tsl], "rc" + str(rckey[1]), writes=[rckey])
            P.dma("sp", rs[:], E.d_ropes[:, tsl], "rs" + str(rskey[1]), writes=[rskey])
            for cc in range(4):
                b_gc, k_gc = ibanks.next()
                proj(b_gc, k_gc, 512 + cc * 128, xb, xbkey)
                b_xi, k_xi = ibanks.next()
                proj(b_xi, k_xi, 1024 + cc * 128, xb, xbkey)
                b_gb, k_gb = ibanks.next()
                proj(b_gb, k_gb, cc * 128, xb, xbkey)
                gcs, gkey = gcs_r.next()
                P.copy("act", gcs[:], b_gc[:, 0:TT], reads=[k_gc], writes=[gkey])
                u, ukey = u_r.next()
                P.copy("pool", u[:, 0:2], carry[:, cc, :], reads=[("carry", cc)], writes=[(ukey, "c")])
                P.tt("dve", u[:, 2:TT + 2], b_xi[:, 0:TT], gcs[:], ALU.mult, reads=[k_xi, gkey], writes=[(ukey, "m")])
                P.copy("pool", carry[:, cc, :], u[:, TT:TT + 2], reads=[(ukey, "m")], writes=[("carry", cc)])
                cv, ckey = cv_r.next()
                w0c = par[:, PC_ECW + cc:PC_ECW + cc + 1]
                w1c = par[:, PC_ECW + 4 + cc:PC_ECW + 4 + cc + 1]
                w2c = par[:, PC_ECW + 8 + cc:PC_ECW + 8 + cc + 1]
                P.ts("dve", cv[:], u[:, 2:TT + 2], w2c, None, ALU.mult, reads=[(ukey, "m"), "par"], writes=[ckey])
                P.stt(cv[:], u[:, 1:TT + 1], w1c, cv[:], ALU.mult, ALU.add, reads=[(ukey, "m"), (ukey, "c"), ckey], writes=[ckey])
                P.stt(cv[:], u[:, 0:TT], w0c, cv[:], ALU.mult, ALU.add, reads=[(ukey, "m"), (ukey, "c"), ckey], writes=[ckey])
                P.tt("dve", catT[:, cc, :], b_gb[:, 0:TT], cv[:], ALU.mult, reads=[k_gb, ckey], writes=[("catT", cc)])
            for cc in range(4):
                for (col0, colsw, dst, dkeyw) in ((1536, 3072, QT[:, cc, :], ("QT", cc)),
                                                  (2048, 3584, KT[:, cc, tsl], ("KT", tile, cc))):
                    b1, k1 = ibanks.next()
                    proj(b1, k1, col0 + cc * 128, xb, xbkey)
                    b2, k2 = ibanks.next()
                    proj(b2, k2, colsw + cc * 128, xb, xbkey)
                    t1, t1k = t1_r.next()
                    t2, t2k = t2_r.next()
                    P.tt("dve", t1[:], b1[:, 0:TT], rc[:], ALU.mult, reads=[k1, rckey], writes=[t1k])
                    P.tt("dve", t2[:], b2[:, 0:TT], rs[:], ALU.mult, reads=[k2, rskey], writes=[t2k])
                    P.tt("pool", dst, t1[:], t2[:], ALU.add, reads=[t1k, t2k], writes=[dkeyw])
            for sub in range(TT // 128):
                kb = tile * (TT // 128) + sub
                bv, kv = ibanks.next()
                for c in range(8):
                    P.mm(bv[:], xb[:, c, sub * 128:(sub + 1) * 128], win[:, c, 2560:3072], start=(c == 0), stop=(c == 7),
                         reads=["win", xbkey], writes=[kv])
                P.copy("act", VP[:, kb, :, 0:128], bv[:].rearrange("p (h d) -> p h d", h=4), reads=[kv], writes=[("VP", kb)])
            pend = []

            def drain():
                while pend:
                    pend.pop(0)()
            for sub in range(TT // 128):
                qb = tile * (TT // 128) + sub
                qsl = slice(sub * 128, (sub + 1) * 128)
                for h in range(4):
                    accs = [abanks.next(), abanks.next()]
                    kbs = list(range(qb + 1))
                    for g0 in range(0, qb + 1, 4):
                        grp = kbs[g0:g0 + 4]
                        ng = len(grp)
                        sb2 = [ibanks.next(), ibanks.next()]
                        for g, kb in enumerate(grp):
                            for m in range(2):
                                sbk, skey = sb2[m]
                                P.mm(sbk[:, g * 128:(g + 1) * 128], KT[64 * m:64 * m + 64, h, kb * 128:(kb + 1) * 128],
                                     QT[64 * m:64 * m + 64, h, qsl], start=True, stop=True,
                                     reads=[("KT", kb // (TT // 128), h), ("QT", h)], writes=[skey])
                        pts = []
                        for m in range(2):
                            sbk, skey = sb2[m]
                            pT, pkey = pT_r.next()
                            P.act(pT[:, 0:ng, :], sbk[:, 0:ng * 128].rearrange("p (g q) -> p g q", g=ng), AF.Exp, scale=0.125,
                                  reads=[skey], writes=[pkey])
                            if qb in grp:
                                gd = grp.index(qb)
                                P.tt("pool", pT[:, gd, :], pT[:, gd, :], maskUI, ALU.mult, reads=[pkey, "consts"], writes=[pkey])
                            pts.append((pT, pkey))
                        drain()
                        for m in range(2):
                            def pv(grp=grp, pT=pts[m][0], pkey=pts[m][1], acc=accs[m][0], akey=accs[m][1], h=h, qb=qb):
                                for g, kb in enumerate(grp):
                                    P.mm(acc[:, 0:129], pT[:, g, :], VP[:, kb, h, :], start=(kb == 0), stop=(kb == qb),
                                         reads=[pkey, ("VP", kb)], writes=[akey])
                            pend.append(pv)

                    def norm(accs=accs, h=h, qsl=qsl, sub=sub):
                        (acc0, ak0), (acc1, ak1) = accs
                        sm, smk = sm_r.next()
                        P.recip(sm[:, 0:1], acc0[:, 128:129], reads=[ak0], writes=[(smk, 0)])
                        P.recip(sm[:, 1:2], acc1[:, 128:129], reads=[ak1], writes=[(smk, 1)])
                        P.tt("dve", sm[:, 2:3], sm[:, 1:2], neglam, ALU.mult, reads=[(smk, 1), "neglam"], writes=[(smk, 2)])
                        tn, tnk = tn_r.next()
                        P.ts("dve", tn[:], acc1[:, 0:128], sm[:, 2:3], None, ALU.mult, reads=[ak1, (smk, 2)], writes=[tnk])
                        o, ok = o_r.next()
                        P.stt(o[:], acc0[:, 0:128], sm[:, 0:1], tn[:], ALU.mult, ALU.add, reads=[ak0, (smk, 0), tnk], writes=[ok])
                        P.act(junk[:], o[:], AF.Square, accum_out=sm[:, 3:4], reads=[ok], writes=["junk", (smk, 3)])
                        P.ts("dve", sm[:, 4:5], sm[:, 3:4], 1.0 / 128.0, 1e-5, ALU.mult, ALU.add, reads=[(smk, 3)], writes=[(smk, 4)])
                        P.act(sm[:, 4:5], sm[:, 4:5], AF.Sqrt, reads=[(smk, 4)], writes=[(smk, 4)])
                        P.recip(sm[:, 5:6], sm[:, 4:5], reads=[(smk, 4)], writes=[(smk, 5)])
                        on, onk = on_r.next()
                        P.stt(on[:], o[:], sm[:, 5:6], gsub[:], ALU.mult, ALU.mult, reads=[ok, (smk, 5), "gsub"], writes=[onk])
                        trb, trk = ibanks.next()
                        P.tr(trb[:, 0:128], on[:], ident, reads=[onk, "consts"], writes=[trk])
                        P.copy("act", catT[:, 4 + h, qsl], trb[:, 0:128], reads=[trk], writes=[("catT", 4 + h, sub)])
                    pend.append(norm)
            drain()
            if tile + 1 < NT:
                nxt = load_xtile(P, E, xres_r, xb_r, tile + 1)

            def emit_y(m, out_ap, okey):
                for c in range(8):
                    rd = [("catT", c)] if c < 4 else [("catT", c, sb_) for sb_ in range(TT // 128)]
                    P.mm(out_ap, wout[:, c, m * 128:(m + 1) * 128], catT[:, c, :], start=(c == 0), stop=(c == 7),
                         reads=["wout"] + rd, writes=[okey])
            ln_tail(P, E, lnb, xres, xkey, PC_LN1G, PC_LN1B, emit_y, tile, ybanks, sbanks)
        P.flush()


def phase_rwkv_a(P, E, l):
    nc = E.nc
    j = l // 2
    has_vres = j > 0
    NS = TT // 128
    with contextlib.ExitStack() as st:
        load_par(P, E, st, l)
        par = E.par
        wr = st.enter_context(nc.sbuf_tensor(U("wr"), [128, 8, D], BF16))
        wk = st.enter_context(nc.sbuf_tensor(U("wk"), [128, 8, D], BF16))
        wv = st.enter_context(nc.sbuf_tensor(U("wv"), [128, 8, D], BF16))
        load_weight(P, E, wr, E.d_rw_wr[j], "wr", nsplit=2)
        wl = st.enter_context(nc.sbuf_tensor(U("wl"), [128, 8, 320], BF16))
        P.dma("pool", wl[:, :, 0:64], E.d_rw_w1[j].rearrange("(c p) n -> p c n", p=128), "w_l", writes=["wl"])
        P.dma("pool", wl[:, :, 64:128], E.d_rw_a1[j].rearrange("(c p) n -> p c n", p=128), "w_l", writes=["wl"])
        P.dma("pool", wl[:, :, 128:288], E.d_rw_g1[j].rearrange("(c p) n -> p c n", p=128), "w_l", writes=["wl"])
        if has_vres:
            P.dma("pool", wl[:, :, 288:320], E.d_rw_v1[j - 1].rearrange("(c p) n -> p c n", p=128), "w_l", writes=["wl"])
        w2s = st.enter_context(nc.sbuf_tensor(U("w2s"), [64, D], BF16))
        a2s = st.enter_context(nc.sbuf_tensor(U("a2s"), [64, D], BF16))
        g2s = st.enter_context(nc.sbuf_tensor(U("g2s"), [128, 2, D], BF16))
        v2s = st.enter_context(nc.sbuf_tensor(U("v2s"), [32, D], BF16))
        P.dma("pool", w2s[:], E.d_rw_w2[j], "w_l", writes=["wl"], max_dma_last_dim=4096)
        P.dma("pool", a2s[:], E.d_rw_a2[j], "w_l", writes=["wl"], max_dma_last_dim=4096)
        P.dma("pool", g2s[:, 0, :], E.d_rw_g2[j, 0:128, :], "w_l", writes=["wl"], max_dma_last_dim=4096)
        P.dma("pool", g2s[0:32, 1, :], E.d_rw_g2[j, 128:160, :], "w_l", writes=["wl"], max_dma_last_dim=4096)
        if has_vres:
            P.dma("pool", v2s[:], E.d_rw_v2[j - 1], "w_l", writes=["wl"], max_dma_last_dim=4096)
        load_weight(P, E, wk, E.d_rw_wk[j], "wk", nsplit=2)
        load_weight(P, E, wv, E.d_rw_wv[j], "wv", nsplit=2)
        rowf = st.enter_context(nc.sbuf_tensor(U("rowf"), [1, 2, D], F32))
        P.dma("sp", rowf[:], E.d_rowp[j:j + 1, :, :], "rowf", writes=["rowf"])
        ones1 = st.enter_context(nc.sbuf_tensor(U("ones1"), [1, 128], F32))
        P.memset("pool", ones1[:], 1.0, writes=["ones1"])
        omka = st.enter_context(nc.sbuf_tensor(U("omka"), [128, 8], F32))
        P.ts("dve", omka[:], par[:, PC_KA:PC_KA + 8], -1.0, 1.0, ALU.mult, ALU.add, reads=["par"], writes=["omka"])
        xres_r = Ring(st, nc, "xres", [128, 8, TT], F32, 2)
        xx_r = Ring(st, nc, "xx", [128, 8, TT], F32, 1)
        xprev = st.enter_context(nc.sbuf_tensor(U("xprev"), [128, 8, 1], F32))
        P.memset("pool", xprev[:], 0.0, writes=["xprev"])
        mx_r = Ring(st, nc, "mx", [128, 8, TT], BF16, 2)
        rbuf_r = Ring(st, nc, "rbuf", [128, 8, TT], F32, 1)
        kbuf_r = Ring(st, nc, "kbuf", [128, 8, TT], F32, 1)
        abuf_r = Ring(st, nc, "abuf", [128, 8, TT], F32, 1)
        gbuf = st.enter_context(nc.sbuf_tensor(U("gbuf"), [128, 8, TT], BF16))
        kkbuf = st.enter_context(nc.sbuf_tensor(U("kkbuf"), [128, 8, TT], F32))
        k2buf = st.enter_context(nc.sbuf_tensor(U("k2buf"), [128, 8, TT], F32))
        prbuf = st.enter_context(nc.sbuf_tensor(U("prbuf"), [128, 8, TT], BF16))
        h1 = st.enter_context(nc.sbuf_tensor(U("h1"), [64, TT], BF16))
        ha = st.enter_context(nc.sbuf_tensor(U("ha"), [64, TT], BF16))
        hv = st.enter_context(nc.sbuf_tensor(U("hv"), [32, TT], BF16))
        hgb = st.enter_context(nc.sbuf_tensor(U("hgb"), [128, 2, TT], BF16))
        sqb8 = st.enter_context(nc.sbuf_tensor(U("sqb8"), [128, 8, TT], BF16))
        sd8 = st.enter_context(nc.sbuf_tensor(U("sd8"), [128, 8, TT], F32))
        f8 = st.enter_context(nc.sbuf_tensor(U("f8"), [128, 8, TT], F32))
        sgt_r = Ring(st, nc, "sgt", [128, D], F32, 1)
        vt_r = Ring(st, nc, "vt", [128, D], F32, 1)
        if has_vres:
            sgv = st.enter_context(nc.sbuf_tensor(U("sgv"), [128, D], F32))
            vf = st.enter_context(nc.sbuf_tensor(U("vf"), [128, D], F32))
            dd = st.enter_context(nc.sbuf_tensor(U("dd"), [128, D], F32))
        banks = bank_ring(E, [0, 1, 2, 3, 4, 5, 6, 7])

        cur_xx = [None, None]

        def mixed(q, xres, xkey, xx=None, xxk=None):
            xx, xxk = cur_xx
            mx, mk = mx_r.next()
            for c in range(8):
                P.stt(mx[:, c, :], xx[:, c, :], par[:, PC_MIX + q * 8 + c:PC_MIX + q * 8 + c + 1], xres[:, c, :],
                      ALU.mult, ALU.add, reads=[xxk, xkey, "par"], writes=[(mk, c)])
            return mx, [(mk, c) for c in range(8)]

        def proj_fm(w, m, mx, mkeys, bk, bkey, M=128, col0=None):
            cs = slice(m * 128, (m + 1) * 128) if col0 is None else slice(col0, col0 + M)
            wkey = {id(wr): "wr", id(wk): "wk", id(wv): "wv", id(wl): "wl"}[id(w)]
            for c in range(8):
                P.mm(bk[0:M, 0:TT], w[:, c, cs], mx[:, c, :], start=(c == 0), stop=(c == 7),
                     reads=[wkey, mkeys[c]], writes=[bkey])

        NTA = int(_os.environ.get("RW_NT", NT))
        xnext = None
        for tile in range(NTA):
            if tile == 0:
                xnext = xres_r.next()
                P.dma("sp", xnext[0][:], E.xs[:, :, 0:TT].rearrange("c p t -> p c t"), "xs_ld" + str(xnext[1][1]),
                      reads=[("xs", 0)], writes=[xnext[1]])
            xres, xkey = xnext
            if tile + 1 < NTA:
                xnext = xres_r.next()
                P.dma("sp", xnext[0][:], E.xs[:, :, (tile + 1) * TT:(tile + 2) * TT].rearrange("c p t -> p c t"),
                      "xs_ld" + str(xnext[1][1]), reads=[("xs", tile + 1)], writes=[xnext[1]])
            xx, xxk = xx_r.next()
            rbuf, rbk = rbuf_r.next()
            kbuf, kbk = kbuf_r.next()
            abuf, abk = abuf_r.next()
            tsl = slice(tile * TT, (tile + 1) * TT)
            cur_xx[:] = [xx, xxk]
            P.tt("pool", xx[:, :, 1:TT], xres[:, :, 0:TT - 1], xres[:, :, 1:TT], ALU.subtract, reads=[xkey], writes=[xxk])
            P.tt("pool", xx[:, :, 0:1], xprev[:], xres[:, :, 0:1], ALU.subtract, reads=[xkey, "xprev"], writes=[xxk])
            P.copy("pool", xprev[:], xres[:, :, TT - 1:TT], reads=[xkey], writes=["xprev"])
            mx, mkeys = mixed(0, xres, xkey)
            for m in range(8):
                bk, bkey = banks.next()
                proj_fm(wr, m, mx, mkeys, bk, bkey)
                P.copy("act", rbuf[:, m, :], bk[:, 0:TT], reads=[bkey], writes=[(rbk, m)])
            mx, mkeys = mixed(1, xres, xkey)
            bk, bkey = banks.next()
            proj_fm(wl, 0, mx, mkeys, bk, bkey, M=64, col0=0)
            P.act(h1[:], bk[0:64, 0:TT], AF.Tanh, reads=[bkey], writes=["h1"])
            for sub in range(NS):
                sgt, sgk = sgt_r.next()
                for half in range(2):
                    bk, bkey = banks.next()
                    hs_ = slice(half * 512, (half + 1) * 512)
                    P.mm(bk[:], h1[0:64, sub * 128:(sub + 1) * 128], w2s[0:64, hs_], start=True, stop=False,
                         reads=["h1", "wl"], writes=[bkey])
                    P.mm(bk[:], ones1[0:1, :], rowf[0:1, 0, hs_], start=False, stop=True, reads=["ones1", "rowf"], writes=[bkey])
                    P.act(sgt[:, hs_], bk[:], AF.Sigmoid, reads=[bkey], writes=[(sgk, half)])
                r0 = tile * TT + sub * 128
                P.dma("sp", E.rwsg[r0:r0 + 128, :], sgt[:], "st_sg", reads=[(sgk, 0), (sgk, 1)], writes=[("rwsg", r0)])
            mx, mkeys = mixed(2, xres, xkey)
            for m in range(8):
                bk, bkey = banks.next()
                proj_fm(wk, m, mx, mkeys, bk, bkey)
                P.copy("act", kbuf[:, m, :], bk[:, 0:TT], reads=[bkey], writes=[(kbk, m)])
            mx, mkeys = mixed(3, xres, xkey)
            if has_vres:
                bk, bkey = banks.next()
                proj_fm(wl, 0, mx, mkeys, bk, bkey, M=32, col0=288)
                P.copy("act", hv[:], bk[0:32, 0:TT], reads=[bkey], writes=["hv"])
            for sub in range(NS):
                r0 = tile * TT + sub * 128
                vt, vk = vt_r.next()
                vbk = []
                for half in range(2):
                    bk, bkey = banks.next()
                    hs_ = slice(half * 512, (half + 1) * 512)
                    for c in range(8):
                        P.mm(bk[:], mx[:, c, sub * 128:(sub + 1) * 128], wv[:, c, hs_], start=(c == 0), stop=(c == 7),
                             reads=["wv", mkeys[c]], writes=[bkey])
                    vbk.append((bk, bkey))
                if has_vres:
                    P.dma("sp", vf[:], E.vfirst[r0:r0 + 128, :], "ld_vf", reads=[("vfirst", r0)], writes=["vf"])
                    for half in range(2):
                        hs_ = slice(half * 512, (half + 1) * 512)
                        bk, bkey = banks.next()
                        P.mm(bk[:], hv[0:32, sub * 128:(sub + 1) * 128], v2s[0:32, hs_], start=True, stop=False,
                             reads=["hv", "wl"], writes=[bkey])
                        P.mm(bk[:], ones1[0:1, :], rowf[0:1, 1, hs_], start=False, stop=True, reads=["ones1", "rowf"], writes=[bkey])
                        P.act(sgv[:, hs_], bk[:], AF.Sigmoid, reads=[bkey], writes=[("sgv", half)])
                        vb_, vbk_ = vbk[half]
                        P.tt("dve", dd[:, hs_], vf[:, hs_], vb_[:], ALU.subtract, reads=["vf", vbk_], writes=[("dd", half)])
                        P.tt("pool", dd[:, hs_], dd[:, hs_], sgv[:, hs_], ALU.mult, reads=[("dd", half), ("sgv", half)], writes=[("dd", half)])
                        P.tt("dve", vt[:, hs_], dd[:, hs_], vb_[:], ALU.add, reads=[("dd", half), vbk_], writes=[(vk, half)])
                else:
                    for half in range(2):
                        hs_ = slice(half * 512, (half + 1) * 512)
                        vb_, vbk_ = vbk[half]
                        P.copy("act", vt[:, hs_], vb_[:], reads=[vbk_], writes=[(vk, half)])
                    P.dma("sp", E.vfirst[r0:r0 + 128, :], vt[:], "st_vf", reads=[(vk, 0), (vk, 1)], writes=[("vfirst", r0)])
                P.dma("sp", E.rwv[r0:r0 + 128, :], vt[:], "st_v", reads=[(vk, 0), (vk, 1)], writes=[("rwv", r0)])
            mx, mkeys = mixed(4, xres, xkey)
            bk, bkey = banks.next()
            proj_fm(wl, 0, mx, mkeys, bk, bkey, M=64, col0=64)
            P.copy("act", ha[:], bk[0:64, 0:TT], reads=[bkey], writes=["ha"])
            for m in range(8):
                bk, bkey = banks.next()
                P.mm(bk[:, 0:TT], a2s[0:64, m * 128:(m + 1) * 128], ha[0:64, :], start=True, stop=True,
                     reads=["ha", "wl"], writes=[bkey])
                P.act(abuf[:, m, :], bk[:, 0:TT], AF.Sigmoid, bias=par[:, PC_A0 + m:PC_A0 + m + 1],
                      reads=[bkey, "par"], writes=[(abk, m)])
            mx, mkeys = mixed(5, xres, xkey)
            bk, bkey = banks.next()
            proj_fm(wl, 0, mx, mkeys, bk, bkey, M=128, col0=128)
            P.act(hgb[:, 0, :], bk[:, 0:TT], AF.Sigmoid, reads=[bkey], writes=[("hgb", 0)])
            bk, bkey = banks.next()
            proj_fm(wl, 0, mx, mkeys, bk, bkey, M=32, col0=256)
            P.act(hgb[0:32, 1, :], bk[0:32, 0:TT], AF.Sigmoid, reads=[bkey], writes=[("hgb", 1)])
            for m in range(8):
                bk, bkey = banks.next()
                P.mm(bk[:, 0:TT], g2s[:, 0, m * 128:(m + 1) * 128], hgb[:, 0, :], start=True, stop=False,
                     reads=[("hgb", 0), "wl"], writes=[bkey])
                P.mm(bk[:, 0:TT], g2s[0:32, 1, m * 128:(m + 1) * 128], hgb[0:32, 1, :], start=False, stop=True,
                     reads=[("hgb", 1), "wl"], writes=[bkey])
                P.copy("act", gbuf[:, m, :], bk[:, 0:TT], reads=[bkey], writes=[("gbuf", m)])
            def pc(base, m):
                return par[:, base + m:base + m + 1]
            for m in range(8):
                P.ts("dve", kkbuf[:, m, :], kbuf[:, m, :], pc(PC_KK, m), None, ALU.mult, reads=[(kbk, m), "par"], writes=[("kk", m)])
            for m in range(8):
                P.act(sqb8[:, m, :], kkbuf[:, m, :], AF.Square, reads=[("kk", m)], writes=[("sqb", m)])
            for m in range(8):
                P.ts("dve", f8[:, m, :], abuf[:, m, :], pc(PC_KA, m), None, ALU.mult, reads=[(abk, m), "par"], writes=[("f8", m)])
            nbk = []
            for m2 in range(4):
                bk, bkey = banks.next()
                for q in range(2):
                    m = 2 * m2 + q
                    P.mm(bk[:, q * TT:(q + 1) * TT], E.bones[:], sqb8[:, m, :], start=True, stop=True, reads=[("sqb", m), "consts2"], writes=[bkey])
                P.act(sd8[:, 2 * m2:2 * m2 + 2, :], bk[:].rearrange("p (q t) -> p q t", q=2), AF.Sqrt, reads=[bkey], writes=[("sd", m2)])
            for m in range(8):
                P.stt(k2buf[:, m, :], f8[:, m, :], omka[:, m:m + 1], kbuf[:, m, :], ALU.add, ALU.mult,
                      reads=[("f8", m), "omka", (kbk, m)], writes=[("k2", m)])
            for m2 in range(4):
                ms = slice(2 * m2, 2 * m2 + 2)
                P.ts("pool", sd8[:, ms, :], sd8[:, ms, :], 1e-12, None, ALU.max, reads=[("sd", m2)], writes=[("sd", m2)])
                P.recip(sd8[:, ms, :], sd8[:, ms, :], reads=[("sd", m2)], writes=[("sd", m2)])
                P.tt("dve", kkbuf[:, ms, :], kkbuf[:, ms, :], sd8[:, ms, :], ALU.mult,
                     reads=[("kk", 2 * m2), ("kk", 2 * m2 + 1), ("sd", m2)], writes=[("kk", 2 * m2), ("kk", 2 * m2 + 1)])
            for m in range(8):
                P.stt(prbuf[:, m, :], rbuf[:, m, :], pc(PC_RK, m), k2buf[:, m, :], ALU.mult, ALU.mult,
                      reads=[(rbk, m), ("k2", m), "par"], writes=[("pr", m)])
            for m2 in range(4):
                ms = slice(2 * m2, 2 * m2 + 2)
                P.tt("pool", abuf[:, ms, :], abuf[:, ms, :], kkbuf[:, ms, :], ALU.mult,
                     reads=[(abk, 2 * m2), (abk, 2 * m2 + 1), ("kk", 2 * m2), ("kk", 2 * m2 + 1)], writes=[(abk, 2 * m2), (abk, 2 * m2 + 1)])
            for idx, (buf, kn) in enumerate(((rbuf, rbk), (k2buf, "k2"), (kkbuf, "kk"), (abuf, abk))):
                P.dma("sp", E.rwd[idx, :, :, tsl].rearrange("c p t -> p c t"), buf[:], "st_rwd%d" % idx,
                      reads=[(kn, m) for m in range(8)], writes=[("rwd", idx, tile)])
            for idx, (buf, kn) in enumerate(((gbuf, "gbuf"), (prbuf, "pr"))):
                P.dma("sp", E.rwdb[idx, :, :, tsl].rearrange("c p t -> p c t"), buf[:], "st_rwdb%d" % idx,
                      reads=[(kn, m) for m in range(8)], writes=[("rwdb", idx, tile)])
        P.flush()


def phase_rwkv_b(P, E, l):
    nc = E.nc
    j = l // 2
    GN_EPS = 64e-5
    with contextlib.ExitStack() as st:
        load_par(P, E, st, l)
        par = E.par
        wo = st.enter_context(nc.sbuf_tensor(U("wo"), [128, 8, D], BF16))
        load_weight(P, E, wo, E.d_rw_wo[j], "wo", nsplit=2)
        mSU4 = st.enter_context(nc.sbuf_tensor(U("mSU4"), [128, 4, 128], F32))
        mUI4 = st.enter_context(nc.sbuf_tensor(U("mUI4"), [128, 4, 128], F32))
        mSL4 = st.enter_context(nc.sbuf_tensor(U("mSL4"), [128, 4, 128], F32))
        id4 = st.enter_context(nc.sbuf_tensor(U("id4"), [128, 4, 128], F32))
        for q in range(4):
            P.copy("pool", mSU4[:, q, :], E.cst[:, C_SU:C_SU + 128], reads=["consts"], writes=["m4"])
            P.copy("pool", mUI4[:, q, :], E.cst[:, C_UI:C_UI + 128], reads=["consts"], writes=["m4"])
            P.copy("pool", mSL4[:, q, :], E.cst[:, C_SL:C_SL + 128], reads=["consts"], writes=["m4"])
            P.copy("pool", id4[:, q, :], E.cst[:, C_ID:C_ID + 128], reads=["consts"], writes=["m4"])
        bones64 = st.enter_context(nc.sbuf_tensor(U("bones64"), [128, 128], BF16))
        P.ts("dve", bones64[:], E.cst[:, C_BO:C_BO + 128], 1.0 / 64.0, None, ALU.mult, reads=["consts"], writes=["m4"])
        maskUI = E.cst[:, C_UI:C_UI + 128]
        maskSU = E.cst[:, C_SU:C_SU + 128]
        ident = E.cst[:, C_ID:C_ID + 128]
        Pf = st.enter_context(nc.sbuf_tensor(U("Pf"), [128, 512], F32))
        Pb = st.enter_context(nc.sbuf_tensor(U("Pb"), [128, 512], BF16))
        P.memset("pool", Pf[:], 0.0, writes=["Pf"])
        P.memset("pool", Pb[:], 0.0, writes=["Pb"])
        lnb = alloc_ln(st, nc)
        xres_r = Ring(st, nc, "xres", [128, 8, TT], F32, 1)
        zT = st.enter_context(nc.sbuf_tensor(U("zT"), [128, 8, TT], BF16))
        fm_r = [Ring(st, nc, "fm%d" % i, [128, 8, 128], F32, 2) for i in range(4)]
        fb_r = [Ring(st, nc, "fb%d" % i, [128, 8, 128], BF16, 2) for i in range(2)]
        sg_r = Ring(st, nc, "sgl", [128, D], F32, 2)
        vt_r = Ring(st, nc, "vtl", [128, D], F32, 2)
        vb = st.enter_context(nc.sbuf_tensor(U("vb"), [128, D], BF16))
        gam = st.enter_context(nc.sbuf_tensor(U("gam"), [128, 4, 128], F32))
        ginv = st.enter_context(nc.sbuf_tensor(U("ginv"), [128, 4, 128], F32))
        game = st.enter_context(nc.sbuf_tensor(U("game"), [128, 4, 128], F32))
        ghat = st.enter_context(nc.sbuf_tensor(U("ghat"), [128, 4, 128], F32))
        gsm = st.enter_context(nc.sbuf_tensor(U("gsm"), [128, 16], F32))
        Rt = st.enter_context(nc.sbuf_tensor(U("Rt"), [128, 8, 128], BF16))
        At = st.enter_context(nc.sbuf_tensor(U("At"), [128, 8, 128], BF16))
        Bt = st.enter_context(nc.sbuf_tensor(U("Bt"), [128, 8, 128], BF16))
        Kt = st.enter_context(nc.sbuf_tensor(U("Kt"), [128, 8, 128], BF16))
        Atf = st.enter_context(nc.sbuf_tensor(U("Atf"), [128, 4, 128], F32))
        Bhf = st.enter_context(nc.sbuf_tensor(U("Bhf"), [128, 4, 128], F32))
        Khf = st.enter_context(nc.sbuf_tensor(U("Khf"), [128, 4, 128], F32))
        Atm = st.enter_context(nc.sbuf_tensor(U("Atm"), [128, 8, 128], BF16))
        Bhm = st.enter_context(nc.sbuf_tensor(U("Bhm"), [128, 8, 128], BF16))
        Khm = st.enter_context(nc.sbuf_tensor(U("Khm"), [128, 8, 128], BF16))
        LT_r = [Ring(st, nc, "LTs%d" % i, [128, 4, 128], BF16, 2) for i in range(4)]
        L_r = [Ring(st, nc, "Ls%d" % i, [128, 4, 128], BF16, 2) for i in range(4)]
        TT_r = [Ring(st, nc, "TTm%d" % i, [128, 4, 128], BF16, 2) for i in range(4)]
        TTall = st.enter_context(nc.sbuf_tensor(U("TTall"), [128, 16, 128], BF16))
        Lak_r = Ring(st, nc, "LakTs", [128, 4, 128], BF16, 2)
        Y2s = st.enter_context(nc.sbuf_tensor(U("Y2s"), [128, 16, 64], BF16))
        WTs = st.enter_context(nc.sbuf_tensor(U("WTs"), [128, 8, 128], BF16))
        MrbT = st.enter_context(nc.sbuf_tensor(U("MrbT"), [128, 16, 128], BF16))
        MrkT = st.enter_context(nc.sbuf_tensor(U("MrkT"), [128, 16, 128], BF16))
        Us = st.enter_context(nc.sbuf_tensor(U("Us"), [128, 16, 64], BF16))
        Os = st.enter_context(nc.sbuf_tensor(U("Os"), [128, D], F32))
        ob = st.enter_context(nc.sbuf_tensor(U("ob"), [128, 8, 128], BF16))
        osq = st.enter_context(nc.sbuf_tensor(U("osq"), [128, 8, 128], BF16))
        tb = bank_ring(E, [0, 1, 2, 3])
        tb8 = bank_ring(E, [0, 1, 2, 3, 4, 5, 6, 7])
        ybanks = bank_ring(E, [4, 5])
        sbanks = bank_ring(E, [6, 7])
        UB = [(E.PB[4], ("pb", 4)), (E.PB[5], ("pb", 5))]
        OB = [(E.PB[6], ("pb", 6)), (E.PB[7], ("pb", 7))]

        def v4(bank):
            return bank[:].rearrange("p (q t) -> p q t", q=4)

        def issue_loads(ch):
            tile_ = ch // 2
            csl = slice(ch * 128, (ch + 1) * 128)
            fm, fmk = [], []
            for i in range(4):
                t_, k_ = fm_r[i].next()
                P.dma("sp", t_[:], E.rwd[i, :, :, csl].rearrange("c p t -> p c t"), "ld_fm%d_%d" % (i, k_[1]),
                      reads=[("rwd", i, tile_)], writes=[k_])
                fm.append(t_)
                fmk.append(k_)
            fb, fbk = [], []
            for i in range(2):
                t_, k_ = fb_r[i].next()
                P.dma("sp", t_[:], E.rwdb[i, :, :, csl].rearrange("c p t -> p c t"), "ld_fb%d_%d" % (i, k_[1]),
                      reads=[("rwdb", i, tile_)], writes=[k_])
                fb.append(t_)
                fbk.append(k_)
            sg, sgk = sg_r.next()
            P.dma("sp", sg[:], E.rwsg[csl, :], "ld_sg%d" % sgk[1], reads=[("rwsg", ch * 128)], writes=[sgk])
            vt, vtk = vt_r.next()
            P.dma("sp", vt[:], E.rwv[csl, :], "ld_vt%d" % vtk[1], reads=[("rwv", ch * 128)], writes=[vtk])
            return fm, fmk, fb, fbk, sg, sgk, vt, vtk

        xcur = None
        nxt_ld = None
        NCH = int(_os.environ.get("RW_NCH", T // 128))
        for ch in range(NCH):
            tile, sub = ch // 2, ch % 2
            csl = slice(ch * 128, (ch + 1) * 128)
            if sub == 0:
                xres, xkey = xres_r.next()
                P.dma("sp", xres[:], E.xs[:, :, tile * TT:(tile + 1) * TT].rearrange("c p t -> p c t"), "xs_ld" + str(xkey[1]),
                      reads=[("xs", tile)], writes=[xkey])
                xcur = (xres, xkey)
            if ch == 0:
                nxt_ld = issue_loads(0)
            fm, fmk, fb, fbk, sg, sgk, vt, vtk = nxt_ld
            if ch + 1 < NCH:
                nxt_ld = issue_loads(ch + 1)
            P.copy("pool", vb[:], vt[:], reads=[vtk], writes=["vb"])
            r_f, k2_f, kk_f, bv_f = fm
            rk_, k2k_, kkk_, bvk_ = fmk
            g_b, pr_b = fb
            gk_, prk_ = fbk
            for grp in range(2):
                gi, gik = tb.next()
                ge, gek = tb.next()
                for p4 in range(4):
                    c = grp * 4 + p4
                    P.mm(gi[:, p4 * 128:(p4 + 1) * 128], sg[:, c * 128:(c + 1) * 128], maskUI, start=True, stop=True,
                         reads=[sgk, "consts"], writes=[gik])
                for p4 in range(4):
                    c = grp * 4 + p4
                    P.mm(ge[:, p4 * 128:(p4 + 1) * 128], sg[:, c * 128:(c + 1) * 128], maskSU, start=True, stop=True,
                         reads=[sgk, "consts"], writes=[gek])
                P.act(gam[:], v4(gi), AF.Exp, scale=-C0, reads=[gik], writes=["gam"])
                P.act(ginv[:], v4(gi), AF.Exp, scale=C0, reads=[gik], writes=["ginv"])
                P.act(game[:], v4(ge), AF.Exp, scale=-C0, reads=[gek], writes=["game"])
                for p4 in range(4):
                    c = grp * 4 + p4
                    last = gi[:, p4 * 128 + 127:p4 * 128 + 128]
                    P.act(gsm[:, c:c + 1], last, AF.Identity, scale=-C0, reads=[gik], writes=[("nb", c)])
                    P.act(gsm[:, 8 + c:9 + c], last, AF.Exp, scale=-C0, reads=[gik], writes=[("gC", c)])
                    P.act(ghat[:, p4, :], gi[:, p4 * 128:(p4 + 1) * 128], AF.Exp, scale=C0, bias=gsm[:, c:c + 1],
                          reads=[gik, ("nb", c)], writes=[("ghat", p4)])
                cs = slice(grp * 4, grp * 4 + 4)
                P.tt("dve", Rt[:, cs, :], r_f[:, cs, :], gam[:], ALU.mult, reads=[rk_, "gam"], writes=[("Rt", grp)])
                P.stt(Atf[:], kk_f[:, cs, :], -1.0, game[:], ALU.mult, ALU.mult, reads=[kkk_, "game"], writes=["Atf"])
                P.copy("pool", At[:, cs, :], Atf[:], reads=["Atf"], writes=[("At", grp)])
                P.tt("dve", Bt[:, cs, :], bv_f[:, cs, :], ginv[:], ALU.mult, reads=[bvk_, "ginv"], writes=[("Bt", grp)])
                P.tt("pool", Kt[:, cs, :], k2_f[:, cs, :], ginv[:], ALU.mult, reads=[k2k_, "ginv"], writes=[("Kt", grp)])
                P.tt("dve", Bhf[:], bv_f[:, cs, :], ghat[:], ALU.mult, reads=[bvk_] + [("ghat", q) for q in range(4)], writes=["Bhf"])
                P.tt("pool", Khf[:], k2_f[:, cs, :], ghat[:], ALU.mult, reads=[k2k_] + [("ghat", q) for q in range(4)], writes=["Khf"])
                for (src, skey, dst, dname, eng) in ((Atf, "Atf", Atm, "Atm", "act"), (Bhf, "Bhf", Bhm, "Bhm", "dve"), (Khf, "Khf", Khm, "Khm", "act")):
                    bk, bkey = tb.next()
                    for p4 in range(4):
                        P.tr(bk[:, p4 * 128:(p4 + 1) * 128], src[:, p4, :], ident, reads=[skey, "consts"], writes=[bkey])
                    P.copy(eng, dst[:, cs, :], v4(bk), reads=[bkey], writes=[(dname, grp)])
            if _DBG_STOP <= 1:
                continue
            def hop(hg, q):
                par_i, pblk = hg % 2, hg // 2
                c = 4 * pblk + q
                return 2 * c + par_i, c, slice(64 * par_i, 64 * par_i + 64), pblk
            G = [dict() for _ in range(4)]
            for hg in range(4):
                g = G[hg]
                ltb, ltk = tb8.next()
                lb, lk = tb8.next()
                for q in range(4):
                    h, c, rs, g_ = hop(hg, q)
                    P.mm(ltb[:, q * 128:(q + 1) * 128], Bt[rs, c, :], At[rs, c, :], start=True, stop=True,
                         reads=[("Bt", g_), ("At", g_)], writes=[ltk])
                for q in range(4):
                    h, c, rs, g_ = hop(hg, q)
                    P.mm(lb[:, q * 128:(q + 1) * 128], At[rs, c, :], Bt[rs, c, :], start=True, stop=True,
                         reads=[("Bt", g_), ("At", g_)], writes=[lk])
                g["LTs"], g["LTk"] = LT_r[hg].next()
                g["Ls"], g["Lk"] = L_r[hg].next()
                P.tt("dve", g["LTs"][:], v4(ltb), mSU4[:], ALU.mult, reads=[ltk, "m4"], writes=[g["LTk"]])
                P.tt("dve", g["Ls"][:], v4(lb), mSL4[:], ALU.mult, reads=[lk, "m4"], writes=[g["Lk"]])
                g["TTm"], g["TTk"] = TT_r[hg].next()
                P.tt("pool", g["TTm"][:], g["LTs"][:], id4[:], ALU.add, reads=[g["LTk"], "m4"], writes=[g["TTk"]])
            for hg in range(4):
                g = G[hg]
                lab, lak = tb8.next()
                for q in range(4):
                    h, c, rs, g_ = hop(hg, q)
                    P.mm(lab[:, q * 128:(q + 1) * 128], Kt[rs, c, :], At[rs, c, :], start=True, stop=True,
                         reads=[("Kt", g_), ("At", g_)], writes=[lak])
                g["LakTs"], g["Lakk"] = Lak_r.next()
                P.tt("dve", g["LakTs"][:], v4(lab), mSU4[:], ALU.mult, reads=[lak, "m4"], writes=[g["Lakk"]])
                y2b, y2k = tb8.next()
                for q in range(4):
                    h, c, rs, g_ = hop(hg, q)
                    P.mm(y2b[:, q * 64:(q + 1) * 64], g["LakTs"][:, q, :], vb[:, h * 64:(h + 1) * 64], start=True, stop=True,
                         reads=[g["Lakk"], "vb"], writes=[y2k])
                P.copy("act", Y2s[:, 4 * hg:4 * hg + 4, :], y2b[:, 0:256].rearrange("p (q v) -> p q v", q=4), reads=[y2k], writes=[("Y2s", hg)])
                mbb, mbk = tb8.next()
                for q in range(4):
                    h, c, rs, g_ = hop(hg, q)
                    P.mm(mbb[:, q * 128:(q + 1) * 128], Bt[rs, c, :], Rt[rs, c, :], start=True, stop=True,
                         reads=[("Bt", g_), ("Rt", g_)], writes=[mbk])
                P.tt("dve", MrbT[:, 4 * hg:4 * hg + 4, :], v4(mbb), mUI4[:], ALU.mult, reads=[mbk, "m4"], writes=[("MrbT", hg)])
                mkb, mkk = tb8.next()
                for q in range(4):
                    h, c, rs, g_ = hop(hg, q)
                    P.mm(mkb[:, q * 128:(q + 1) * 128], Kt[rs, c, :], Rt[rs, c, :], start=True, stop=True,
                         reads=[("Kt", g_), ("Rt", g_)], writes=[mkk])
                P.tt("dve", MrkT[:, 4 * hg:4 * hg + 4, :], v4(mkb), mUI4[:], ALU.mult, reads=[mkk, "m4"], writes=[("MrkT", hg)])
            n = 2
            while n <= 64:
                for hg in range(4):
                    g = G[hg]
                    g["lnb"], g["lnk"] = tb8.next()
                    for q in range(4):
                        P.mm(g["lnb"][:, q * 128:(q + 1) * 128], g["LTs"][:, q, :], g["Ls"][:, q, :], start=True, stop=True,
                             reads=[g["LTk"], g["Lk"]], writes=[g["lnk"]])
                    if n < 64:
                        g["ltnb"], g["ltnk"] = tb8.next()
                        for q in range(4):
                            P.mm(g["ltnb"][:, q * 128:(q + 1) * 128], g["Ls"][:, q, :], g["LTs"][:, q, :], start=True, stop=True,
                                 reads=[g["LTk"], g["Lk"]], writes=[g["ltnk"]])
                    g["Ls2"], g["Lk2"] = L_r[hg].next()
                    P.copy("act", g["Ls2"][:], v4(g["lnb"]), reads=[g["lnk"]], writes=[g["Lk2"]])
                    if n < 64:
                        g["LTs2"], g["LTk2"] = LT_r[hg].next()
                        P.copy("act", g["LTs2"][:], v4(g["ltnb"]), reads=[g["ltnk"]], writes=[g["LTk2"]])
                for hg in range(4):
                    g = G[hg]
                    pb_, pk_ = tb8.next()
                    for q in range(4):
                        P.mm(pb_[:, q * 128:(q + 1) * 128], g["Ls2"][:, q, :], g["TTm"][:, q, :], start=True, stop=True,
                             reads=[g["Lk2"], g["TTk"]], writes=[pk_])
                    if n < 64:
                        TTm2, TTk2 = TT_r[hg].next()
                        P.tt("dve", TTm2[:], v4(pb_), g["TTm"][:], ALU.add, reads=[pk_, g["TTk"]], writes=[TTk2])
                        g["TTm"], g["TTk"] = TTm2, TTk2
                        g["LTs"], g["LTk"] = g["LTs2"], g["LTk2"]
                    else:
                        P.tt("dve", TTall[:, 4 * hg:4 * hg + 4, :], v4(pb_), g["TTm"][:], ALU.add, reads=[pk_, g["TTk"]], writes=[("TTall", hg)])
                    g["Ls"], g["Lk"] = g["Ls2"], g["Lk2"]
                n *= 2
            for hg in range(4):
                par_i, pblk = hg % 2, hg // 2
                wtb, wtk = tb8.next()
                for q in range(4):
                    h, c, rs, g_ = hop(hg, q)
                    P.mm(wtb[rs, q * 128:(q + 1) * 128], Atm[:, c, rs], TTall[:, 4 * hg + q, :], start=True, stop=True,
                         reads=[("Atm", g_), ("TTall", hg)], writes=[wtk])
                rs_ = slice(64 * par_i, 64 * par_i + 64)
                P.copy("act", WTs[rs_, 4 * pblk:4 * pblk + 4, :], wtb[rs_, :].rearrange("p (q t) -> p q t", q=4), reads=[wtk], writes=[("WTs", hg)])
            if _DBG_STOP <= 2:
                continue
            def slot(h):
                i_, c_ = h % 2, h // 2
                return (2 * (c_ // 4) + i_) * 4 + (c_ % 4)
            for i in range(2):
                rs = slice(64 * i, 64 * i + 64)
                ub, ubk = UB[i]
                for c in range(8):
                    h = 2 * c + i
                    sl_ = slot(h)
                    P.mm(ub[:, c * 64:(c + 1) * 64], TTall[:, sl_, :], Y2s[:, sl_, :], start=True, stop=False,
                         reads=[("TTall", sl_ // 4), ("Y2s", sl_ // 4)], writes=[ubk])
                    P.mm(ub[:, c * 64:(c + 1) * 64], WTs[rs, c, :], Pb[rs, c * 64:(c + 1) * 64], start=False, stop=True,
                         reads=[("WTs", sl_ // 4), "Pb"], writes=[ubk])
            for i in range(2):
                ub, ubk = UB[i]
                P.copy("act", Us[:, 8 * i:8 * i + 8, :], ub[:].rearrange("p (q v) -> p q v", q=8), reads=[ubk], writes=[("Us", i)])
            for i in range(2):
                rs = slice(64 * i, 64 * i + 64)
                obk_, obkk = OB[i]
                for c in range(8):
                    h = 2 * c + i
                    sl_ = slot(h)
                    P.mm(obk_[:, c * 64:(c + 1) * 64], Rt[rs, c, :], Pb[rs, c * 64:(c + 1) * 64], start=True, stop=False,
                         reads=[("Rt", c // 4), "Pb"], writes=[obkk])
                    P.mm(obk_[:, c * 64:(c + 1) * 64], MrkT[:, sl_, :], vb[:, h * 64:(h + 1) * 64], start=False, stop=False,
                         reads=[("MrkT", sl_ // 4), "vb"], writes=[obkk])
                    P.mm(obk_[:, c * 64:(c + 1) * 64], MrbT[:, sl_, :], Us[:, 8 * i + c, :], start=False, stop=True,
                         reads=[("MrbT", sl_ // 4), ("Us", i)], writes=[obkk])
            pnb, pnk = tb.next()
            for h in range(16):
                c, i = h // 2, h % 2
                rs = slice(64 * i, 64 * i + 64)
                P.mm(pnb[rs, c * 64:(c + 1) * 64], Bhm[:, c, rs], Us[:, 8 * i + c, :], start=True, stop=False,
                     reads=[("Bhm", c // 4), ("Us", i)], writes=[pnk])
                P.mm(pnb[rs, c * 64:(c + 1) * 64], Khm[:, c, rs], vb[:, h * 64:(h + 1) * 64], start=False, stop=True,
                     reads=[("Khm", c // 4), "vb"], writes=[pnk])
            for c in range(8):
                P.stt(Pf[:, c * 64:(c + 1) * 64], Pf[:, c * 64:(c + 1) * 64], gsm[:, 8 + c:9 + c], pnb[:, c * 64:(c + 1) * 64],
                      ALU.mult, ALU.add, reads=["Pf", ("gC", c), pnk], writes=["Pf"])
            P.copy("pool", Pb[:], Pf[:], reads=["Pf"], writes=["Pb"])
            if _DBG_STOP <= 3:
                continue
            for i in range(2):
                obk_, obkk = OB[i]
                P.copy("act", Os[:].rearrange("p (c i v) -> p c i v", c=8, i=2)[:, :, i, :],
                       obk_[:].rearrange("p (c v) -> p c v", c=8), reads=[obkk], writes=[("Os", i)])
            for grp in range(2):
                cs = slice(grp * 4, grp * 4 + 4)
                otb, otk = tb.next()
                for p4 in range(4):
                    c = grp * 4 + p4
                    P.tr(otb[:, p4 * 128:(p4 + 1) * 128], Os[:, c * 128:(c + 1) * 128], ident, reads=[("Os", 0), ("Os", 1), "consts"], writes=[otk])
                P.act(ob[:, cs, :], v4(otb), AF.Identity, reads=[otk], writes=[("ob", grp)])
                P.act(osq[:, cs, :], v4(otb), AF.Square, reads=[otk], writes=[("osq", grp)])
                P.copy("act", gam[:], v4(otb), reads=[otk], writes=["gam"])
                vtb, vtbk = tb.next()
                for p4 in range(4):
                    c = grp * 4 + p4
                    P.tr(vtb[:, p4 * 128:(p4 + 1) * 128], vt[:, c * 128:(c + 1) * 128], ident, reads=[vtk, "consts"], writes=[vtbk])
                P.copy("act", ginv[:], v4(vtb), reads=[vtbk], writes=["ginv"])
                mnb, mnk = tb.next()
                for p4 in range(4):
                    c = grp * 4 + p4
                    P.mm(mnb[:, p4 * 128:(p4 + 1) * 128], bones64[:], ob[:, c, :], start=True, stop=True, reads=[("ob", grp), "m4"], writes=[mnk])
                P.copy("act", game[:], v4(mnb), reads=[mnk], writes=["game"])
                msb, msk = tb.next()
                for p4 in range(4):
                    c = grp * 4 + p4
                    P.mm(msb[:, p4 * 128:(p4 + 1) * 128], bones64[:], osq[:, c, :], start=True, stop=True, reads=[("osq", grp), "m4"], writes=[msk])
                P.tt("pool", Atf[:], game[:], game[:], ALU.mult, reads=["game"], writes=["Atf"])
                P.stt(Bhf[:], v4(msb), GN_EPS, Atf[:], ALU.add, ALU.subtract, reads=[msk, "Atf"], writes=["Bhf"])
                P.act(Bhf[:], Bhf[:], AF.Sqrt, reads=["Bhf"], writes=["Bhf"])
                P.recip(Bhf[:], Bhf[:], reads=["Bhf"], writes=["Bhf"])
                P.tt("pool", gam[:], gam[:], game[:], ALU.subtract, reads=["gam", "game"], writes=["gam"])
                P.tt("dve", gam[:], gam[:], Bhf[:], ALU.mult, reads=["gam", "Bhf"], writes=["gam"])
                for p4 in range(4):
                    c = grp * 4 + p4
                    P.act(ghat[:, p4, :], gam[:, p4, :], AF.Identity, scale=par[:, PC_GNG + c:PC_GNG + c + 1],
                          bias=par[:, PC_GNB + c:PC_GNB + c + 1], reads=["gam", "par"], writes=[("ghat", p4)])
                bnb, bnk = tb.next()
                for p4 in range(4):
                    c = grp * 4 + p4
                    P.mm(bnb[:, p4 * 128:(p4 + 1) * 128], E.bones[:], pr_b[:, c, :], start=True, stop=True, reads=[prk_, "consts2"], writes=[bnk])
                P.tt("dve", Khf[:], v4(bnb), ginv[:], ALU.mult, reads=[bnk, "ginv"], writes=["Khf"])
                P.tt("pool", Khf[:], Khf[:], ghat[:], ALU.add, reads=["Khf"] + [("ghat", q) for q in range(4)], writes=["Khf"])
                P.tt("dve", zT[:, cs, sub * 128:(sub + 1) * 128], Khf[:], g_b[:, cs, :], ALU.mult, reads=["Khf", gk_], writes=[("zT", grp, sub)])
            if sub == 1:
                xres, xkey = xcur

                def emit_y(m, out_ap, okey):
                    for c in range(8):
                        P.mm(out_ap, wo[:, c, m * 128:(m + 1) * 128], zT[:, c, :], start=(c == 0), stop=(c == 7),
                             reads=["wo"] + [("zT", c // 4, s_) for s_ in range(2)], writes=[okey])
                ln_tail(P, E, lnb, xres, xkey, PC_LN1G, PC_LN1B, emit_y, tile, ybanks, sbanks)
        P.flush()


def build(plan, debug_xs=False):
    nc = bass.Bass("TRN2", target_bir_lowering=False)
    E = Env()
    E.nc = nc

    def din(name, shape):
        return nc.dram_tensor(name, list(shape), F32, kind="ExternalInput").ap()
    E.d_x = din("x", [T, D])
    E.d_cst = din("cst", [128, 640])
    E.d_ropec = din("ropec", [128, T])
    E.d_ropes = din("ropes", [128, T])
    E.d_par = din("par", [4, 128, NPAR])
    E.d_rowp = din("rowp", [2, 2, D])
    E.d_lamp = din("lamp", [2, 128, 256])
    E.d_gsub = din("gsub", [2, 128, 128])
    E.d_ev_win = din("ev_win", [2, D, 4096])
    E.d_ev_wout = din("ev_wout", [2, D, D])
    for n in ("rw_wr", "rw_wk", "rw_wv", "rw_wo"):
        setattr(E, "d_" + n, din(n, [2, D, D]))
    E.d_rw_w1 = din("rw_w1", [2, D, 64])
    E.d_rw_w2 = din("rw_w2", [2, 64, D])
    E.d_rw_a1 = din("rw_a1", [2, D, 64])
    E.d_rw_a2 = din("rw_a2", [2, 64, D])
    E.d_rw_g1 = din("rw_g1", [2, D, 160])
    E.d_rw_g2 = din("rw_g2", [2, 160, D])
    E.d_rw_v1 = din("rw_v1", [1, D, 32])
    E.d_rw_v2 = din("rw_v2", [1, 32, D])
    E.d_ffn_up = din("ffn_up", [4, D, 2 * DFF])
    E.d_ffn_dn = din("ffn_dn", [4, DFF, D])
    E.d_out = nc.dram_tensor("out", [T, D], F32, kind="ExternalOutput").ap()
    E.xs = nc.dram_tensor("xs_scratch", [8, 128, T], F32).ap()
    E.vfirst = nc.dram_tensor("vfirst_scratch", [T, D], F32).ap()
    E.rwd = nc.dram_tensor("rwd_scratch", [4, 8, 128, T], F32).ap()
    E.rwdb = nc.dram_tensor("rwdb_scratch", [2, 8, 128, T], BF16).ap()
    E.rwsg = nc.dram_tensor("rwsg_scratch", [T, D], F32).ap()
    E.rwv = nc.dram_tensor("rwv_scratch", [T, D], F32).ap()
    with contextlib.ExitStack() as st:
        P = Prog(nc, st)
        E.PB = [st.enter_context(nc.psum_tensor("pb%d" % i, [128, 512], F32)) for i in range(8)]
        E.cst = st.enter_context(nc.sbuf_tensor(U("cst_sb"), [128, 640], F32))
        E.onesD = st.enter_context(nc.sbuf_tensor(U("onesD"), [128, 128], BF16))
        E.identb = st.enter_context(nc.sbuf_tensor(U("identb"), [128, 128], BF16))
        E.bones = st.enter_context(nc.sbuf_tensor(U("bones"), [128, 128], BF16))
        P.dma("sp", E.cst[:], E.d_cst[:, :], "cst", writes=["consts"])
        P.memset("pool", E.onesD[:], 1.0 / D, writes=["consts"])
        P.copy("dve", E.identb[:], E.cst[:, C_ID:C_ID + 128], reads=["consts"], writes=["consts2"])
        P.copy("dve", E.bones[:], E.cst[:, C_BO:C_BO + 128], reads=["consts"], writes=["consts2"])
        P.flush()
        for ph in plan:
            if ph[0] == "in":
                phase_in(P, E)
            elif ph[0] == "out":
                phase_out(P, E)
            elif ph[0] == "ffn":
                phase_ffn(P, E, ph[1])
            elif ph[0] == "even":
                phase_even(P, E, ph[1])
            elif ph[0] == "rwkv":
                phase_rwkv_a(P, E, ph[1])
                if len(ph) < 3:
                    phase_rwkv_b(P, E, ph[1])
        E.n_instr = P.n_instr
    return nc, E


FULL_PLAN = [("in",), ("even", 0), ("ffn", 0), ("rwkv", 1), ("ffn", 1), ("even", 2), ("ffn", 2), ("rwkv", 3), ("ffn", 3), ("out",)]


def fm(v):
    return np.ascontiguousarray(np.asarray(v, np.float32).reshape(-1, 128).T)


def host_prepare(inp):
    sh = {}
    idx = np.arange(128)
    ident = (idx[:, None] == idx[None, :]).astype(np.float32)
    su = (idx[:, None] < idx[None, :]).astype(np.float32)
    ui = (idx[:, None] <= idx[None, :]).astype(np.float32)
    sl = (idx[:, None] > idx[None, :]).astype(np.float32)
    bo = ((idx[:, None] // 64) == (idx[None, :] // 64)).astype(np.float32)
    sh["cst"] = np.ascontiguousarray(np.concatenate([ident, su, ui, sl, bo], axis=1))
    inv = (1.0 / (np.float32(10000.0) ** (np.arange(0, 64, 2, dtype=np.float32) / np.float32(64)))).astype(np.float32)
    ang = (np.arange(T, dtype=np.float32)[:, None] * inv[None, :]).astype(np.float32)
    cos = np.cos(ang).astype(np.float32).T
    sin = np.sin(ang).astype(np.float32).T
    cos64 = np.concatenate([cos, cos], 0)
    sin64 = np.concatenate([-sin, sin], 0)
    sh["ropec"] = np.ascontiguousarray(np.concatenate([cos64, cos64], 0))
    sh["ropes"] = np.ascontiguousarray(np.concatenate([sin64, sin64], 0))
    par = np.zeros((4, 128, NPAR), np.float32)
    for l in range(4):
        par[l, :, PC_LN1G:PC_LN1G + 8] = fm(inp["ln1_g"][l])
        par[l, :, PC_LN1B:PC_LN1B + 8] = fm(inp["ln1_b"][l])
        par[l, :, PC_LN2G:PC_LN2G + 8] = fm(inp["ln2_g"][l])
        par[l, :, PC_LN2B:PC_LN2B + 8] = fm(inp["ln2_b"][l])
        for k in range(3):
            par[l, :, PC_FCW + 44 * k:PC_FCW + 44 * k + 44] = fm(inp["ffn_conv_w"][l, k])
        par[l, :, PC_FCB:PC_FCB + 44] = fm(inp["ffn_conv_b"][l])
        if l % 2 == 0:
            i = l // 2
            for k in range(3):
                par[l, :, PC_ECW + 4 * k:PC_ECW + 4 * k + 4] = fm(inp["ev_conv_w"][i, k])
        else:
            j = l // 2
            for q in range(6):
                par[l, :, PC_MIX + 8 * q:PC_MIX + 8 * q + 8] = fm(inp["rw_mix"][j, q])
            par[l, :, PC_W0:PC_W0 + 8] = fm(inp["rw_w0"][j])
            par[l, :, PC_A0:PC_A0 + 8] = fm(inp["rw_a0"][j])
            par[l, :, PC_KK:PC_KK + 8] = fm(inp["rw_k_k"][j])
            par[l, :, PC_KA:PC_KA + 8] = fm(inp["rw_k_a"][j])
            par[l, :, PC_RK:PC_RK + 8] = fm(inp["rw_r_k"][j].reshape(-1))
            par[l, :, PC_GNG:PC_GNG + 8] = fm(inp["rw_gn_g"][j])
            par[l, :, PC_GNB:PC_GNB + 8] = fm(inp["rw_gn_b"][j])
    sh["par"] = par
    rowp = np.zeros((2, 2, D), np.float32)
    rowp[:, 0, :] = inp["rw_w0"]
    rowp[1, 1, :] = inp["rw_v0"][0]
    sh["rowp"] = rowp
    lamp = np.stack([np.concatenate([inp["ev_lam_q1"][i], inp["ev_lam_k1"][i], inp["ev_lam_q2"][i], inp["ev_lam_k2"][i]])
                     for i in range(2)])
    sh["lamp"] = np.ascontiguousarray(np.broadcast_to(lamp[:, None, :], (2, 128, 256))).astype(np.float32)
    sh["gsub"] = np.ascontiguousarray(np.broadcast_to(inp["ev_subln_g"][:, None, :], (2, 128, 128))).astype(np.float32)
    w = np.asarray(inp["ev_w_in"], np.float32)
    perm = np.concatenate([np.concatenate([np.arange(m * 64 + 32, m * 64 + 64), np.arange(m * 64, m * 64 + 32)]) for m in range(8)])
    sh["ev_win"] = np.ascontiguousarray(np.concatenate([w, w[:, :, 1536 + perm], w[:, :, 2048 + perm]], axis=2))
    sh["ev_wout"] = inp["ev_w_out"]
    sh["rw_wr"], sh["rw_wk"], sh["rw_wv"], sh["rw_wo"] = inp["rw_w_r"], inp["rw_w_k"], inp["rw_w_v"], inp["rw_w_o"]
    for n in ("rw_w1", "rw_w2", "rw_a1", "rw_a2", "rw_g1", "rw_g2", "rw_v1", "rw_v2"):
        sh[n] = inp[n]
    sh["ffn_up"] = inp["ffn_w_up"]
    sh["ffn_dn"] = inp["ffn_w_down"]
    return {k: np.ascontiguousarray(np.asarray(v, np.float32)) for k, v in sh.items()}


_CACHE = {}


def run_plan(inputs, plan, x_override=None, n_cores=8):
    key = tuple(plan)
    if key not in _CACHE:
        _CACHE[key] = build(plan)
    nc, E = _CACHE[key]
    shared = host_prepare(inputs)
    x = np.asarray(inputs["x"] if x_override is None else x_override, np.float32)
    in_maps = []
    for b in range(n_cores):
        m = dict(shared)
        m["x"] = np.ascontiguousarray(x[b])
        in_maps.append(m)
    res = run_bass_kernel_spmd(nc, in_maps, core_ids=list(range(n_cores)))
    return np.stack([r["out"] for r in res.results], axis=0)


def kernel(**inputs):
    return run_plan(inputs, FULL_PLAN).astype(np.float32)
```

```python
import contextlib
import math
import numpy as np
import concourse.bass as bass
import concourse.mybir as mybir
from concourse.bass_utils import run_bass_kernel_spmd

F32 = mybir.dt.float32
BF16 = mybir.dt.bfloat16
AF = mybir.ActivationFunctionType
ALU = mybir.AluOpType
AX = mybir.AxisListType

SEM_LIMIT = 30000
T = 4096
D = 1024
TT = 256
NT = T // TT
DFF = 2816
NFC = DFF // 128
ALPHA = float(8 ** 0.25)
LN_EPS = 1e-5
C0 = float(math.exp(-0.5))
NPAR = 320


import os as _os
_DBG_STOP = int(_os.environ.get("RWB_STOP", "9"))
_DBG_SUB = int(_os.environ.get("RWB_SUB", "9"))
_DBG_NMAX = int(_os.environ.get("RWB_NMAX", "64"))


class _Op:
    __slots__ = ("eng", "fn", "deps", "sig", "is_dma", "dkey", "sigval", "idx")


class Prog:
    ENGS = ("pe", "act", "dve", "pool", "sp")
    N_EPOCH = 4
    N_DMA_SEMS = 60

    def __init__(self, nc, stack):
        self.nc = nc
        self.ops = []
        self.last_w = {}
        self.readers = {}
        self.sems = {}
        for e in self.ENGS:
            for i in range(self.N_EPOCH):
                self.sems[(e, i)] = stack.enter_context(nc.semaphore("s_%s_%d" % (e, i)))
        self.dma_pool = [(stack.enter_context(nc.semaphore("s_dma_%d" % i)), 0) for i in range(self.N_DMA_SEMS)]
        self.eng_cnt = {e: 0 for e in self.ENGS}
        self.dma_cnt = {}
        self.waited = {e: {} for e in self.ENGS}
        self.emitted = 0
        self.n_instr = 0

    def add(self, eng, fn, reads=(), writes=(), dkey=None):
        op = _Op()
        op.eng = eng
        op.fn = fn
        op.is_dma = dkey is not None
        op.dkey = dkey
        op.sig = False
        op.sigval = None
        op.idx = len(self.ops)
        deps = {}

        def adddep(d):
            if d is None or d.idx < self.emitted:
                return
            if (not d.is_dma) and d.eng == "pe" and eng == "pe" and not op.is_dma:
                return
            k = ("dma", d.dkey) if d.is_dma else d.eng
            o = deps.get(k)
            if o is None or o.idx < d.idx:
                deps[k] = d
        for k in reads:
            adddep(self.last_w.get(k))
        for k in writes:
            adddep(self.last_w.get(k))
            for r in self.readers.get(k, {}).values():
                adddep(r)
        for k in writes:
            self.last_w[k] = op
            self.readers[k] = {}
        for k in reads:
            rk = ("dma", dkey) if op.is_dma else eng
            self.readers.setdefault(k, {})[rk] = op
        op.deps = list(deps.values())
        for d in op.deps:
            d.sig = True
        self.ops.append(op)
        return op

    def mm(self, out, lhsT, rhs, start=True, stop=True, reads=(), writes=()):
        return self.add("pe", lambda e: e.matmul(out, lhsT, rhs, start=start, stop=stop), reads, writes)

    def tr(self, out, in_, ident, reads=(), writes=()):
        return self.add("pe", lambda e: e.transpose(out, in_, ident), reads, writes)

    def act(self, out, in_, func, bias=None, scale=None, accum_out=None, reads=(), writes=()):
        kw = {}
        if bias is not None:
            kw["bias"] = bias
        if scale is not None:
            kw["scale"] = scale
        if accum_out is not None:
            kw["accum_out"] = accum_out
        return self.add("act", lambda e: e.activation(out, in_, func, **kw), reads, writes)

    def tt(self, eng, out, in0, in1, op, reads=(), writes=()):
        return self.add(eng, lambda e: e.tensor_tensor(out, in0, in1, op), reads, writes)

    def ts(self, eng, out, in0, s1, s2, op0, op1=None, reads=(), writes=()):
        if op1 is None:
            return self.add(eng, lambda e: e.tensor_scalar(out, in0, s1, None, op0), reads, writes)
        return self.add(eng, lambda e: e.tensor_scalar(out, in0, s1, s2, op0, op1), reads, writes)

    def stt(self, out, in0, scalar, in1, op0, op1, reads=(), writes=()):
        return self.add("dve", lambda e: e.scalar_tensor_tensor(out, in0, scalar, in1, op0, op1), reads, writes)

    def copy(self, eng, out, in_, reads=(), writes=()):
        if eng == "act":
            return self.add(eng, lambda e: e.copy(out, in_), reads, writes)
        return self.add(eng, lambda e: e.tensor_copy(out, in_), reads, writes)

    def recip(self, out, in_, reads=(), writes=()):
        return self.add("dve", lambda e: e.reciprocal(out, in_), reads, writes)

    def memset(self, eng, ap, val, writes=()):
        return self.add(eng, lambda e: e.memset(ap, val), (), writes)

    def dma(self, q, out, in_, dkey, reads=(), writes=(), **kw):
        return self.add(q, lambda e: e.dma_start(out, in_, **kw), reads, writes, dkey=dkey)

    def flush(self):
        nc = self.nc
        ops = self.ops[self.emitted:]
        self.emitted = len(self.ops)
        if not ops:
            return
        last = {}
        for op in ops:
            last[op.eng] = op
        for e, op in last.items():
            op.sig = True
        eng_cnt = self.eng_cnt
        dma_cnt = self.dma_cnt
        for op in ops:
            if op.is_dma:
                c = dma_cnt.get(op.dkey, 0) + 16
                dma_cnt[op.dkey] = c
                sn = ("dma", op.dkey)
                if sn not in self.sems:
                    self.dma_pool.sort(key=lambda t: -t[1])
                    sem_, base_ = self.dma_pool.pop()
                    self.sems[sn] = sem_
                    c = base_ + 16
                    dma_cnt[op.dkey] = c
                op.sigval = (sn, c)
            elif op.sig:
                c = eng_cnt[op.eng] + 1
                eng_cnt[op.eng] = c
                op.sigval = ((op.eng, (c - 1) // SEM_LIMIT), (c - 1) % SEM_LIMIT + 1)
        sems = self.sems
        per_eng = {e: [] for e in self.ENGS}
        for op in ops:
            per_eng[op.eng].append(op)
        finals = {}
        for e, op in last.items():
            if not op.is_dma:
                finals[op.sigval[0]] = max(finals.get(op.sigval[0], 0), op.sigval[1])
        for dk, c in dma_cnt.items():
            finals[("dma", dk)] = c
        N_EPOCH = self.N_EPOCH
        with nc.Block() as block:
            def make(ename):
                eops = per_eng[ename]
                waited = self.waited[ename]

                def do_wait(e, sn, v):
                    if waited.get(sn, 0) >= v:
                        return
                    if sn[0] != "dma":
                        if any(waited.get((sn[0], j), 0) > 0 for j in range(sn[1] + 1, N_EPOCH)):
                            return
                    e.wait_ge(sems[sn], v)
                    waited[sn] = v
                    self.n_instr += 1

                def body(e):
                    for op in eops:
                        need = {}
                        for d in op.deps:
                            sn, v = d.sigval
                            if need.get(sn, 0) < v:
                                need[sn] = v
                        for sn, v in need.items():
                            do_wait(e, sn, v)
                        ins = op.fn(e)
                        self.n_instr += 1
                        if op.is_dma:
                            ins.then_inc(sems[op.sigval[0]], 16)
                        elif op.sig:
                            ins.then_inc(sems[op.sigval[0]], 1)
                    for sn, v in finals.items():
                        do_wait(e, sn, v)
                return body

            block.tensor(make("pe"))
            block.scalar(make("act"))
            block.vector(make("dve"))
            block.gpsimd(make("pool"))
            block.sync(make("sp"))
        for op in ops:
            op.fn = None
        for dk in list(dma_cnt.keys()):
            sn = ("dma", dk)
            self.dma_pool.append((self.sems.pop(sn), dma_cnt.pop(dk)))
            for e in self.ENGS:
                self.waited[e].pop(sn, None)


_UID = [0]


def U(name):
    _UID[0] += 1
    return "sb%d_%s" % (_UID[0], name)


class Ring:
    def __init__(self, st, nc, name, shape, dtype, n):
        self.t = [st.enter_context(nc.sbuf_tensor(U("%s_%d" % (name, i)), shape, dtype)) for i in range(n)]
        self.i = 0
        self.name = name
        self.n = n

    def next(self):
        j = self.i % self.n
        self.i += 1
        return self.t[j], (self.name, j)


class Env:
    pass


C_ID, C_SU, C_UI, C_SL, C_BO = 0, 128, 256, 384, 512

PC_LN1G, PC_LN1B, PC_LN2G, PC_LN2B = 0, 8, 16, 24
PC_FCW, PC_FCB = 32, 164
PC_ECW = 208
PC_MIX, PC_W0, PC_A0, PC_KK, PC_KA, PC_RK, PC_GNG, PC_GNB = 208, 256, 264, 272, 280, 288, 296, 304


def bank_ring(E, idxs):
    r = Env()
    r.idxs = list(idxs)
    r.i = 0

    def nxt():
        j = r.idxs[r.i % len(r.idxs)]
        r.i += 1
        return E.PB[j], ("pb", j)
    r.next = nxt
    return r


def load_weight(P, E, dst, src_ap, key, nsplit=1):
    C = dst.shape[1]
    step = max(1, C // nsplit)
    for c0 in range(0, C, step):
        c1 = min(C, c0 + step)
        P.dma("pool", dst[:, c0:c1, :], src_ap[c0 * 128:c1 * 128, :].rearrange("(c p) n -> p c n", p=128),
              "w_" + key, writes=[key], max_dma_last_dim=4096)


def ln_tail(P, E, st_bufs, xres, xres_key, gcol, bcol, emit_y, tile, ybanks, sbanks):
    nc = E.nc
    s, sb, sq = st_bufs["s"], st_bufs["sb"], st_bufs["sq"]
    xo = xres
    par = E.par
    mb, mkey = sbanks.next()
    qb, qkey = sbanks.next()
    pend = []

    def stats(m, sbm, sbk, sqm, sqk):
        P.mm(mb[:, 0:TT], E.onesD[:], sbm[:], start=(m == 0), stop=(m == 7), reads=[sbk, "consts"], writes=[mkey])
        P.mm(qb[:, 0:TT], E.onesD[:], sqm[:], start=(m == 0), stop=(m == 7), reads=[sqk, "consts"], writes=[qkey])
    for m in range(8):
        bk, bkey = ybanks.next()
        emit_y(m, bk[:, 0:TT], bkey)
        P.stt(s[:, m, :], xres[:, m, :], ALPHA, bk[:, 0:TT], ALU.mult, ALU.add,
              reads=[xres_key, bkey], writes=[("ln_s", m)])
        sbm, sbk = sb.next()
        sqm, sqk = sq.next()
        P.act(sbm[:], s[:, m, :], AF.Identity, reads=[("ln_s", m)], writes=[sbk])
        P.act(sqm[:], s[:, m, :], AF.Square, reads=[("ln_s", m)], writes=[sqk])
        if pend:
            stats(*pend.pop())
        pend.append((m, sbm, sbk, sqm, sqk))
    stats(*pend.pop())
    mean, m2, rstd = st_bufs["mean"], st_bufs["m2"], st_bufs["rstd"]
    P.copy("act", mean[:], mb[:, 0:TT], reads=[mkey], writes=["ln_mean"])
    P.tt("pool", m2[:], mean[:], mean[:], ALU.mult, reads=["ln_mean"], writes=["ln_m2"])
    P.stt(rstd[:], qb[:, 0:TT], LN_EPS, m2[:], ALU.add, ALU.subtract, reads=[qkey, "ln_m2"], writes=["ln_rstd"])
    P.act(rstd[:], rstd[:], AF.Sqrt, reads=["ln_rstd"], writes=["ln_rstd"])
    P.recip(rstd[:], rstd[:], reads=["ln_rstd"], writes=["ln_rstd"])
    for m in range(8):
        P.tt("pool", s[:, m, :], s[:, m, :], mean[:], ALU.subtract, reads=[("ln_s", m), "ln_mean"], writes=[("ln_s", m)])
        P.tt("dve", s[:, m, :], s[:, m, :], rstd[:], ALU.mult, reads=[("ln_s", m), "ln_rstd"], writes=[("ln_s", m)])
        P.act(xo[:, m, :], s[:, m, :], AF.Identity, scale=par[:, gcol + m:gcol + m + 1], bias=par[:, bcol + m:bcol + m + 1],
              reads=[("ln_s", m), "par"], writes=[xres_key])
    P.dma("sp", E.xs[:, :, tile * TT:(tile + 1) * TT].rearrange("c p t -> p c t"), xo[:], "xs_st",
          reads=[xres_key], writes=[("xs", tile)])


def alloc_ln(st, nc):
    b = {}
    b["s"] = st.enter_context(nc.sbuf_tensor(U("ln_s"), [128, 8, TT], F32))
    b["sb"] = Ring(st, nc, "ln_sb", [128, TT], BF16, 3)
    b["sq"] = Ring(st, nc, "ln_sq", [128, TT], BF16, 3)
    b["mean"] = st.enter_context(nc.sbuf_tensor(U("ln_mean"), [128, TT], F32))
    b["m2"] = st.enter_context(nc.sbuf_tensor(U("ln_m2"), [128, TT], F32))
    b["rstd"] = st.enter_context(nc.sbuf_tensor(U("ln_rstd"), [128, TT], F32))
    return b


def load_par(P, E, st, l):
    nc = E.nc
    par = st.enter_context(nc.sbuf_tensor(U("par_sb"), [128, NPAR], F32))
    P.dma("sp", par[:], E.d_par[l], "par", writes=["par"])
    E.par = par


def load_xtile(P, E, xres_ring, xb_ring, tile):
    xres, xkey = xres_ring.next()
    P.dma("sp", xres[:], E.xs[:, :, tile * TT:(tile + 1) * TT].rearrange("c p t -> p c t"), "xs_ld" + str(xkey[1]),
          reads=[("xs", tile)], writes=[xkey])
    xb, xbkey = xb_ring.next()
    P.copy("pool", xb[:], xres[:], reads=[xkey], writes=[xbkey])
    return xres, xkey, xb, xbkey


def phase_in(P, E):
    nc = E.nc
    GB = 4
    with contextlib.ExitStack() as st:
        xt_r = Ring(st, nc, "pi_xt", [128, D], F32, 3)
        xo_r = Ring(st, nc, "pi_xo", [128, 8, GB * 128], F32, 2)
        banks = bank_ring(E, [0, 1, 2, 3, 4, 5, 6, 7])
        for grp in range(T // (128 * GB)):
            xo, ok = xo_r.next()
            for bi in range(GB):
                blk = grp * GB + bi
                xt, xk = xt_r.next()
                P.dma("sp", xt[:], E.d_x[blk * 128:(blk + 1) * 128, :], "pi_ld" + str(xk[1]), writes=[xk])
                for half in range(2):
                    bk, bkey = banks.next()
                    for j in range(4):
                        c = half * 4 + j
                        P.tr(bk[:, j * 128:(j + 1) * 128], xt[:, c * 128:(c + 1) * 128], E.cst[:, C_ID:C_ID + 128],
                             reads=[xk, "consts"], writes=[bkey])
                    eng = "act" if half == 0 else "dve"
                    P.copy(eng, xo[:, half * 4:half * 4 + 4, bi * 128:(bi + 1) * 128], bk[:].rearrange("p (j t) -> p j t", j=4),
                           reads=[bkey], writes=[(ok, half, bi)])
            P.dma("sp", E.xs[:, :, grp * GB * 128:(grp + 1) * GB * 128].rearrange("c p t -> p c t"), xo[:], "pi_st" + str(ok[1]),
                  reads=[(ok, h_, b_) for h_ in range(2) for b_ in range(GB)], writes=[("xs", 2 * grp), ("xs", 2 * grp + 1)])
        P.flush()


def phase_out(P, E):
    nc = E.nc
    GB = 4
    with contextlib.ExitStack() as st:
        xi_r = Ring(st, nc, "po_xi", [128, 8, GB * 128], F32, 2)
        xo_r = Ring(st, nc, "po_xo", [128, D], F32, 3)
        banks = bank_ring(E, [0, 1, 2, 3, 4, 5, 6, 7])
        for grp in range(T // (128 * GB)):
            xi, ik = xi_r.next()
            P.dma("sp", xi[:], E.xs[:, :, grp * GB * 128:(grp + 1) * GB * 128].rearrange("c p t -> p c t"), "po_ld" + str(ik[1]),
                  reads=[("xs", 2 * grp), ("xs", 2 * grp + 1)], writes=[ik])
            for bi in range(GB):
                blk = grp * GB + bi
                xo, ok = xo_r.next()
                for half in range(2):
                    bk, bkey = banks.next()
                    for j in range(4):
                        c = half * 4 + j
                        P.tr(bk[:, j * 128:(j + 1) * 128], xi[:, c, bi * 128:(bi + 1) * 128], E.cst[:, C_ID:C_ID + 128],
                             reads=[ik, "consts"], writes=[bkey])
                    eng = "act" if half == 0 else "dve"
                    P.copy(eng, xo[:, half * 512:(half + 1) * 512], bk[:], reads=[bkey], writes=[(ok, half)])
                P.dma("sp", E.d_out[blk * 128:(blk + 1) * 128, :], xo[:], "po_st" + str(ok[1]),
                      reads=[(ok, 0), (ok, 1)], writes=[("out", blk)])
        P.flush()


def phase_ffn(P, E, l):
    nc = E.nc
    with contextlib.ExitStack() as st:
        load_par(P, E, st, l)
        par = E.par
        wup = st.enter_context(nc.sbuf_tensor(U("wup"), [128, 8, 2 * DFF], BF16))
        wdn = st.enter_context(nc.sbuf_tensor(U("wdn"), [128, NFC, D], BF16))
        GW = 11 * 128
        for gq in (0, 2, 1, 3):
            for c in range(8):
                P.dma("pool", wup[:, c, gq * GW:(gq + 1) * GW], E.d_ffn_up[l, c * 128:(c + 1) * 128, gq * GW:(gq + 1) * GW],
                      "w_up%d" % gq, writes=[("wup", gq)], max_dma_last_dim=4096)
        load_weight(P, E, wdn, E.d_ffn_dn[l], "wdn", nsplit=2)
        lnb = alloc_ln(st, nc)
        xres_r = Ring(st, nc, "xres", [128, 8, TT], F32, 2)
        xb_r = Ring(st, nc, "xb", [128, 8, TT], BF16, 1)
        gT = st.enter_context(nc.sbuf_tensor(U("gT"), [128, NFC, TT], BF16))
        carry = st.enter_context(nc.sbuf_tensor(U("carry"), [128, 2 * NFC, 2], F32))
        hs_r = Ring(st, nc, "hs", [128, TT + 2], F32, 4)
        cv_r = Ring(st, nc, "cv", [128, TT], F32, 4)
        sg_r = Ring(st, nc, "sg", [128, TT], F32, 2)
        P.memset("pool", carry[:], 0.0, writes=[("carry", ch) for ch in range(2 * NFC)])
        hbanks = bank_ring(E, [0, 1, 2, 3])
        ybanks = bank_ring(E, [4, 5])
        sbanks = bank_ring(E, [6, 7])
        nxt = load_xtile(P, E, xres_r, xb_r, 0)
        for tile in range(NT):
            xres, xkey, xb, xbkey = nxt
            cvs = {}
            for j in range(NFC):
                for which in range(2):
                    ch = j + which * NFC
                    bk, bkey = hbanks.next()
                    for c in range(8):
                        P.mm(bk[:, 0:TT], wup[:, c, ch * 128:(ch + 1) * 128], xb[:, c, :], start=(c == 0), stop=(c == 7),
                             reads=[("wup", ch // 11), xbkey], writes=[bkey])
                    hs, hkey = hs_r.next()
                    P.copy("pool", hs[:, 0:2], carry[:, ch, :], reads=[("carry", ch)], writes=[(hkey, "c")])
                    P.copy("act", hs[:, 2:TT + 2], bk[:, 0:TT], reads=[bkey], writes=[(hkey, "m")])
                    P.copy("pool", carry[:, ch, :], hs[:, TT:TT + 2], reads=[(hkey, "m")], writes=[("carry", ch)])
                    cv, ckey = cv_r.next()
                    w0c = par[:, PC_FCW + ch:PC_FCW + ch + 1]
                    w1c = par[:, PC_FCW + 44 + ch:PC_FCW + 44 + ch + 1]
                    w2c = par[:, PC_FCW + 88 + ch:PC_FCW + 88 + ch + 1]
                    bc = par[:, PC_FCB + ch:PC_FCB + ch + 1]
                    P.act(cv[:], bk[:, 0:TT], AF.Identity, scale=w2c, bias=bc, reads=[bkey, "par"], writes=[ckey])
                    P.stt(cv[:], hs[:, 1:TT + 1], w1c, cv[:], ALU.mult, ALU.add,
                          reads=[(hkey, "m"), (hkey, "c"), ckey, "par"], writes=[ckey])
                    P.stt(cv[:], hs[:, 0:TT], w0c, cv[:], ALU.mult, ALU.add,
                          reads=[(hkey, "m"), (hkey, "c"), ckey, "par"], writes=[ckey])
                    cvs[which] = (cv, ckey)
                sg, skey = sg_r.next()
                P.act(sg[:], cvs[0][0][:], AF.Silu, reads=[cvs[0][1]], writes=[skey])
                P.tt("dve", gT[:, j, :], sg[:], cvs[1][0][:], ALU.mult, reads=[skey, cvs[1][1]], writes=[("gT", j)])
            if tile + 1 < NT:
                nxt = load_xtile(P, E, xres_r, xb_r, tile + 1)

            def emit_y(m, out_ap, okey):
                for j in range(NFC):
                    P.mm(out_ap, wdn[:, j, m * 128:(m + 1) * 128], gT[:, j, :], start=(j == 0), stop=(j == NFC - 1),
                         reads=["wdn", ("gT", j)], writes=[okey])
            ln_tail(P, E, lnb, xres, xkey, PC_LN2G, PC_LN2B, emit_y, tile, ybanks, sbanks)
        P.flush()


def phase_even(P, E, l):
    nc = E.nc
    i = l // 2
    lam_init = 0.8 - 0.6 * math.exp(-0.3 * l)
    with contextlib.ExitStack() as st:
        load_par(P, E, st, l)
        par = E.par
        win = st.enter_context(nc.sbuf_tensor(U("win"), [128, 8, 4096], BF16))
        wout = st.enter_context(nc.sbuf_tensor(U("wout"), [128, 8, D], BF16))
        for c in range(8):
            P.dma("pool", win[:, c, :], E.d_ev_win[i, c * 128:(c + 1) * 128, :], "w_in", writes=["win"],
                  max_dma_last_dim=4096)
        load_weight(P, E, wout, E.d_ev_wout[i], "wout", nsplit=2)
        KT = st.enter_context(nc.sbuf_tensor(U("KT"), [128, 4, T], BF16))
        VP = st.enter_context(nc.sbuf_tensor(U("VP"), [128, T // 128, 4, 129], BF16))
        P.memset("pool", VP[:], 1.0, writes=[("VP", kb) for kb in range(T // 128)])
        lamp = st.enter_context(nc.sbuf_tensor(U("lamp"), [128, 256], F32))
        gsub = st.enter_context(nc.sbuf_tensor(U("gsub_sb"), [128, 128], F32))
        lsm = st.enter_context(nc.sbuf_tensor(U("lsm"), [128, 8], F32))
        lpr = st.enter_context(nc.sbuf_tensor(U("lpr"), [128, 2, 64], F32))
        P.dma("sp", lamp[:], E.d_lamp[i], "lamp", writes=["lamp"])
        P.dma("sp", gsub[:], E.d_gsub[i], "gsubd", writes=["gsub"])
        P.ts("dve", gsub[:], gsub[:], float(1.0 - lam_init), None, ALU.mult, reads=["gsub"], writes=["gsub"])
        P.tt("dve", lpr[:, 0, :], lamp[:, 0:64], lamp[:, 64:128], ALU.mult, reads=["lamp"], writes=["lpr"])
        P.tt("dve", lpr[:, 1, :], lamp[:, 128:192], lamp[:, 192:256], ALU.mult, reads=["lamp"], writes=["lpr"])
        P.add("dve", lambda e: e.reduce_sum(lsm[:, 0:1], lpr[:, 0, :], AX.X), ["lpr"], ["lsm"])
        P.add("dve", lambda e: e.reduce_sum(lsm[:, 1:2], lpr[:, 1, :], AX.X), ["lpr"], ["lsm"])
        P.act(lsm[:, 2:4], lsm[:, 0:2], AF.Exp, reads=["lsm"], writes=["lsm"])
        P.tt("dve", lsm[:, 4:5], lsm[:, 3:4], lsm[:, 2:3], ALU.subtract, reads=["lsm"], writes=["lsm"])
        P.ts("dve", lsm[:, 5:6], lsm[:, 4:5], float(-lam_init), None, ALU.add, reads=["lsm"], writes=["neglam"])
        neglam = lsm[:, 5:6]
        lnb = alloc_ln(st, nc)
        xres_r = Ring(st, nc, "xres", [128, 8, TT], F32, 2)
        xb_r = Ring(st, nc, "xb", [128, 8, TT], BF16, 1)
        catT = st.enter_context(nc.sbuf_tensor(U("catT"), [128, 8, TT], BF16))
        QT = st.enter_context(nc.sbuf_tensor(U("QT"), [128, 4, TT], BF16))
        carry = st.enter_context(nc.sbuf_tensor(U("carry_u"), [128, 4, 2], F32))
        P.memset("pool", carry[:], 0.0, writes=[("carry", c) for c in range(4)])
        rc_r = Ring(st, nc, "rc", [128, TT], F32, 1)
        rs_r = Ring(st, nc, "rs", [128, TT], F32, 1)
        gcs_r = Ring(st, nc, "gcs", [128, TT], F32, 1)
        u_r = Ring(st, nc, "u", [128, TT + 2], F32, 2)
        cv_r = Ring(st, nc, "cv", [128, TT], F32, 1)
        t1_r = Ring(st, nc, "t1", [128, TT], F32, 1)
        t2_r = Ring(st, nc, "t2", [128, TT], F32, 1)
        pT_r = Ring(st, nc, "pT", [128, 4, 128], BF16, 4)
        sm_r = Ring(st, nc, "sm", [128, 8], F32, 2)
        tn_r = Ring(st, nc, "tn", [128, 128], F32, 2)
        o_r = Ring(st, nc, "o", [128, 128], F32, 2)
        on_r = Ring(st, nc, "on", [128, 128], F32, 2)
        junk = st.enter_context(nc.sbuf_tensor(U("junk"), [128, 128], BF16))
        ibanks = bank_ring(E, [0, 1, 2, 3])
        abanks = bank_ring(E, [4, 5, 6, 7])
        ybanks = bank_ring(E, [4, 5])
        sbanks = bank_ring(E, [6, 7])
        maskUI = E.cst[:, C_UI:C_UI + 128]
        ident = E.cst[:, C_ID:C_ID + 128]

        def proj(bk, bkey, col0, xb, xbkey):
            for c in range(8):
                P.mm(bk[:, 0:TT], win[:, c, col0:col0 + 128], xb[:, c, :], start=(c == 0), stop=(c == 7),
                     reads=["win", xbkey], writes=[bkey])

        nxt = load_xtile(P, E, xres_r, xb_r, 0)
        for tile in range(NT):
            xres, xkey, xb, xbkey = nxt
            tsl = slice(tile * TT, (tile + 1) * TT)
            rc, rckey = rc_r.next()
            rs, rskey = rs_r.next()
            P.dma("sp", rc[:], E.d_ropec[:, tsl], "rc" + str(rckey[1]), writes=[rckey])
            P.dma("sp", rs[:], E.d_ropes[:, tsl], "rs" + str(rskey[1]), writes=[rskey])
            for cc in range(4):
                b_gc, k_gc = ibanks.next()
                proj(b_gc, k_gc, 512 + cc * 128, xb, xbkey)
                b_xi, k_xi = ibanks.next()
                proj(b_xi, k_xi, 1024 + cc * 128, xb, xbkey)
                b_gb, k_gb = ibanks.next()
                proj(b_gb, k_gb, cc * 128, xb, xbkey)
                gcs, gkey = gcs_r.next()
                P.copy("act", gcs[:], b_gc[:, 0:TT], reads=[k_gc], writes=[gkey])
                u, ukey = u_r.next()
                P.copy("pool", u[:, 0:2], carry[:, cc, :], reads=[("carry", cc)], writes=[(ukey, "c")])
                P.tt("dve", u[:, 2:TT + 2], b_xi[:, 0:TT], gcs[:], ALU.mult, reads=[k_xi, gkey], writes=[(ukey, "m")])
                P.copy("pool", carry[:, cc, :], u[:, TT:TT + 2], reads=[(ukey, "m")], writes=[("carry", cc)])
                cv, ckey = cv_r.next()
                w0c = par[:, PC_ECW + cc:PC_ECW + cc + 1]
                w1c = par[:, PC_ECW + 4 + cc:PC_ECW + 4 + cc + 1]
                w2c = par[:, PC_ECW + 8 + cc:PC_ECW + 8 + cc + 1]
                P.ts("dve", cv[:], u[:, 2:TT + 2], w2c, None, ALU.mult, reads=[(ukey, "m"), "par"], writes=[ckey])
                P.stt(cv[:], u[:, 1:TT + 1], w1c, cv[:], ALU.mult, ALU.add, reads=[(ukey, "m"), (ukey, "c"), ckey], writes=[ckey])
                P.stt(cv[:], u[:, 0:TT], w0c, cv[:], ALU.mult, ALU.add, reads=[(ukey, "m"), (ukey, "c"), ckey], writes=[ckey])
                P.tt("dve", catT[:, cc, :], b_gb[:, 0:TT], cv[:], ALU.mult, reads=[k_gb, ckey], writes=[("catT", cc)])
            for cc in range(4):
                for (col0, colsw, dst, dkeyw) in ((1536, 3072, QT[:, cc, :], ("QT", cc)),
                                                  (2048, 3584, KT[:, cc, tsl], ("KT", tile, cc))):
                    b1, k1 = ibanks.next()
                    proj(b1, k1, col0 + cc * 128, xb, xbkey)
                    b2, k2 = ibanks.next()
                    proj(b2, k2, colsw + cc * 128, xb, xbkey)
                    t1, t1k = t1_r.next()
                    t2, t2k = t2_r.next()
                    P.tt("dve", t1[:], b1[:, 0:TT], rc[:], ALU.mult, reads=[k1, rckey], writes=[t1k])
                    P.tt("dve", t2[:], b2[:, 0:TT], rs[:], ALU.mult, reads=[k2, rskey], writes=[t2k])
                    P.tt("pool", dst, t1[:], t2[:], ALU.add, reads=[t1k, t2k], writes=[dkeyw])
            for sub in range(TT // 128):
                kb = tile * (TT // 128) + sub
                bv, kv = ibanks.next()
                for c in range(8):
                    P.mm(bv[:], xb[:, c, sub * 128:(sub + 1) * 128], win[:, c, 2560:3072], start=(c == 0), stop=(c == 7),
                         reads=["win", xbkey], writes=[kv])
                P.copy("act", VP[:, kb, :, 0:128], bv[:].rearrange("p (h d) -> p h d", h=4), reads=[kv], writes=[("VP", kb)])
            pend = []

            def drain():
                while pend:
                    pend.pop(0)()
            for sub in range(TT // 128):
                qb = tile * (TT // 128) + sub
                qsl = slice(sub * 128, (sub + 1) * 128)
                for h in range(4):
                    accs = [abanks.next(), abanks.next()]
                    kbs = list(range(qb + 1))
                    for g0 in range(0, qb + 1, 4):
                        grp = kbs[g0:g0 + 4]
                        ng = len(grp)
                        sb2 = [ibanks.next(), ibanks.next()]
                        for g, kb in enumerate(grp):
                            for m in range(2):
                                sbk, skey = sb2[m]
                                P.mm(sbk[:, g * 128:(g + 1) * 128], KT[64 * m:64 * m + 64, h, kb * 128:(kb + 1) * 128],
                                     QT[64 * m:64 * m + 64, h, qsl], start=True, stop=True,
                                     reads=[("KT", kb // (TT // 128), h), ("QT", h)], writes=[skey])
                        pts = []
                        for m in range(2):
                            sbk, skey = sb2[m]
                            pT, pkey = pT_r.next()
                            P.act(pT[:, 0:ng, :], sbk[:, 0:ng * 128].rearrange("p (g q) -> p g q", g=ng), AF.Exp, scale=0.125,
                                  reads=[skey], writes=[pkey])
                            if qb in grp:
                                gd = grp.index(qb)
                                P.tt("pool", pT[:, gd, :], pT[:, gd, :], maskUI, ALU.mult, reads=[pkey, "consts"], writes=[pkey])
                            pts.append((pT, pkey))
                        drain()
                        for m in range(2):
                            def pv(grp=grp, pT=pts[m][0], pkey=pts[m][1], acc=accs[m][0], akey=accs[m][1], h=h, qb=qb):
                                for g, kb in enumerate(grp):
                                    P.mm(acc[:, 0:129], pT[:, g, :], VP[:, kb, h, :], start=(kb == 0), stop=(kb == qb),
                                         reads=[pkey, ("VP", kb)], writes=[akey])
                            pend.append(pv)

                    def norm(accs=accs, h=h, qsl=qsl, sub=sub):
                        (acc0, ak0), (acc1, ak1) = accs
                        sm, smk = sm_r.next()
                        P.recip(sm[:, 0:1], acc0[:, 128:129], reads=[ak0], writes=[(smk, 0)])
                        P.recip(sm[:, 1:2], acc1[:, 128:129], reads=[ak1], writes=[(smk, 1)])
                        P.tt("dve", sm[:, 2:3], sm[:, 1:2], neglam, ALU.mult, reads=[(smk, 1), "neglam"], writes=[(smk, 2)])
                        tn, tnk = tn_r.next()
                        P.ts("dve", tn[:], acc1[:, 0:128], sm[:, 2:3], None, ALU.mult, reads=[ak1, (smk, 2)], writes=[tnk])
                        o, ok = o_r.next()
                        P.stt(o[:], acc0[:, 0:128], sm[:, 0:1], tn[:], ALU.mult, ALU.add, reads=[ak0, (smk, 0), tnk], writes=[ok])
                        P.act(junk[:], o[:], AF.Square, accum_out=sm[:, 3:4], reads=[ok], writes=["junk", (smk, 3)])
                        P.ts("dve", sm[:, 4:5], sm[:, 3:4], 1.0 / 128.0, 1e-5, ALU.mult, ALU.add, reads=[(smk, 3)], writes=[(smk, 4)])
                        P.act(sm[:, 4:5], sm[:, 4:5], AF.Sqrt, reads=[(smk, 4)], writes=[(smk, 4)])
                        P.recip(sm[:, 5:6], sm[:, 4:5], reads=[(smk, 4)], writes=[(smk, 5)])
                        on, onk = on_r.next()
                        P.stt(on[:], o[:], sm[:, 5:6], gsub[:], ALU.mult, ALU.mult, reads=[ok, (smk, 5), "gsub"], writes=[onk])
                        trb, trk = ibanks.next()
                        P.tr(trb[:, 0:128], on[:], ident, reads=[onk, "consts"], writes=[trk])
                        P.copy("act", catT[:, 4 + h, qsl], trb[:, 0:128], reads=[trk], writes=[("catT", 4 + h, sub)])
                    pend.append(norm)
            drain()
            if tile + 1 < NT:
                nxt = load_xtile(P, E, xres_r, xb_r, tile + 1)

            def emit_y(m, out_ap, okey):
                for c in range(8):
                    rd = [("catT", c)] if c < 4 else [("catT", c, sb_) for sb_ in range(TT // 128)]
                    P.mm(out_ap, wout[:, c, m * 128:(m + 1) * 128], catT[:, c, :], start=(c == 0), stop=(c == 7),
                         reads=["wout"] + rd, writes=[okey])
            ln_tail(P, E, lnb, xres, xkey, PC_LN1G, PC_LN1B, emit_y, tile, ybanks, sbanks)
        P.flush()


def phase_rwkv_a(P, E, l):
    nc = E.nc
    j = l // 2
    has_vres = j > 0
    NS = TT // 128
    with contextlib.ExitStack() as st:
        load_par(P, E, st, l)
        par = E.par
        wr = st.enter_context(nc.sbuf_tensor(U("wr"), [128, 8, D], BF16))
        wk = st.enter_context(nc.sbuf_tensor(U("wk"), [128, 8, D], BF16))
        wv = st.enter_context(nc.sbuf_tensor(U("wv"), [128, 8, D], BF16))
        load_weight(P, E, wr, E.d_rw_wr[j], "wr", nsplit=2)
        wl = st.enter_context(nc.sbuf_tensor(U("wl"), [128, 8, 320], BF16))
        P.dma("pool", wl[:, :, 0:64], E.d_rw_w1[j].rearrange("(c p) n -> p c n", p=128), "w_l", writes=["wl"])
        P.dma("pool", wl[:, :, 64:128], E.d_rw_a1[j].rearrange("(c p) n -> p c n", p=128), "w_l", writes=["wl"])
        P.dma("pool", wl[:, :, 128:288], E.d_rw_g1[j].rearrange("(c p) n -> p c n", p=128), "w_l", writes=["wl"])
        if has_vres:
            P.dma("pool", wl[:, :, 288:320], E.d_rw_v1[j - 1].rearrange("(c p) n -> p c n", p=128), "w_l", writes=["wl"])
        w2s = st.enter_context(nc.sbuf_tensor(U("w2s"), [64, D], BF16))
        a2s = st.enter_context(nc.sbuf_tensor(U("a2s"), [64, D], BF16))
        g2s = st.enter_context(nc.sbuf_tensor(U("g2s"), [128, 2, D], BF16))
        v2s = st.enter_context(nc.sbuf_tensor(U("v2s"), [32, D], BF16))
        P.dma("pool", w2s[:], E.d_rw_w2[j], "w_l", writes=["wl"], max_dma_last_dim=4096)
        P.dma("pool", a2s[:], E.d_rw_a2[j], "w_l", writes=["wl"], max_dma_last_dim=4096)
        P.dma("pool", g2s[:, 0, :], E.d_rw_g2[j, 0:128, :], "w_l", writes=["wl"], max_dma_last_dim=4096)
        P.dma("pool", g2s[0:32, 1, :], E.d_rw_g2[j, 128:160, :], "w_l", writes=["wl"], max_dma_last_dim=4096)
        if has_vres:
            P.dma("pool", v2s[:], E.d_rw_v2[j - 1], "w_l", writes=["wl"], max_dma_last_dim=4096)
        load_weight(P, E, wk, E.d_rw_wk[j], "wk", nsplit=2)
        load_weight(P, E, wv, E.d_rw_wv[j], "wv", nsplit=2)
        rowf = st.enter_context(nc.sbuf_tensor(U("rowf"), [1, 2, D], F32))
        P.dma("sp", rowf[:], E.d_rowp[j:j + 1, :, :], "rowf", writes=["rowf"])
        ones1 = st.enter_context(nc.sbuf_tensor(U("ones1"), [1, 128], F32))
        P.memset("pool", ones1[:], 1.0, writes=["ones1"])
        omka = st.enter_context(nc.sbuf_tensor(U("omka"), [128, 8], F32))
        P.ts("dve", omka[:], par[:, PC_KA:PC_KA + 8], -1.0, 1.0, ALU.mult, ALU.add, reads=["par"], writes=["omka"])
        xres_r = Ring(st, nc, "xres", [128, 8, TT], F32, 2)
        xx_r = Ring(st, nc, "xx", [128, 8, TT], F32, 1)
        xprev = st.enter_context(nc.sbuf_tensor(U("xprev"), [128, 8, 1], F32))
        P.memset("pool", xprev[:], 0.0, writes=["xprev"])
        mx_r = Ring(st, nc, "mx", [128, 8, TT], BF16, 2)
        rbuf_r = Ring(st, nc, "rbuf", [128, 8, TT], F32, 1)
        kbuf_r = Ring(st, nc, "kbuf", [128, 8, TT], F32, 1)
        abuf_r = Ring(st, nc, "abuf", [128, 8, TT], F32, 1)
        gbuf = st.enter_context(nc.sbuf_tensor(U("gbuf"), [128, 8, TT], BF16))
        kkbuf = st.enter_context(nc.sbuf_tensor(U("kkbuf"), [128, 8, TT], F32))
        k2buf = st.enter_context(nc.sbuf_tensor(U("k2buf"), [128, 8, TT], F32))
        prbuf = st.enter_context(nc.sbuf_tensor(U("prbuf"), [128, 8, TT], BF16))
        h1 = st.enter_context(nc.sbuf_tensor(U("h1"), [64, TT], BF16))
        ha = st.enter_context(nc.sbuf_tensor(U("ha"), [64, TT], BF16))
        hv = st.enter_context(nc.sbuf_tensor(U("hv"), [32, TT], BF16))
        hgb = st.enter_context(nc.sbuf_tensor(U("hgb"), [128, 2, TT], BF16))
        sqb8 = st.enter_context(nc.sbuf_tensor(U("sqb8"), [128, 8, TT], BF16))
        sd8 = st.enter_context(nc.sbuf_tensor(U("sd8"), [128, 8, TT], F32))
        f8 = st.enter_context(nc.sbuf_tensor(U("f8"), [128, 8, TT], F32))
        sgt_r = Ring(st, nc, "sgt", [128, D], F32, 1)
        vt_r = Ring(st, nc, "vt", [128, D], F32, 1)
        if has_vres:
            sgv = st.enter_context(nc.sbuf_tensor(U("sgv"), [128, D], F32))
            vf = st.enter_context(nc.sbuf_tensor(U("vf"), [128, D], F32))
            dd = st.enter_context(nc.sbuf_tensor(U("dd"), [128, D], F32))
        banks = bank_ring(E, [0, 1, 2, 3, 4, 5, 6, 7])

        cur_xx = [None, None]

        def mixed(q, xres, xkey, xx=None, xxk=None):
            xx, xxk = cur_xx
            mx, mk = mx_r.next()
            for c in range(8):
                P.stt(mx[:, c, :], xx[:, c, :], par[:, PC_MIX + q * 8 + c:PC_MIX + q * 8 + c + 1], xres[:, c, :],
                      ALU.mult, ALU.add, reads=[xxk, xkey, "par"], writes=[(mk, c)])
            return mx, [(mk, c) for c in range(8)]

        def proj_fm(w, m, mx, mkeys, bk, bkey, M=128, col0=None):
            cs = slice(m * 128, (m + 1) * 128) if col0 is None else slice(col0, col0 + M)
            wkey = {id(wr): "wr", id(wk): "wk", id(wv): "wv", id(wl): "wl"}[id(w)]
            for c in range(8):
                P.mm(bk[0:M, 0:TT], w[:, c, cs], mx[:, c, :], start=(c == 0), stop=(c == 7),
                     reads=[wkey, mkeys[c]], writes=[bkey])

        NTA = int(_os.environ.get("RW_NT", NT))
        xnext = None
        for tile in range(NTA):
            if tile == 0:
                xnext = xres_r.next()
                P.dma("sp", xnext[0][:], E.xs[:, :, 0:TT].rearrange("c p t -> p c t"), "xs_ld" + str(xnext[1][1]),
                      reads=[("xs", 0)], writes=[xnext[1]])
            xres, xkey = xnext
            if tile + 1 < NTA:
                xnext = xres_r.next()
                P.dma("sp", xnext[0][:], E.xs[:, :, (tile + 1) * TT:(tile + 2) * TT].rearrange("c p t -> p c t"),
                      "xs_ld" + str(xnext[1][1]), reads=[("xs", tile + 1)], writes=[xnext[1]])
            xx, xxk = xx_r.next()
            rbuf, rbk = rbuf_r.next()
            kbuf, kbk = kbuf_r.next()
            abuf, abk = abuf_r.next()
            tsl = slice(tile * TT, (tile + 1) * TT)
            cur_xx[:] = [xx, xxk]
            P.tt("pool", xx[:, :, 1:TT], xres[:, :, 0:TT - 1], xres[:, :, 1:TT], ALU.subtract, reads=[xkey], writes=[xxk])
            P.tt("pool", xx[:, :, 0:1], xprev[:], xres[:, :, 0:1], ALU.subtract, reads=[xkey, "xprev"], writes=[xxk])
            P.copy("pool", xprev[:], xres[:, :, TT - 1:TT], reads=[xkey], writes=["xprev"])
            mx, mkeys = mixed(0, xres, xkey)
            for m in range(8):
                bk, bkey = banks.next()
                proj_fm(wr, m, mx, mkeys, bk, bkey)
                P.copy("act", rbuf[:, m, :], bk[:, 0:TT], reads=[bkey], writes=[(rbk, m)])
            mx, mkeys = mixed(1, xres, xkey)
            bk, bkey = banks.next()
            proj_fm(wl, 0, mx, mkeys, bk, bkey, M=64, col0=0)
            P.act(h1[:], bk[0:64, 0:TT], AF.Tanh, reads=[bkey], writes=["h1"])
            for sub in range(NS):
                sgt, sgk = sgt_r.next()
                for half in range(2):
                    bk, bkey = banks.next()
                    hs_ = slice(half * 512, (half + 1) * 512)
                    P.mm(bk[:], h1[0:64, sub * 128:(sub + 1) * 128], w2s[0:64, hs_], start=True, stop=False,
                         reads=["h1", "wl"], writes=[bkey])
                    P.mm(bk[:], ones1[0:1, :], rowf[0:1, 0, hs_], start=False, stop=True, reads=["ones1", "rowf"], writes=[bkey])
                    P.act(sgt[:, hs_], bk[:], AF.Sigmoid, reads=[bkey], writes=[(sgk, half)])
                r0 = tile * TT + sub * 128
                P.dma("sp", E.rwsg[r0:r0 + 128, :], sgt[:], "st_sg", reads=[(sgk, 0), (sgk, 1)], writes=[("rwsg", r0)])
            mx, mkeys = mixed(2, xres, xkey)
            for m in range(8):
                bk, bkey = banks.next()
                proj_fm(wk, m, mx, mkeys, bk, bkey)
                P.copy("act", kbuf[:, m, :], bk[:, 0:TT], reads=[bkey], writes=[(kbk, m)])
            mx, mkeys = mixed(3, xres, xkey)
            if has_vres:
                bk, bkey = banks.next()
                proj_fm(wl, 0, mx, mkeys, bk, bkey, M=32, col0=288)
                P.copy("act", hv[:], bk[0:32, 0:TT], reads=[bkey], writes=["hv"])
            for sub in range(NS):
                r0 = tile * TT + sub * 128
                vt, vk = vt_r.next()
                vbk = []
                for half in range(2):
                    bk, bkey = banks.next()
                    hs_ = slice(half * 512, (half + 1) * 512)
                    for c in range(8):
                        P.mm(bk[:], mx[:, c, sub * 128:(sub + 1) * 128], wv[:, c, hs_], start=(c == 0), stop=(c == 7),
                             reads=["wv", mkeys[c]], writes=[bkey])
                    vbk.append((bk, bkey))
                if has_vres:
                    P.dma("sp", vf[:], E.vfirst[r0:r0 + 128, :], "ld_vf", reads=[("vfirst", r0)], writes=["vf"])
                    for half in range(2):
                        hs_ = slice(half * 512, (half + 1) * 512)
                        bk, bkey = banks.next()
                        P.mm(bk[:], hv[0:32, sub * 128:(sub + 1) * 128], v2s[0:32, hs_], start=True, stop=False,
                             reads=["hv", "wl"], writes=[bkey])
                        P.mm(bk[:], ones1[0:1, :], rowf[0:1, 1, hs_], start=False, stop=True, reads=["ones1", "rowf"], writes=[bkey])
                        P.act(sgv[:, hs_], bk[:], AF.Sigmoid, reads=[bkey], writes=[("sgv", half)])
                        vb_, vbk_ = vbk[half]
                        P.tt("dve", dd[:, hs_], vf[:, hs_], vb_[:], ALU.subtract, reads=["vf", vbk_], writes=[("dd", half)])
                        P.tt("pool", dd[:, hs_], dd[:, hs_], sgv[:, hs_], ALU.mult, reads=[("dd", half), ("sgv", half)], writes=[("dd", half)])
                        P.tt("dve", vt[:, hs_], dd[:, hs_], vb_[:], ALU.add, reads=[("dd", half), vbk_], writes=[(vk, half)])
                else:
                    for half in range(2):
                        hs_ = slice(half * 512, (half + 1) * 512)
                        vb_, vbk_ = vbk[half]
                        P.copy("act", vt[:, hs_], vb_[:], reads=[vbk_], writes=[(vk, half)])
                    P.dma("sp", E.vfirst[r0:r0 + 128, :], vt[:], "st_vf", reads=[(vk, 0), (vk, 1)], writes=[("vfirst", r0)])
                P.dma("sp", E.rwv[r0:r0 + 128, :], vt[:], "st_v", reads=[(vk, 0), (vk, 1)], writes=[("rwv", r0)])
            mx, mkeys = mixed(4, xres, xkey)
            bk, bkey = banks.next()
            proj_fm(wl, 0, mx, mkeys, bk, bkey, M=64, col0=64)
            P.copy("act", ha[:], bk[0:64, 0:TT], reads=[bkey], writes=["ha"])
            for m in range(8):
                bk, bkey = banks.next()
                P.mm(bk[:, 0:TT], a2s[0:64, m * 128:(m + 1) * 128], ha[0:64, :], start=True, stop=True,
                     reads=["ha", "wl"], writes=[bkey])
                P.act(abuf[:, m, :], bk[:, 0:TT], AF.Sigmoid, bias=par[:, PC_A0 + m:PC_A0 + m + 1],
                      reads=[bkey, "par"], writes=[(abk, m)])
            mx, mkeys = mixed(5, xres, xkey)
            bk, bkey = banks.next()
            proj_fm(wl, 0, mx, mkeys, bk, bkey, M=128, col0=128)
            P.act(hgb[:, 0, :], bk[:, 0:TT], AF.Sigmoid, reads=[bkey], writes=[("hgb", 0)])
            bk, bkey = banks.next()
            proj_fm(wl, 0, mx, mkeys, bk, bkey, M=32, col0=256)
            P.act(hgb[0:32, 1, :], bk[0:32, 0:TT], AF.Sigmoid, reads=[bkey], writes=[("hgb", 1)])
            for m in range(8):
                bk, bkey = banks.next()
                P.mm(bk[:, 0:TT], g2s[:, 0, m * 128:(m + 1) * 128], hgb[:, 0, :], start=True, stop=False,
                     reads=[("hgb", 0), "wl"], writes=[bkey])
                P.mm(bk[:, 0:TT], g2s[0:32, 1, m * 128:(m + 1) * 128], hgb[0:32, 1, :], start=False, stop=True,
                     reads=[("hgb", 1), "wl"], writes=[bkey])
                P.copy("act", gbuf[:, m, :], bk[:, 0:TT], reads=[bkey], writes=[("gbuf", m)])
            def pc(base, m):
                return par[:, base + m:base + m + 1]
            for m in range(8):
                P.ts("dve", kkbuf[:, m, :], kbuf[:, m, :], pc(PC_KK, m), None, ALU.mult, reads=[(kbk, m), "par"], writes=[("kk", m)])
            for m in range(8):
                P.act(sqb8[:, m, :], kkbuf[:, m, :], AF.Square, reads=[("kk", m)], writes=[("sqb", m)])
            for m in range(8):
                P.ts("dve", f8[:, m, :], abuf[:, m, :], pc(PC_KA, m), None, ALU.mult, reads=[(abk, m), "par"], writes=[("f8", m)])
            nbk = []
            for m2 in range(4):
                bk, bkey = banks.next()
                for q in range(2):
                    m = 2 * m2 + q
                    P.mm(bk[:, q * TT:(q + 1) * TT], E.bones[:], sqb8[:, m, :], start=True, stop=True, reads=[("sqb", m), "consts2"], writes=[bkey])
                P.act(sd8[:, 2 * m2:2 * m2 + 2, :], bk[:].rearrange("p (q t) -> p q t", q=2), AF.Sqrt, reads=[bkey], writes=[("sd", m2)])
            for m in range(8):
                P.stt(k2buf[:, m, :], f8[:, m, :], omka[:, m:m + 1], kbuf[:, m, :], ALU.add, ALU.mult,
                      reads=[("f8", m), "omka", (kbk, m)], writes=[("k2", m)])
            for m2 in range(4):
                ms = slice(2 * m2, 2 * m2 + 2)
                P.ts("pool", sd8[:, ms, :], sd8[:, ms, :], 1e-12, None, ALU.max, reads=[("sd", m2)], writes=[("sd", m2)])
                P.recip(sd8[:, ms, :], sd8[:, ms, :], reads=[("sd", m2)], writes=[("sd", m2)])
                P.tt("dve", kkbuf[:, ms, :], kkbuf[:, ms, :], sd8[:, ms, :], ALU.mult,
                     reads=[("kk", 2 * m2), ("kk", 2 * m2 + 1), ("sd", m2)], writes=[("kk", 2 * m2), ("kk", 2 * m2 + 1)])
            for m in range(8):
                P.stt(prbuf[:, m, :], rbuf[:, m, :], pc(PC_RK, m), k2buf[:, m, :], ALU.mult, ALU.mult,
                      reads=[(rbk, m), ("k2", m), "par"], writes=[("pr", m)])
            for m2 in range(4):
                ms = slice(2 * m2, 2 * m2 + 2)
                P.tt("pool", abuf[:, ms, :], abuf[:, ms, :], kkbuf[:, ms, :], ALU.mult,
                     reads=[(abk, 2 * m2), (abk, 2 * m2 + 1), ("kk", 2 * m2), ("kk", 2 * m2 + 1)], writes=[(abk, 2 * m2), (abk, 2 * m2 + 1)])
            for idx, (buf, kn) in enumerate(((rbuf, rbk), (k2buf, "k2"), (kkbuf, "kk"), (abuf, abk))):
                P.dma("sp", E.rwd[idx, :, :, tsl].rearrange("c p t -> p c t"), buf[:], "st_rwd%d" % idx,
                      reads=[(kn, m) for m in range(8)], writes=[("rwd", idx, tile)])
            for idx, (buf, kn) in enumerate(((gbuf, "gbuf"), (prbuf, "pr"))):
                P.dma("sp", E.rwdb[idx, :, :, tsl].rearrange("c p t -> p c t"), buf[:], "st_rwdb%d" % idx,
                      reads=[(kn, m) for m in range(8)], writes=[("rwdb", idx, tile)])
        P.flush()


def phase_rwkv_b(P, E, l):
    nc = E.nc
    j = l // 2
    GN_EPS = 64e-5
    with contextlib.ExitStack() as st:
        load_par(P, E, st, l)
        par = E.par
        wo = st.enter_context(nc.sbuf_tensor(U("wo"), [128, 8, D], BF16))
        load_weight(P, E, wo, E.d_rw_wo[j], "wo", nsplit=2)
        mSU4 = st.enter_context(nc.sbuf_tensor(U("mSU4"), [128, 4, 128], F32))
        mUI4 = st.enter_context(nc.sbuf_tensor(U("mUI4"), [128, 4, 128], F32))
        mSL4 = st.enter_context(nc.sbuf_tensor(U("mSL4"), [128, 4, 128], F32))
        id4 = st.enter_context(nc.sbuf_tensor(U("id4"), [128, 4, 128], F32))
        for q in range(4):
            P.copy("pool", mSU4[:, q, :], E.cst[:, C_SU:C_SU + 128], reads=["consts"], writes=["m4"])
            P.copy("pool", mUI4[:, q, :], E.cst[:, C_UI:C_UI + 128], reads=["consts"], writes=["m4"])
            P.copy("pool", mSL4[:, q, :], E.cst[:, C_SL:C_SL + 128], reads=["consts"], writes=["m4"])
            P.copy("pool", id4[:, q, :], E.cst[:, C_ID:C_ID + 128], reads=["consts"], writes=["m4"])
        bones64 = st.enter_context(nc.sbuf_tensor(U("bones64"), [128, 128], BF16))
        P.ts("dve", bones64[:], E.cst[:, C_BO:C_BO + 128], 1.0 / 64.0, None, ALU.mult, reads=["consts"], writes=["m4"])
        maskUI = E.cst[:, C_UI:C_UI + 128]
        maskSU = E.cst[:, C_SU:C_SU + 128]
        ident = E.cst[:, C_ID:C_ID + 128]
        Pf = st.enter_context(nc.sbuf_tensor(U("Pf"), [128, 512], F32))
        Pb = st.enter_context(nc.sbuf_tensor(U("Pb"), [128, 512], BF16))
        P.memset("pool", Pf[:], 0.0, writes=["Pf"])
        P.memset("pool", Pb[:], 0.0, writes=["Pb"])
        lnb = alloc_ln(st, nc)
        xres_r = Ring(st, nc, "xres", [128, 8, TT], F32, 1)
        zT = st.enter_context(nc.sbuf_tensor(U("zT"), [128, 8, TT], BF16))
        fm_r = [Ring(st, nc, "fm%d" % i, [128, 8, 128], F32, 2) for i in range(4)]
        fb_r = [Ring(st, nc, "fb%d" % i, [128, 8, 128], BF16, 2) for i in range(2)]
        sg_r = Ring(st, nc, "sgl", [128, D], F32, 2)
        vt_r = Ring(st, nc, "vtl", [128, D], F32, 2)
        vb = st.enter_context(nc.sbuf_tensor(U("vb"), [128, D], BF16))
        gam = st.enter_context(nc.sbuf_tensor(U("gam"), [128, 4, 128], F32))
        ginv = st.enter_context(nc.sbuf_tensor(U("ginv"), [128, 4, 128], F32))
        game = st.enter_context(nc.sbuf_tensor(U("game"), [128, 4, 128], F32))
        ghat = st.enter_context(nc.sbuf_tensor(U("ghat"), [128, 4, 128], F32))
        gsm = st.enter_context(nc.sbuf_tensor(U("gsm"), [128, 16], F32))
        Rt = st.enter_context(nc.sbuf_tensor(U("Rt"), [128, 8, 128], BF16))
        At = st.enter_context(nc.sbuf_tensor(U("At"), [128, 8, 128], BF16))
        Bt = st.enter_context(nc.sbuf_tensor(U("Bt"), [128, 8, 128], BF16))
        Kt = st.enter_context(nc.sbuf_tensor(U("Kt"), [128, 8, 128], BF16))
        Atf = st.enter_context(nc.sbuf_tensor(U("Atf"), [128, 4, 128], F32))
        Bhf = st.enter_context(nc.sbuf_tensor(U("Bhf"), [128, 4, 128], F32))
        Khf = st.enter_context(nc.sbuf_tensor(U("Khf"), [128, 4, 128], F32))
        Atm = st.enter_context(nc.sbuf_tensor(U("Atm"), [128, 8, 128], BF16))
        Bhm = st.enter_context(nc.sbuf_tensor(U("Bhm"), [128, 8, 128], BF16))
        Khm = st.enter_context(nc.sbuf_tensor(U("Khm"), [128, 8, 128], BF16))
        LT_r = [Ring(st, nc, "LTs%d" % i, [128, 4, 128], BF16, 2) for i in range(4)]
        L_r = [Ring(st, nc, "Ls%d" % i, [128, 4, 128], BF16, 2) for i in range(4)]
        TT_r = [Ring(st, nc, "TTm%d" % i, [128, 4, 128], BF16, 2) for i in range(4)]
        TTall = st.enter_context(nc.sbuf_tensor(U("TTall"), [128, 16, 128], BF16))
        Lak_r = Ring(st, nc, "LakTs", [128, 4, 128], BF16, 2)
        Y2s = st.enter_context(nc.sbuf_tensor(U("Y2s"), [128, 16, 64], BF16))
        WTs = st.enter_context(nc.sbuf_tensor(U("WTs"), [128, 8, 128], BF16))
        MrbT = st.enter_context(nc.sbuf_tensor(U("MrbT"), [128, 16, 128], BF16))
        MrkT = st.enter_context(nc.sbuf_tensor(U("MrkT"), [128, 16, 128], BF16))
        Us = st.enter_context(nc.sbuf_tensor(U("Us"), [128, 16, 64], BF16))
        Os = st.enter_context(nc.sbuf_tensor(U("Os"), [128, D], F32))
        ob = st.enter_context(nc.sbuf_tensor(U("ob"), [128, 8, 128], BF16))
        osq = st.enter_context(nc.sbuf_tensor(U("osq"), [128, 8, 128], BF16))
        tb = bank_ring(E, [0, 1, 2, 3])
        tb8 = bank_ring(E, [0, 1, 2, 3, 4, 5, 6, 7])
        ybanks = bank_ring(E, [4, 5])
        sbanks = bank_ring(E, [6, 7])
        UB = [(E.PB[4], ("pb", 4)), (E.PB[5], ("pb", 5))]
        OB = [(E.PB[6], ("pb", 6)), (E.PB[7], ("pb", 7))]

        def v4(bank):
            return bank[:].rearrange("p (q t) -> p q t", q=4)

        def issue_loads(ch):
            tile_ = ch // 2
            csl = slice(ch * 128, (ch + 1) * 128)
            fm, fmk = [], []
            for i in range(4):
                t_, k_ = fm_r[i].next()
                P.dma("sp", t_[:], E.rwd[i, :, :, csl].rearrange("c p t -> p c t"), "ld_fm%d_%d" % (i, k_[1]),
                      reads=[("rwd", i, tile_)], writes=[k_])
                fm.append(t_)
                fmk.append(k_)
            fb, fbk = [], []
            for i in range(2):
                t_, k_ = fb_r[i].next()
                P.dma("sp", t_[:], E.rwdb[i, :, :, csl].rearrange("c p t -> p c t"), "ld_fb%d_%d" % (i, k_[1]),
                      reads=[("rwdb", i, tile_)], writes=[k_])
                fb.append(t_)
                fbk.append(k_)
            sg, sgk = sg_r.next()
            P.dma("sp", sg[:], E.rwsg[csl, :], "ld_sg%d" % sgk[1], reads=[("rwsg", ch * 128)], writes=[sgk])
            vt, vtk = vt_r.next()
            P.dma("sp", vt[:], E.rwv[csl, :], "ld_vt%d" % vtk[1], reads=[("rwv", ch * 128)], writes=[vtk])
            return fm, fmk, fb, fbk, sg, sgk, vt, vtk

        xcur = None
        nxt_ld = None
        NCH = int(_os.environ.get("RW_NCH", T // 128))
        for ch in range(NCH):
            tile, sub = ch // 2, ch % 2
            csl = slice(ch * 128, (ch + 1) * 128)
            if sub == 0:
                xres, xkey = xres_r.next()
                P.dma("sp", xres[:], E.xs[:, :, tile * TT:(tile + 1) * TT].rearrange("c p t -> p c t"), "xs_ld" + str(xkey[1]),
                      reads=[("xs", tile)], writes=[xkey])
                xcur = (xres, xkey)
            if ch == 0:
                nxt_ld = issue_loads(0)
            fm, fmk, fb, fbk, sg, sgk, vt, vtk = nxt_ld
            if ch + 1 < NCH:
                nxt_ld = issue_loads(ch + 1)
            P.copy("pool", vb[:], vt[:], reads=[vtk], writes=["vb"])
            r_f, k2_f, kk_f, bv_f = fm
            rk_, k2k_, kkk_, bvk_ = fmk
            g_b, pr_b = fb
            gk_, prk_ = fbk
            for grp in range(2):
                gi, gik = tb.next()
                ge, gek = tb.next()
                for p4 in range(4):
                    c = grp * 4 + p4
                    P.mm(gi[:, p4 * 128:(p4 + 1) * 128], sg[:, c * 128:(c + 1) * 128], maskUI, start=True, stop=True,
                         reads=[sgk, "consts"], writes=[gik])
                for p4 in range(4):
                    c = grp * 4 + p4
                    P.mm(ge[:, p4 * 128:(p4 + 1) * 128], sg[:, c * 128:(c + 1) * 128], maskSU, start=True, stop=True,
                         reads=[sgk, "consts"], writes=[gek])
                P.act(gam[:], v4(gi), AF.Exp, scale=-C0, reads=[gik], writes=["gam"])
                P.act(ginv[:], v4(gi), AF.Exp, scale=C0, reads=[gik], writes=["ginv"])
                P.act(game[:], v4(ge), AF.Exp, scale=-C0, reads=[gek], writes=["game"])
                for p4 in range(4):
                    c = grp * 4 + p4
                    last = gi[:, p4 * 128 + 127:p4 * 128 + 128]
                    P.act(gsm[:, c:c + 1], last, AF.Identity, scale=-C0, reads=[gik], writes=[("nb", c)])
                    P.act(gsm[:, 8 + c:9 + c], last, AF.Exp, scale=-C0, reads=[gik], writes=[("gC", c)])
                    P.act(ghat[:, p4, :], gi[:, p4 * 128:(p4 + 1) * 128], AF.Exp, scale=C0, bias=gsm[:, c:c + 1],
                          reads=[gik, ("nb", c)], writes=[("ghat", p4)])
                cs = slice(grp * 4, grp * 4 + 4)
                P.tt("dve", Rt[:, cs, :], r_f[:, cs, :], gam[:], ALU.mult, reads=[rk_, "gam"], writes=[("Rt", grp)])
                P.stt(Atf[:], kk_f[:, cs, :], -1.0, game[:], ALU.mult, ALU.mult, reads=[kkk_, "game"], writes=["Atf"])
                P.copy("pool", At[:, cs, :], Atf[:], reads=["Atf"], writes=[("At", grp)])
                P.tt("dve", Bt[:, cs, :], bv_f[:, cs, :], ginv[:], ALU.mult, reads=[bvk_, "ginv"], writes=[("Bt", grp)])
                P.tt("pool", Kt[:, cs, :], k2_f[:, cs, :], ginv[:], ALU.mult, reads=[k2k_, "ginv"], writes=[("Kt", grp)])
                P.tt("dve", Bhf[:], bv_f[:, cs, :], ghat[:], ALU.mult, reads=[bvk_] + [("ghat", q) for q in range(4)], writes=["Bhf"])
                P.tt("pool", Khf[:], k2_f[:, cs, :], ghat[:], ALU.mult, reads=[k2k_] + [("ghat", q) for q in range(4)], writes=["Khf"])
                for (src, skey, dst, dname, eng) in ((Atf, "Atf", Atm, "Atm", "act"), (Bhf, "Bhf", Bhm, "Bhm", "dve"), (Khf, "Khf", Khm, "Khm", "act")):
                    bk, bkey = tb.next()
                    for p4 in range(4):
                        P.tr(bk[:, p4 * 128:(p4 + 1) * 128], src[:, p4, :], ident, reads=[skey, "consts"], writes=[bkey])
                    P.copy(eng, dst[:, cs, :], v4(bk), reads=[bkey], writes=[(dname, grp)])
            if _DBG_STOP <= 1:
                continue
            def hop(hg, q):
                par_i, pblk = hg % 2, hg // 2
                c = 4 * pblk + q
                return 2 * c + par_i, c, slice(64 * par_i, 64 * par_i + 64), pblk
            G = [dict() for _ in range(4)]
            for hg in range(4):
                g = G[hg]
                ltb, ltk = tb8.next()
                lb, lk = tb8.next()
                for q in range(4):
                    h, c, rs, g_ = hop(hg, q)
                    P.mm(ltb[:, q * 128:(q + 1) * 128], Bt[rs, c, :], At[rs, c, :], start=True, stop=True,
                         reads=[("Bt", g_), ("At", g_)], writes=[ltk])
                for q in range(4):
                    h, c, rs, g_ = hop(hg, q)
                    P.mm(lb[:, q * 128:(q + 1) * 128], At[rs, c, :], Bt[rs, c, :], start=True, stop=True,
                         reads=[("Bt", g_), ("At", g_)], writes=[lk])
                g["LTs"], g["LTk"] = LT_r[hg].next()
                g["Ls"], g["Lk"] = L_r[hg].next()
                P.tt("dve", g["LTs"][:], v4(ltb), mSU4[:], ALU.mult, reads=[ltk, "m4"], writes=[g["LTk"]])
                P.tt("dve", g["Ls"][:], v4(lb), mSL4[:], ALU.mult, reads=[lk, "m4"], writes=[g["Lk"]])
                g["TTm"], g["TTk"] = TT_r[hg].next()
                P.tt("pool", g["TTm"][:], g["LTs"][:], id4[:], ALU.add, reads=[g["LTk"], "m4"], writes=[g["TTk"]])
            for hg in range(4):
                g = G[hg]
                lab, lak = tb8.next()
                for q in range(4):
                    h, c, rs, g_ = hop(hg, q)
                    P.mm(lab[:, q * 128:(q + 1) * 128], Kt[rs, c, :], At[rs, c, :], start=True, stop=True,
                         reads=[("Kt", g_), ("At", g_)], writes=[lak])
                g["LakTs"], g["Lakk"] = Lak_r.next()
                P.tt("dve", g["LakTs"][:], v4(lab), mSU4[:], ALU.mult, reads=[lak, "m4"], writes=[g["Lakk"]])
                y2b, y2k = tb8.next()
                for q in range(4):
                    h, c, rs, g_ = hop(hg, q)
                    P.mm(y2b[:, q * 64:(q + 1) * 64], g["LakTs"][:, q, :], vb[:, h * 64:(h + 1) * 64], start=True, stop=True,
                         reads=[g["Lakk"], "vb"], writes=[y2k])
                P.copy("act", Y2s[:, 4 * hg:4 * hg + 4, :], y2b[:, 0:256].rearrange("p (q v) -> p q v", q=4), reads=[y2k], writes=[("Y2s", hg)])
                mbb, mbk = tb8.next()
                for q in range(4):
                    h, c, rs, g_ = hop(hg, q)
                    P.mm(mbb[:, q * 128:(q + 1) * 128], Bt[rs, c, :], Rt[rs, c, :], start=True, stop=True,
                         reads=[("Bt", g_), ("Rt", g_)], writes=[mbk])
                P.tt("dve", MrbT[:, 4 * hg:4 * hg + 4, :], v4(mbb), mUI4[:], ALU.mult, reads=[mbk, "m4"], writes=[("MrbT", hg)])
                mkb, mkk = tb8.next()
                for q in range(4):
                    h, c, rs, g_ = hop(hg, q)
                    P.mm(mkb[:, q * 128:(q + 1) * 128], Kt[rs, c, :], Rt[rs, c, :], start=True, stop=True,
                         reads=[("Kt", g_), ("Rt", g_)], writes=[mkk])
                P.tt("dve", MrkT[:, 4 * hg:4 * hg + 4, :], v4(mkb), mUI4[:], ALU.mult, reads=[mkk, "m4"], writes=[("MrkT", hg)])
            n = 2
            while n <= 64:
                for hg in range(4):
                    g = G[hg]
                    g["lnb"], g["lnk"] = tb8.next()
                    for q in range(4):
                        P.mm(g["lnb"][:, q * 128:(q + 1) * 128], g["LTs"][:, q, :], g["Ls"][:, q, :], start=True, stop=True,
                             reads=[g["LTk"], g["Lk"]], writes=[g["lnk"]])
                    if n < 64:
                        g["ltnb"], g["ltnk"] = tb8.next()
                        for q in range(4):
                            P.mm(g["ltnb"][:, q * 128:(q + 1) * 128], g["Ls"][:, q, :], g["LTs"][:, q, :], start=True, stop=True,
                                 reads=[g["LTk"], g["Lk"]], writes=[g["ltnk"]])
                    g["Ls2"], g["Lk2"] = L_r[hg].next()
                    P.copy("act", g["Ls2"][:], v4(g["lnb"]), reads=[g["lnk"]], writes=[g["Lk2"]])
                    if n < 64:
                        g["LTs2"], g["LTk2"] = LT_r[hg].next()
                        P.copy("act", g["LTs2"][:], v4(g["ltnb"]), reads=[g["ltnk"]], writes=[g["LTk2"]])
                for hg in range(4):
                    g = G[hg]
                    pb_, pk_ = tb8.next()
                    for q in range(4):
                        P.mm(pb_[:, q * 128:(q + 1) * 128], g["Ls2"][:, q, :], g["TTm"][:, q, :], start=True, stop=True,
                             reads=[g["Lk2"], g["TTk"]], writes=[pk_])
                    if n < 64:
                        TTm2, TTk2 = TT_r[hg].next()
                        P.tt("dve", TTm2[:], v4(pb_), g["TTm"][:], ALU.add, reads=[pk_, g["TTk"]], writes=[TTk2])
                        g["TTm"], g["TTk"] = TTm2, TTk2
                        g["LTs"], g["LTk"] = g["LTs2"], g["LTk2"]
                    else:
                        P.tt("dve", TTall[:, 4 * hg:4 * hg + 4, :], v4(pb_), g["TTm"][:], ALU.add, reads=[pk_, g["TTk"]], writes=[("TTall", hg)])
                    g["Ls"], g["Lk"] = g["Ls2"], g["Lk2"]
                n *= 2
            for hg in range(4):
                par_i, pblk = hg % 2, hg // 2
                wtb, wtk = tb8.next()
                for q in range(4):
                    h, c, rs, g_ = hop(hg, q)
                    P.mm(wtb[rs, q * 128:(q + 1) * 128], Atm[:, c, rs], TTall[:, 4 * hg + q, :], start=True, stop=True,
                         reads=[("Atm", g_), ("TTall", hg)], writes=[wtk])
                rs_ = slice(64 * par_i, 64 * par_i + 64)
                P.copy("act", WTs[rs_, 4 * pblk:4 * pblk + 4, :], wtb[rs_, :].rearrange("p (q t) -> p q t", q=4), reads=[wtk], writes=[("WTs", hg)])
            if _DBG_STOP <= 2:
                continue
            def slot(h):
                i_, c_ = h % 2, h // 2
                return (2 * (c_ // 4) + i_) * 4 + (c_ % 4)
            for i in range(2):
                rs = slice(64 * i, 64 * i + 64)
                ub, ubk = UB[i]
                for c in range(8):
                    h = 2 * c + i
                    sl_ = slot(h)
                    P.mm(ub[:, c * 64:(c + 1) * 64], TTall[:, sl_, :], Y2s[:, sl_, :], start=True, stop=False,
                         reads=[("TTall", sl_ // 4), ("Y2s", sl_ // 4)], writes=[ubk])
                    P.mm(ub[:, c * 64:(c + 1) * 64], WTs[rs, c, :], Pb[rs, c * 64:(c + 1) * 64], start=False, stop=True,
                         reads=[("WTs", sl_ // 4), "Pb"], writes=[ubk])
            for i in range(2):
                ub, ubk = UB[i]
                P.copy("act", Us[:, 8 * i:8 * i + 8, :], ub[:].rearrange("p (q v) -> p q v", q=8), reads=[ubk], writes=[("Us", i)])
            for i in range(2):
                rs = slice(64 * i, 64 * i + 64)
                obk_, obkk = OB[i]
                for c in range(8):
                    h = 2 * c + i
                    sl_ = slot(h)
                    P.mm(obk_[:, c * 64:(c + 1) * 64], Rt[rs, c, :], Pb[rs, c * 64:(c + 1) * 64], start=True, stop=False,
                         reads=[("Rt", c // 4), "Pb"], writes=[obkk])
                    P.mm(obk_[:, c * 64:(c + 1) * 64], MrkT[:, sl_, :], vb[:, h * 64:(h + 1) * 64], start=False, stop=False,
                         reads=[("MrkT", sl_ // 4), "vb"], writes=[obkk])
                    P.mm(obk_[:, c * 64:(c + 1) * 64], MrbT[:, sl_, :], Us[:, 8 * i + c, :], start=False, stop=True,
                         reads=[("MrbT", sl_ // 4), ("Us", i)], writes=[obkk])
            pnb, pnk = tb.next()
            for h in range(16):
                c, i = h // 2, h % 2
                rs = slice(64 * i, 64 * i + 64)
                P.mm(pnb[rs, c * 64:(c + 1) * 64], Bhm[:, c, rs], Us[:, 8 * i + c, :], start=True, stop=False,
                     reads=[("Bhm", c // 4), ("Us", i)], writes=[pnk])
                P.mm(pnb[rs, c * 64:(c + 1) * 64], Khm[:, c, rs], vb[:, h * 64:(h + 1) * 64], start=False, stop=True,
                     reads=[("Khm", c // 4), "vb"], writes=[pnk])
            for c in range(8):
                P.stt(Pf[:, c * 64:(c + 1) * 64], Pf[:, c * 64:(c + 1) * 64], gsm[:, 8 + c:9 + c], pnb[:, c * 64:(c + 1) * 64],
                      ALU.mult, ALU.add, reads=["Pf", ("gC", c), pnk], writes=["Pf"])
            P.copy("pool", Pb[:], Pf[:], reads=["Pf"], writes=["Pb"])
            if _DBG_STOP <= 3:
                continue
            for i in range(2):
                obk_, obkk = OB[i]
                P.copy("act", Os[:].rearrange("p (c i v) -> p c i v", c=8, i=2)[:, :, i, :],
                       obk_[:].rearrange("p (c v) -> p c v", c=8), reads=[obkk], writes=[("Os", i)])
            for grp in range(2):
                cs = slice(grp * 4, grp * 4 + 4)
                otb, otk = tb.next()
                for p4 in range(4):
                    c = grp * 4 + p4
                    P.tr(otb[:, p4 * 128:(p4 + 1) * 128], Os[:, c * 128:(c + 1) * 128], ident, reads=[("Os", 0), ("Os", 1), "consts"], writes=[otk])
                P.act(ob[:, cs, :], v4(otb), AF.Identity, reads=[otk], writes=[("ob", grp)])
                P.act(osq[:, cs, :], v4(otb), AF.Square, reads=[otk], writes=[("osq", grp)])
                P.copy("act", gam[:], v4(otb), reads=[otk], writes=["gam"])
                vtb, vtbk = tb.next()
                for p4 in range(4):
                    c = grp * 4 + p4
                    P.tr(vtb[:, p4 * 128:(p4 + 1) * 128], vt[:, c * 128:(c + 1) * 128], ident, reads=[vtk, "consts"], writes=[vtbk])
                P.copy("act", ginv[:], v4(vtb), reads=[vtbk], writes=["ginv"])
                mnb, mnk = tb.next()
                for p4 in range(4):
                    c = grp * 4 + p4
                    P.mm(mnb[:, p4 * 128:(p4 + 1) * 128], bones64[:], ob[:, c, :], start=True, stop=True, reads=[("ob", grp), "m4"], writes=[mnk])
                P.copy("act", game[:], v4(mnb), reads=[mnk], writes=["game"])
                msb, msk = tb.next()
                for p4 in range(4):
                    c = grp * 4 + p4
                    P.mm(msb[:, p4 * 128:(p4 + 1) * 128], bones64[:], osq[:, c, :], start=True, stop=True, reads=[("osq", grp), "m4"], writes=[msk])
                P.tt("pool", Atf[:], game[:], game[:], ALU.mult, reads=["game"], writes=["Atf"])
                P.stt(Bhf[:], v4(msb), GN_EPS, Atf[:], ALU.add, ALU.subtract, reads=[msk, "Atf"], writes=["Bhf"])
                P.act(Bhf[:], Bhf[:], AF.Sqrt, reads=["Bhf"], writes=["Bhf"])
                P.recip(Bhf[:], Bhf[:], reads=["Bhf"], writes=["Bhf"])
                P.tt("pool", gam[:], gam[:], game[:], ALU.subtract, reads=["gam", "game"], writes=["gam"])
                P.tt("dve", gam[:], gam[:], Bhf[:], ALU.mult, reads=["gam", "Bhf"], writes=["gam"])
                for p4 in range(4):
                    c = grp * 4 + p4
                    P.act(ghat[:, p4, :], gam[:, p4, :], AF.Identity, scale=par[:, PC_GNG + c:PC_GNG + c + 1],
                          bias=par[:, PC_GNB + c:PC_GNB + c + 1], reads=["gam", "par"], writes=[("ghat", p4)])
                bnb, bnk = tb.next()
                for p4 in range(4):
                    c = grp * 4 + p4
                    P.mm(bnb[:, p4 * 128:(p4 + 1) * 128], E.bones[:], pr_b[:, c, :], start=True, stop=True, reads=[prk_, "consts2"], writes=[bnk])
                P.tt("dve", Khf[:], v4(bnb), ginv[:], ALU.mult, reads=[bnk, "ginv"], writes=["Khf"])
                P.tt("pool", Khf[:], Khf[:], ghat[:], ALU.add, reads=["Khf"] + [("ghat", q) for q in range(4)], writes=["Khf"])
                P.tt("dve", zT[:, cs, sub * 128:(sub + 1) * 128], Khf[:], g_b[:, cs, :], ALU.mult, reads=["Khf", gk_], writes=[("zT", grp, sub)])
            if sub == 1:
                xres, xkey = xcur

                def emit_y(m, out_ap, okey):
                    for c in range(8):
                        P.mm(out_ap, wo[:, c, m * 128:(m + 1) * 128], zT[:, c, :], start=(c == 0), stop=(c == 7),
                             reads=["wo"] + [("zT", c // 4, s_) for s_ in range(2)], writes=[okey])
                ln_tail(P, E, lnb, xres, xkey, PC_LN1G, PC_LN1B, emit_y, tile, ybanks, sbanks)
        P.flush()


def build(plan, debug_xs=False):
    nc = bass.Bass("TRN2", target_bir_lowering=False)
    E = Env()
    E.nc = nc

    def din(name, shape):
        return nc.dram_tensor(name, list(shape), F32, kind="ExternalInput").ap()
    E.d_x = din("x", [T, D])
    E.d_cst = din("cst", [128, 640])
    E.d_ropec = din("ropec", [128, T])
    E.d_ropes = din("ropes", [128, T])
    E.d_par = din("par", [4, 128, NPAR])
    E.d_rowp = din("rowp", [2, 2, D])
    E.d_lamp = din("lamp", [2, 128, 256])
    E.d_gsub = din("gsub", [2, 128, 128])
    E.d_ev_win = din("ev_win", [2, D, 4096])
    E.d_ev_wout = din("ev_wout", [2, D, D])
    for n in ("rw_wr", "rw_wk", "rw_wv", "rw_wo"):
        setattr(E, "d_" + n, din(n, [2, D, D]))
    E.d_rw_w1 = din("rw_w1", [2, D, 64])
    E.d_rw_w2 = din("rw_w2", [2, 64, D])
    E.d_rw_a1 = din("rw_a1", [2, D, 64])
    E.d_rw_a2 = din("rw_a2", [2, 64, D])
    E.d_rw_g1 = din("rw_g1", [2, D, 160])
    E.d_rw_g2 = din("rw_g2", [2, 160, D])
    E.d_rw_v1 = din("rw_v1", [1, D, 32])
    E.d_rw_v2 = din("rw_v2", [1, 32, D])
    E.d_ffn_up = din("ffn_up", [4, D, 2 * DFF])
    E.d_ffn_dn = din("ffn_dn", [4, DFF, D])
    E.d_out = nc.dram_tensor("out", [T, D], F32, kind="ExternalOutput").ap()
    E.xs = nc.dram_tensor("xs_scratch", [8, 128, T], F32).ap()
    E.vfirst = nc.dram_tensor("vfirst_scratch", [T, D], F32).ap()
    E.rwd = nc.dram_tensor("rwd_scratch", [4, 8, 128, T], F32).ap()
    E.rwdb = nc.dram_tensor("rwdb_scratch", [2, 8, 128, T], BF16).ap()
    E.rwsg = nc.dram_tensor("rwsg_scratch", [T, D], F32).ap()
    E.rwv = nc.dram_tensor("rwv_scratch", [T, D], F32).ap()
    with contextlib.ExitStack() as st:
        P = Prog(nc, st)
        E.PB = [st.enter_context(nc.psum_tensor("pb%d" % i, [128, 512], F32)) for i in range(8)]
        E.cst = st.enter_context(nc.sbuf_tensor(U("cst_sb"), [128, 640], F32))
        E.onesD = st.enter_context(nc.sbuf_tensor(U("onesD"), [128, 128], BF16))
        E.identb = st.enter_context(nc.sbuf_tensor(U("identb"), [128, 128], BF16))
        E.bones = st.enter_context(nc.sbuf_tensor(U("bones"), [128, 128], BF16))
        P.dma("sp", E.cst[:], E.d_cst[:, :], "cst", writes=["consts"])
        P.memset("pool", E.onesD[:], 1.0 / D, writes=["consts"])
        P.copy("dve", E.identb[:], E.cst[:, C_ID:C_ID + 128], reads=["consts"], writes=["consts2"])
        P.copy("dve", E.bones[:], E.cst[:, C_BO:C_BO + 128], reads=["consts"], writes=["consts2"])
        P.flush()
        for ph in plan:
            if ph[0] == "in":
                phase_in(P, E)
            elif ph[0] == "out":
                phase_out(P, E)
            elif ph[0] == "ffn":
                phase_ffn(P, E, ph[1])
            elif ph[0] == "even":
                phase_even(P, E, ph[1])
            elif ph[0] == "rwkv":
                phase_rwkv_a(P, E, ph[1])
                if len(ph) < 3:
                    phase_rwkv_b(P, E, ph[1])
        E.n_instr = P.n_instr
    return nc, E


FULL_PLAN = [("in",), ("even", 0), ("ffn", 0), ("rwkv", 1), ("ffn", 1), ("even", 2), ("ffn", 2), ("rwkv", 3), ("ffn", 3), ("out",)]


def fm(v):
    return np.ascontiguousarray(np.asarray(v, np.float32).reshape(-1, 128).T)


def host_prepare(inp):
    sh = {}
    idx = np.arange(128)
    ident = (idx[:, None] == idx[None, :]).astype(np.float32)
    su = (idx[:, None] < idx[None, :]).astype(np.float32)
    ui = (idx[:, None] <= idx[None, :]).astype(np.float32)
    sl = (idx[:, None] > idx[None, :]).astype(np.float32)
    bo = ((idx[:, None] // 64) == (idx[None, :] // 64)).astype(np.float32)
    sh["cst"] = np.ascontiguousarray(np.concatenate([ident, su, ui, sl, bo], axis=1))
    inv = (1.0 / (np.float32(10000.0) ** (np.arange(0, 64, 2, dtype=np.float32) / np.float32(64)))).astype(np.float32)
    ang = (np.arange(T, dtype=np.float32)[:, None] * inv[None, :]).astype(np.float32)
    cos = np.cos(ang).astype(np.float32).T
    sin = np.sin(ang).astype(np.float32).T
    cos64 = np.concatenate([cos, cos], 0)
    sin64 = np.concatenate([-sin, sin], 0)
    sh["ropec"] = np.ascontiguousarray(np.concatenate([cos64, cos64], 0))
    sh["ropes"] = np.ascontiguousarray(np.concatenate([sin64, sin64], 0))
    par = np.zeros((4, 128, NPAR), np.float32)
    for l in range(4):
        par[l, :, PC_LN1G:PC_LN1G + 8] = fm(inp["ln1_g"][l])
        par[l, :, PC_LN1B:PC_LN1B + 8] = fm(inp["ln1_b"][l])
        par[l, :, PC_LN2G:PC_LN2G + 8] = fm(inp["ln2_g"][l])
        par[l, :, PC_LN2B:PC_LN2B + 8] = fm(inp["ln2_b"][l])
        for k in range(3):
            par[l, :, PC_FCW + 44 * k:PC_FCW + 44 * k + 44] = fm(inp["ffn_conv_w"][l, k])
        par[l, :, PC_FCB:PC_FCB + 44] = fm(inp["ffn_conv_b"][l])
        if l % 2 == 0:
            i = l // 2
            for k in range(3):
                par[l, :, PC_ECW + 4 * k:PC_ECW + 4 * k + 4] = fm(inp["ev_conv_w"][i, k])
        else:
            j = l // 2
            for q in range(6):
                par[l, :, PC_MIX + 8 * q:PC_MIX + 8 * q + 8] = fm(inp["rw_mix"][j, q])
            par[l, :, PC_W0:PC_W0 + 8] = fm(inp["rw_w0"][j])
            par[l, :, PC_A0:PC_A0 + 8] = fm(inp["rw_a0"][j])
            par[l, :, PC_KK:PC_KK + 8] = fm(inp["rw_k_k"][j])
            par[l, :, PC_KA:PC_KA + 8] = fm(inp["rw_k_a"][j])
            par[l, :, PC_RK:PC_RK + 8] = fm(inp["rw_r_k"][j].reshape(-1))
            par[l, :, PC_GNG:PC_GNG + 8] = fm(inp["rw_gn_g"][j])
            par[l, :, PC_GNB:PC_GNB + 8] = fm(inp["rw_gn_b"][j])
    sh["par"] = par
    rowp = np.zeros((2, 2, D), np.float32)
    rowp[:, 0, :] = inp["rw_w0"]
    rowp[1, 1, :] = inp["rw_v0"][0]
    sh["rowp"] = rowp
    lamp = np.stack([np.concatenate([inp["ev_lam_q1"][i], inp["ev_lam_k1"][i], inp["ev_lam_q2"][i], inp["ev_lam_k2"][i]])
                     for i in range(2)])
    sh["lamp"] = np.ascontiguousarray(np.broadcast_to(lamp[:, None, :], (2, 128, 256))).astype(np.float32)
    sh["gsub"] = np.ascontiguousarray(np.broadcast_to(inp["ev_subln_g"][:, None, :], (2, 128, 128))).astype(np.float32)
    w = np.asarray(inp["ev_w_in"], np.float32)
    perm = np.concatenate([np.concatenate([np.arange(m * 64 + 32, m * 64 + 64), np.arange(m * 64, m * 64 + 32)]) for m in range(8)])
    sh["ev_win"] = np.ascontiguousarray(np.concatenate([w, w[:, :, 1536 + perm], w[:, :, 2048 + perm]], axis=2))
    sh["ev_wout"] = inp["ev_w_out"]
    sh["rw_wr"], sh["rw_wk"], sh["rw_wv"], sh["rw_wo"] = inp["rw_w_r"], inp["rw_w_k"], inp["rw_w_v"], inp["rw_w_o"]
    for n in ("rw_w1", "rw_w2", "rw_a1", "rw_a2", "rw_g1", "rw_g2", "rw_v1", "rw_v2"):
        sh[n] = inp[n]
    sh["ffn_up"] = inp["ffn_w_up"]
    sh["ffn_dn"] = inp["ffn_w_down"]
    return {k: np.ascontiguousarray(np.asarray(v, np.float32)) for k, v in sh.items()}


_CACHE = {}


def run_plan(inputs, plan, x_override=None, n_cores=8):
    key = tuple(plan)
    if key not in _CACHE:
        _CACHE[key] = build(plan)
    nc, E = _CACHE[key]
    shared = host_prepare(inputs)
    x = np.asarray(inputs["x"] if x_override is None else x_override, np.float32)
    in_maps = []
    for b in range(n_cores):
        m = dict(shared)
        m["x"] = np.ascontiguousarray(x[b])
        in_maps.append(m)
    res = run_bass_kernel_spmd(nc, in_maps, core_ids=list(range(n_cores)))
    return np.stack([r["out"] for r in res.results], axis=0)


def kernel(**inputs):
    return run_plan(inputs, FULL_PLAN).astype(np.float32)
```
